# Optimizing a Trainium2 kernel written in Bass

```python
import math
import jax, jax.numpy as jnp
from jax import lax
import numpy as np

D_MODEL = 1024
BATCH = 4
SEQ = 8192
DEPTH = 4

N_MIXERS = 3
N_HEADS = 16
HEAD_DIM = 64
BLOCK = 128
SWA_KV_HEADS = 2
SWA_WINDOW = 128
DIL_PATTERNS = ((128, 1), (512, 4), (2048, 16))
MLA_Q_RANK = 256
MLA_KV_RANK = 128
MLA_NOPE = 64
MLA_ROPE = 32
MLA_V = 64
ROPE_THETA = 10000.0
REL_BUCKETS = 32
REL_MAX_DIST = 2048
PEER_HEADS = 8
PEER_KEYS = 128
PEER_EXPERTS = PEER_KEYS * PEER_KEYS
PEER_DKEY = 256
PEER_TOPK = 16
PEER_CHUNK = 128
DN_ALPHA = (2 * DEPTH) ** 0.25
DN_BETA = (8 * DEPTH) ** -0.25
LN_EPS = 1e-5
RMS_EPS = 1e-6
NEG = -1e30

N_SWA = (DEPTH + 2) // 3
N_DIL = (DEPTH + 1) // 3
N_MLA = DEPTH // 3

kernel_name = "hybrid_swa_dilated_mla_peer_deepnorm"

f32 = jnp.float32


def layer_norm(x, g, b):
    xf = x.astype(f32)
    mu = xf.mean(-1, keepdims=True)
    var = jnp.square(xf - mu).mean(-1, keepdims=True)
    return ((xf - mu) * lax.rsqrt(var + LN_EPS) * g + b).astype(x.dtype)


def rms_norm(x, g):
    xf = x.astype(f32)
    return (xf * lax.rsqrt(jnp.square(xf).mean(-1, keepdims=True) + RMS_EPS) * g).astype(x.dtype)


def rel_bucket(dist):
    max_exact = REL_BUCKETS // 2
    n = jnp.maximum(dist, 0)
    nf = jnp.maximum(n, 1).astype(f32)
    large = max_exact + (jnp.log(nf / max_exact) / math.log(REL_MAX_DIST / max_exact)
                         * (REL_BUCKETS - max_exact)).astype(jnp.int32)
    large = jnp.minimum(large, REL_BUCKETS - 1)
    return jnp.where(n < max_exact, n, large)


def banded_attention(q, k, v, rel_bias, max_dist, dilation):
    b, hk, g, L, dh = q.shape
    nb = L // BLOCK
    qb = q.reshape(b, hk, g, nb, BLOCK, dh)

    def band(t):
        tb = t.reshape(b, hk, nb, BLOCK, t.shape[-1])
        prev = jnp.concatenate([jnp.zeros_like(tb[:, :, :1]), tb[:, :, :-1]], axis=2)
        return jnp.concatenate([prev, tb], axis=3)

    kb, vb = band(k), band(v)
    s = jnp.einsum('bhgnqd,bhnkd->bhgnqk', qb, kb).astype(f32) * (dh ** -0.5)
    qi = jnp.arange(BLOCK)[:, None]
    kj = jnp.arange(2 * BLOCK)[None, :]
    dist = BLOCK + qi - kj
    bias = rel_bias[rel_bucket(dist * dilation)].astype(f32)
    bias = jnp.moveaxis(bias, -1, 0).reshape(hk, g, BLOCK, 2 * BLOCK)
    valid = ((dist >= 0) & (dist <= max_dist))[None] & \
        ((jnp.arange(nb)[:, None, None] > 0) | (kj >= BLOCK)[None])
    s = jnp.where(valid, s + bias[:, :, None], NEG)
    m = s.max(-1, keepdims=True)
    p = jnp.exp(s - m)
    l = p.sum(-1, keepdims=True)
    o = jnp.einsum('bhgnqk,bhnkd->bhgnqd', p, vb) / l
    lse = (m + jnp.log(l))[..., 0]
    return o.reshape(b, hk, g, L, dh), lse.reshape(b, hk, g, L)


def swa_mixer(x, w_in, sinks, w_out, rel_bias):
    b, s, _ = x.shape
    grp = N_HEADS // SWA_KV_HEADS
    q, k, v = jnp.split(x @ w_in, [N_HEADS * HEAD_DIM, (N_HEADS + SWA_KV_HEADS) * HEAD_DIM], axis=-1)
    q = q.reshape(b, s, SWA_KV_HEADS, grp, HEAD_DIM).transpose(0, 2, 3, 1, 4)
    k = k.reshape(b, s, SWA_KV_HEADS, HEAD_DIM).transpose(0, 2, 1, 3)
    v = v.reshape(b, s, SWA_KV_HEADS, HEAD_DIM).transpose(0, 2, 1, 3)
    o, lse = banded_attention(q, k, v, rel_bias, SWA_WINDOW - 1, 1)
    sink = sinks.astype(f32).reshape(SWA_KV_HEADS, grp)[None, :, :, None]
    o = o * jax.nn.sigmoid(lse - sink)[..., None]
    o = o.transpose(0, 3, 1, 2, 4).reshape(b, s, N_HEADS * HEAD_DIM).astype(x.dtype)
    return o @ w_out


def dilated_mixer(x, w_in, w_out, rel_bias):
    b, s, _ = x.shape
    n_pat = len(DIL_PATTERNS)
    proj = (x @ w_in).reshape(b, s, n_pat, 3, N_HEADS, HEAD_DIM)
    outs, lses = [], []
    for gi, (window, dil) in enumerate(DIL_PATTERNS):
        seg = dil * BLOCK
        sp = -(-s // seg) * seg
        t = jnp.pad(proj[:, :, gi], ((0, 0), (0, sp - s), (0, 0), (0, 0), (0, 0)))
        m_len = sp // dil
        t = t.reshape(b, m_len, dil, 3, N_HEADS, HEAD_DIM).transpose(3, 0, 2, 4, 1, 5)
        t = t.reshape(3, b * dil, N_HEADS, m_len, HEAD_DIM)
        o, lse = banded_attention(t[0][:, :, None], t[1], t[2], rel_bias, window // dil, dil)
        o = o[:, :, 0].reshape(b, dil, N_HEADS, m_len, HEAD_DIM).transpose(0, 3, 1, 2, 4)
        o = o.reshape(b, sp, N_HEADS, HEAD_DIM)[:, :s]
        lse = lse[:, :, 0].reshape(b, dil, N_HEADS, m_len).transpose(0, 3, 1, 2).reshape(b, sp, N_HEADS)[:, :s]
        outs.append(o)
        lses.append(lse)
    o = jnp.stack(outs, 0)
    wts = jax.nn.softmax(jnp.stack(lses, 0), axis=0)
    o = (wts[..., None] * o).sum(0).reshape(b, s, N_HEADS * HEAD_DIM).astype(x.dtype)
    return o @ w_out


def rope(t, pos):
    half = t.shape[-1] // 2
    freq = ROPE_THETA ** (-jnp.arange(half, dtype=f32) / half)
    ang = pos[:, None].astype(f32) * freq[None, :]
    cos, sin = jnp.cos(ang), jnp.sin(ang)
    tf = t.astype(f32)
    t1, t2 = tf[..., :half], tf[..., half:]
    return jnp.concatenate([t1 * cos - t2 * sin, t1 * sin + t2 * cos], -1).astype(t.dtype)


def mla_mixer(x, w_in, q_norm, w_uq, kv_norm, w_ukv, w_out):
    b, s, _ = x.shape
    c_q, c_kv, k_r = jnp.split(x @ w_in, [MLA_Q_RANK, MLA_Q_RANK + MLA_KV_RANK], axis=-1)
    q = (rms_norm(c_q, q_norm) @ w_uq).reshape(b, s, N_HEADS, MLA_NOPE + MLA_ROPE).transpose(0, 2, 1, 3)
    kv = (rms_norm(c_kv, kv_norm) @ w_ukv).reshape(b, s, N_HEADS, MLA_NOPE + MLA_V).transpose(0, 2, 1, 3)
    pos = jnp.arange(s)
    q = jnp.concatenate([q[..., :MLA_NOPE], rope(q[..., MLA_NOPE:], pos)], -1)
    k_r = rope(k_r[:, None], pos)
    k = jnp.concatenate([kv[..., :MLA_NOPE], jnp.broadcast_to(k_r, (b, N_HEADS, s, MLA_ROPE))], -1)
    v = kv[..., MLA_NOPE:]
    scale = (MLA_NOPE + MLA_ROPE) ** -0.5
    nb = s // BLOCK
    qb = q.reshape(b, N_HEADS, nb, BLOCK, MLA_NOPE + MLA_ROPE).transpose(2, 0, 1, 3, 4)

    def query_block(args):
        qblk, n = args
        sc = jnp.einsum('bhqd,bhkd->bhqk', qblk, k).astype(f32) * scale
        qpos = n * BLOCK + jnp.arange(BLOCK)
        sc = jnp.where(pos[None, :] <= qpos[:, None], sc, NEG)
        return jnp.einsum('bhqk,bhkd->bhqd', jax.nn.softmax(sc, axis=-1), v)

    o = lax.map(query_block, (qb, jnp.arange(nb)))
    o = o.transpose(1, 0, 3, 2, 4).reshape(b, s, N_HEADS * MLA_V).astype(x.dtype)
    return o @ w_out


def peer(x, w_q, keys, u, v):
    b, s, d = x.shape
    t = x.reshape(b * s, d)
    n_tok = b * s
    q = (t @ w_q).reshape(n_tok, PEER_HEADS, 2, PEER_DKEY // 2)
    sc = jnp.einsum('thpd,hpkd->thpk', q, keys).astype(f32)
    s_top, i_top = lax.top_k(sc, PEER_TOPK)
    cand = s_top[:, :, 0, :, None] + s_top[:, :, 1, None, :]
    cand_idx = i_top[:, :, 0, :, None] * PEER_KEYS + i_top[:, :, 1, None, :]
    best, sel = lax.top_k(cand.reshape(n_tok, PEER_HEADS, PEER_TOPK * PEER_TOPK), PEER_TOPK)
    idx = jnp.take_along_axis(cand_idx.reshape(n_tok, PEER_HEADS, -1), sel, axis=-1)
    gate = jax.nn.softmax(best, axis=-1)
    nc = n_tok // PEER_CHUNK
    hk = PEER_HEADS * PEER_TOPK

    def token_chunk(args):
        tc, ic, gc = args
        ue = jnp.take(u, ic, axis=0)
        ve = jnp.take(v, ic, axis=0)
        h = jax.nn.gelu(jnp.einsum('cd,ced->ce', tc, ue).astype(f32), approximate=False)
        return jnp.einsum('ce,ced->cd', gc * h, ve)

    y = lax.map(token_chunk, (t.reshape(nc, PEER_CHUNK, d), idx.reshape(nc, PEER_CHUNK, hk),
                              gate.reshape(nc, PEER_CHUNK, hk)))
    return y.reshape(b, s, d).astype(x.dtype)


def setup_inputs(seed: int = 0) -> dict:
    key = jax.random.key(seed)
    ks = iter(jax.random.split(key, 32))
    D = D_MODEL

    def nrm(shape, scale):
        return jax.random.normal(next(ks), shape, f32) * scale

    x = nrm((BATCH, SEQ, D), 1.0)
    rel_bias = nrm((REL_BUCKETS, N_HEADS), 0.5)
    ln_g = 1.0 + nrm((DEPTH, 2, D), 0.02)
    ln_b = nrm((DEPTH, 2, D), 0.02)
    swa_w_in = nrm((N_SWA, D, (N_HEADS + 2 * SWA_KV_HEADS) * HEAD_DIM), D ** -0.5)
    swa_sinks = nrm((N_SWA, N_HEADS), 1.0)
    swa_w_out = nrm((N_SWA, N_HEADS * HEAD_DIM, D), DN_BETA * (N_HEADS * HEAD_DIM) ** -0.5)
    dil_w_in = nrm((N_DIL, D, len(DIL_PATTERNS) * 3 * N_HEADS * HEAD_DIM), D ** -0.5)
    dil_w_out = nrm((N_DIL, N_HEADS * HEAD_DIM, D), DN_BETA * (N_HEADS * HEAD_DIM) ** -0.5)
    mla_w_in = nrm((N_MLA, D, MLA_Q_RANK + MLA_KV_RANK + MLA_ROPE), D ** -0.5)
    mla_q_norm = 1.0 + nrm((N_MLA, MLA_Q_RANK), 0.02)
    mla_w_uq = nrm((N_MLA, MLA_Q_RANK, N_HEADS * (MLA_NOPE + MLA_ROPE)), MLA_Q_RANK ** -0.5)
    mla_kv_norm = 1.0 + nrm((N_MLA, MLA_KV_RANK), 0.02)
    mla_w_ukv = nrm((N_MLA, MLA_KV_RANK, N_HEADS * (MLA_NOPE + MLA_V)), MLA_KV_RANK ** -0.5)
    mla_w_out = nrm((N_MLA, N_HEADS * MLA_V, D), DN_BETA * (N_HEADS * MLA_V) ** -0.5)
    peer_w_q = nrm((DEPTH, D, PEER_HEADS * PEER_DKEY), D ** -0.5)
    peer_keys = nrm((DEPTH, PEER_HEADS, 2, PEER_KEYS, PEER_DKEY // 2), (PEER_DKEY // 2) ** -0.5)
    peer_u = nrm((DEPTH, PEER_EXPERTS, D), D ** -0.5)
    peer_v = nrm((DEPTH, PEER_EXPERTS, D), DN_BETA * PEER_HEADS ** -0.5)
    return {"x": x, "rel_bias": rel_bias, "ln_g": ln_g, "ln_b": ln_b,
            "swa_w_in": swa_w_in, "swa_sinks": swa_sinks, "swa_w_out": swa_w_out,
            "dil_w_in": dil_w_in, "dil_w_out": dil_w_out,
            "mla_w_in": mla_w_in, "mla_q_norm": mla_q_norm, "mla_w_uq": mla_w_uq,
            "mla_kv_norm": mla_kv_norm, "mla_w_ukv": mla_w_ukv, "mla_w_out": mla_w_out,
            "peer_w_q": peer_w_q, "peer_keys": peer_keys, "peer_u": peer_u, "peer_v": peer_v}


def reference(x, rel_bias, ln_g, ln_b, swa_w_in, swa_sinks, swa_w_out, dil_w_in, dil_w_out,
              mla_w_in, mla_q_norm, mla_w_uq, mla_kv_norm, mla_w_ukv, mla_w_out,
              peer_w_q, peer_keys, peer_u, peer_v):
    for i in range(DEPTH):
        kind, j = i % N_MIXERS, i // N_MIXERS
        if kind == 0:
            y = swa_mixer(x, swa_w_in[j], swa_sinks[j], swa_w_out[j], rel_bias)
        elif kind == 1:
            y = dilated_mixer(x, dil_w_in[j], dil_w_out[j], rel_bias)
        else:
            y = mla_mixer(x, mla_w_in[j], mla_q_norm[j], mla_w_uq[j], mla_kv_norm[j],
                          mla_w_ukv[j], mla_w_out[j])
        x = layer_norm(DN_ALPHA * x + y, ln_g[i, 0], ln_b[i, 0])
        y = peer(x, peer_w_q[i], peer_keys[i], peer_u[i], peer_v[i])
        x = layer_norm(DN_ALPHA * x + y, ln_g[i, 1], ln_b[i, 1])
    return x
```

```python
import math
from contextlib import ExitStack

import numpy as np
import concourse.bass as bass
import concourse.mybir as mybir
from concourse.bass_utils import run_bass_kernel_spmd

F32 = mybir.dt.float32
BF16 = mybir.dt.bfloat16
U32 = mybir.dt.uint32
ALU = mybir.AluOpType
AF = mybir.ActivationFunctionType
AX = mybir.AxisListType

DEPTH = 4
DN_ALPHA = (2 * DEPTH) ** 0.25
LN_EPS = 1e-5
RMS_EPS = 1e-6
NEGM = -30000.0
SEQ = 8192
NBLK = 64
DILS = (1, 4, 16)


class Sched:
    KDMA = 12

    def __init__(self, nc, stack):
        self.nc = nc
        self.stack = stack
        self.eng = {"pe": nc.tensor, "dve": nc.vector, "act": nc.scalar, "pool": nc.gpsimd}
        self.stream = {"pe": "pe", "dve": "dve", "act": "act", "pool": "pool"}
        self.inc = {"pe": 1, "dve": 1, "act": 1, "pool": 1}
        self.EPOCH = {"pe": 30000, "dve": 30000, "act": 30000, "pool": 30000}
        self.rr = {"sp": 0, "poolq": 0, "actq": 0}
        for base, e, st in (("sp", nc.sync, "sp"), ("poolq", nc.gpsimd, "pool"), ("actq", nc.scalar, "act")):
            for k in range(self.KDMA):
                nm = f"{base}{k}"
                self.eng[nm] = e
                self.stream[nm] = st
                self.inc[nm] = 16
                self.EPOCH[nm] = 1800
        self.cnt = {k: 0 for k in self.eng}
        self.sems = {k: [] for k in self.eng}
        self.waited = {}
        self.last_w = {}
        self.readers = {}
        self.nsem = 0
        self.nwait = 0

    def _sem(self, q, epoch):
        while len(self.sems[q]) <= epoch:
            s = self.stack.enter_context(self.nc.semaphore(f"s_{q}_{len(self.sems[q])}"))
            self.sems[q].append(s)
            self.nsem += 1
        return self.sems[q][epoch]

    def _wait(self, q_issuer, dep):
        dq, dn = dep
        st = self.stream[q_issuer]
        ep = (dn - 1) // self.EPOCH[dq]
        local = dn - ep * self.EPOCH[dq]
        key = (st, dq, ep)
        if self.waited.get(key, 0) >= local:
            return
        self.waited[key] = local
        self.nwait += 1
        self.eng[q_issuer].wait_ge(self._sem(dq, ep), local * self.inc[dq])

    def emit(self, q, fn, reads=(), writes=()):
        if q in self.rr:
            k = self.rr[q] % self.KDMA
            self.rr[q] += 1
            q = f"{q}{k}"
        deps = set()
        for b in reads:
            if b in self.last_w:
                deps.add(self.last_w[b])
        for b in writes:
            if b in self.last_w:
                deps.add(self.last_w[b])
            for r in self.readers.get(b, ()):
                deps.add(r)
        if q[-1].isdigit() and self.cnt[q] > 0:
            deps.add((q, self.cnt[q]))
        for d in sorted(deps):
            if d[0] == q and q == "pe":
                continue
            self._wait(q, d)
        ins = fn()
        self.cnt[q] += 1
        n = self.cnt[q]
        ep = (n - 1) // self.EPOCH[q]
        ins.then_inc(self._sem(q, ep), self.inc[q])
        for b in writes:
            self.last_w[b] = (q, n)
            self.readers[b] = []
        for b in reads:
            lst = self.readers.setdefault(b, [])
            lst.append((q, n))
            if len(lst) > 12:
                latest = {}
                for (rq, rn) in lst:
                    latest[rq] = max(latest.get(rq, 0), rn)
                self.readers[b] = list(latest.items())
        return ins

    def barrier(self):
        for rep in ("pe", "dve", "act", "pool", "sp0"):
            for q in self.cnt:
                if self.cnt[q] > 0:
                    self._wait(rep, (q, self.cnt[q]))

    def finish(self):
        for q in self.cnt:
            if self.cnt[q] > 0:
                self._wait("sp0", (q, self.cnt[q]))


class Ctx:
    def __init__(self):
        self.nc = bass.Bass("TRN2", target_bir_lowering=False)
        self.stack = ExitStack()
        self.S = Sched(self.nc, self.stack)

    def din(self, name, shape, dt=F32):
        return self.nc.dram_tensor(name, list(shape), dt, kind="ExternalInput").ap()

    def dout(self, name, shape, dt=F32):
        return self.nc.dram_tensor(name, list(shape), dt, kind="ExternalOutput").ap()

    def dscratch(self, name, shape, dt):
        return self.nc.dram_tensor(name, list(shape), dt).ap()

    def sb(self, name, shape, dt, stack=None):
        return (stack or self.stack).enter_context(self.nc.sbuf_tensor(name, list(shape), dt))

    def ps(self, name, shape, dt, stack=None):
        return (stack or self.stack).enter_context(self.nc.psum_tensor(name, list(shape), dt))


def emit_ln(C, z, zk, out_ap, ok, g_b, b_b, st, junk):
    S, nc = C.S, C.nc
    S.emit("dve", lambda: nc.vector.tensor_scalar(out=junk[:], in0=z, scalar1=1.0, scalar2=0.0, op0=ALU.mult,
                                                   op1=ALU.add, accum_out=st[:, 0:1]),
           reads=[zk], writes=["junk", "lnst0"])
    S.emit("dve", lambda: nc.vector.scalar_tensor_tensor(out=junk[:], in0=z, scalar=1.0, in1=z, op0=ALU.mult,
                                                          op1=ALU.mult, accum_out=st[:, 1:2]),
           reads=[zk], writes=["junk", "lnst1"])
    S.emit("dve", lambda: nc.vector.tensor_scalar(out=st[:, 2:4], in0=st[:, 0:2], scalar1=1.0 / 1024, scalar2=None,
                                                   op0=ALU.mult), reads=["lnst0", "lnst1"], writes=["lnst23"])
    S.emit("dve", lambda: nc.vector.tensor_tensor(out=st[:, 4:5], in0=st[:, 2:3], in1=st[:, 2:3], op=ALU.mult),
           reads=["lnst23"], writes=["lnst4"])
    S.emit("dve", lambda: nc.vector.tensor_tensor(out=st[:, 5:6], in0=st[:, 3:4], in1=st[:, 4:5], op=ALU.subtract),
           reads=["lnst23", "lnst4"], writes=["lnst5"])
    S.emit("dve", lambda: nc.vector.tensor_scalar(out=st[:, 5:6], in0=st[:, 5:6], scalar1=LN_EPS, scalar2=None,
                                                   op0=ALU.add), reads=["lnst5"], writes=["lnst5"])
    S.emit("act", lambda: nc.scalar.activation(out=st[:, 6:7], in_=st[:, 5:6], func=AF.Sqrt),
           reads=["lnst5"], writes=["lnst6"])
    S.emit("dve", lambda: nc.vector.reciprocal(out=st[:, 7:8], in_=st[:, 6:7]), reads=["lnst6"], writes=["lnst7"])
    S.emit("dve", lambda: nc.vector.tensor_scalar(out=z, in0=z, scalar1=st[:, 2:3], scalar2=st[:, 7:8],
                                                   op0=ALU.subtract, op1=ALU.mult),
           reads=[zk, "lnst23", "lnst7"], writes=[zk])
    S.emit("dve", lambda: nc.vector.tensor_tensor(out=z, in0=z, in1=g_b, op=ALU.mult), reads=[zk, "lnw"], writes=[zk])
    S.emit("dve", lambda: nc.vector.tensor_tensor(out=out_ap, in0=z, in1=b_b, op=ALU.add), reads=[zk, "lnw"], writes=[ok])


def emit_att(C, kind, pfx, x, o, ident_sb, ident_bf, PS8, stk):
    nc, S = C.nc, C.S
    G = 3 if kind == "dil" else 1
    DH = 96 if kind == "mla" else 64
    NKV = 1 if kind == "swa" else 8
    QT_d = C.dscratch(pfx + "QT_d", [G, 8, DH, SEQ], BF16)
    KT_d = C.dscratch(pfx + "KT_d", [G, NKV, 64, SEQ], BF16)
    V_d = C.dscratch(pfx + "V_d", [G, NKV, 128, NBLK, 64], BF16)
    if kind == "mla":
        KR_d = C.dscratch(pfx + "KR_d", [32, SEQ], BF16)
        w_in = C.din(pfx + "w_in", [1024, 416])
        qn = C.din(pfx + "qn", [128, 256])
        kvn = C.din(pfx + "kvn", [128, 128])
        w_uq = C.din(pfx + "w_uq", [256, 768])
        w_uk = C.din(pfx + "w_uk", [128, 512])
        w_uv = C.din(pfx + "w_uv", [128, 512])
        cosd = C.din(pfx + "cos", [SEQ, 16])
        sind = C.din(pfx + "sin", [SEQ, 16])
        maskd = C.din(pfx + "mask", [1, 128, 128])
        ND = [1]
    else:
        wq = C.din(pfx + "wq", [G, 1024, 512])
        wk = C.din(pfx + "wk", [G, 1024, NKV * 64])
        wv = C.din(pfx + "wv", [G, 1024, NKV * 64])
        if kind == "swa":
            ND = [2]
            sinkd = C.din(pfx + "sinks", [128, 8])
        else:
            ND = [d + 1 for d in DILS]
        biasd = [C.din(pfx + f"bias{g}", [ND[g], 128, 8, 128]) for g in range(G)]
        maskd = [C.din(pfx + f"mask{g}", [ND[g], 128, 128]) for g in range(G)]

    PSA = PS8[:, 0:4, :]
    PSB = PS8[:, 4:7, :]
    Bt = []
    if kind == "mla":
        cm_bf = C.sb(pfx + "cm_bf", [128, 128], BF16, stk)
        S.emit("poolq", lambda: nc.gpsimd.dma_start(out=cm_bf[:], in_=maskd[0]), writes=["bias"])
    else:
        for g in range(G):
            Bt.append(C.sb(pfx + f"B{g}", [128, ND[g], 8, 128], BF16, stk))
        if kind == "swa":
            esink = C.sb(pfx + "esink", [128, 8], F32, stk)
            S.emit("sp", lambda: nc.sync.dma_start(out=esink[:], in_=sinkd[:, :]), writes=["esink"])
            S.emit("act", lambda: nc.scalar.activation(out=esink[:], in_=esink[:], func=AF.Exp), reads=["esink"], writes=["esink"])

    with ExitStack() as p1:
        if kind != "mla":
            btmp = C.sb(pfx + "btmp", [128, 8, 128], F32, p1)
            mtmp = C.sb(pfx + "mtmp", [128, 128], F32, p1)
            for g in range(G):
                for d in range(ND[g]):
                    S.emit("sp", lambda: nc.sync.dma_start(out=btmp[:], in_=biasd[g][d]), writes=["btmp"])
                    S.emit("sp", lambda: nc.sync.dma_start(out=mtmp[:], in_=maskd[g][d]), writes=["mtmp"])
                    S.emit("dve", lambda: nc.vector.tensor_tensor(
                        out=Bt[g][:, d, :, :], in0=btmp[:], in1=mtmp[:].unsqueeze(1).to_broadcast([128, 8, 128]), op=ALU.add),
                        reads=["btmp", "mtmp"], writes=["bias"])
            wq_bf = C.sb(pfx + "wq_bf", [128, G, 8, 512], BF16, p1)
            wk_bf = C.sb(pfx + "wk_bf", [128, G, 8, NKV * 64], BF16, p1)
            wv_bf = C.sb(pfx + "wv_bf", [128, G, 8, NKV * 64], BF16, p1)
            for g in range(G):
                for c in range(8):
                    S.emit("poolq", lambda: nc.gpsimd.dma_start(out=wq_bf[:, g, c, :], in_=wq[g, c * 128:(c + 1) * 128, :]), writes=[("w", g, c, 0)])
                    S.emit("poolq", lambda: nc.gpsimd.dma_start(out=wk_bf[:, g, c, :], in_=wk[g, c * 128:(c + 1) * 128, :]), writes=[("w", g, c, 1)])
                    S.emit("poolq", lambda: nc.gpsimd.dma_start(out=wv_bf[:, g, c, :], in_=wv[g, c * 128:(c + 1) * 128, :]), writes=[("w", g, c, 2)])
            xin = [C.sb(pfx + f"xin{i}", [128, 4, 1024], F32, p1) for i in range(2)]
            xT = [C.sb(pfx + f"xTb{i}", [128, 8, 512], BF16, p1) for i in range(2)]
            NST = 4
            stg = [C.sb(pfx + f"stg{i}", [128, 512], BF16, p1) for i in range(NST)]
            vst = [C.sb(pfx + f"vst{i}", [128, 4, NKV * 64], BF16, p1) for i in range(2)]
            sti = 0
            ppi = 0
            evi = 0

            def evac(out_ap, in_ap, scale, reads, writes):
                nonlocal evi
                evi += 1
                if evi % 2 == 0:
                    S.emit("act", lambda: nc.scalar.activation(out=out_ap, in_=in_ap, func=AF.Copy, scale=float(scale)),
                           reads=reads, writes=writes)
                else:
                    S.emit("dve", lambda: nc.vector.tensor_scalar(out=out_ap, in0=in_ap, scalar1=float(scale), scalar2=None,
                                                                   op0=ALU.mult), reads=reads, writes=writes)

            for tc in range(SEQ // 512):
                xb = xin[tc % 2]
                xk = ("xin", tc % 2)
                S.emit("sp", lambda: nc.sync.dma_start(out=xb[:], in_=x[tc * 512:(tc + 1) * 512, :].rearrange("(n p) d -> p n d", p=128)),
                       writes=[xk])
                xt = xT[tc % 2]
                xtk = ("xT", tc % 2)
                for ti in range(4):
                    pb = (tc * 4 + ti) % 2
                    pst = PSA[:, 2 * pb:2 * pb + 2, :].rearrange("p b (c t) -> p (b c) t", t=128)
                    for c in range(8):
                        S.emit("pe", lambda: nc.tensor.transpose(out=pst[:, c, :], in_=xb[:, ti, c * 128:(c + 1) * 128], identity=ident_sb[:]),
                               reads=[xk, "ident"], writes=[("A", 2 * pb + c // 4)])
                    evac(xt[:, :, ti * 128:(ti + 1) * 128], pst, 1.0, [("A", 2 * pb), ("A", 2 * pb + 1)], [(xtk, ti)])
                xt_keys = [(xtk, ti) for ti in range(4)]
                for g in range(G):
                    for (wbf, dst_d, nh, scale, wi) in ((wq_bf, QT_d, 8, 0.125, 0), (wk_bf, KT_d, NKV, 1.0, 1)):
                        for h in range(nh):
                            pp = ppi % 3
                            ppi += 1
                            for c in range(8):
                                S.emit("pe", lambda: nc.tensor.matmul(out=PSB[0:64, pp, :], lhsT=wbf[:, g, c, h * 64:(h + 1) * 64],
                                                                      rhs=xt[:, c, :], start=(c == 0), stop=(c == 7)),
                                       reads=xt_keys + [("w", g, c, wi)], writes=[("B", pp)])
                            sg = stg[sti % NST]
                            sk = ("stg", sti % NST)
                            sti += 1
                            evac(sg[0:64, :], PSB[0:64, pp, :], scale, [("B", pp)], [sk])
                            S.emit("sp", lambda: nc.sync.dma_start(out=dst_d[g, h, :, tc * 512:(tc + 1) * 512], in_=sg[0:64, :]),
                                   reads=[sk], writes=[("scr", wi, g, h, tc)])
                    vs = vst[(tc * G + g) % 2]
                    vk = ("vst", (tc * G + g) % 2)
                    for ti in range(4):
                        pp = ppi % 3
                        ppi += 1
                        for c in range(8):
                            S.emit("pe", lambda: nc.tensor.matmul(out=PSB[:, pp, 0:NKV * 64], lhsT=xt[:, c, ti * 128:(ti + 1) * 128],
                                                                  rhs=wv_bf[:, g, c, :], start=(c == 0), stop=(c == 7)),
                                   reads=xt_keys + [("w", g, c, 2)], writes=[("B", pp)])
                        evac(vs[:, ti, :], PSB[:, pp, 0:NKV * 64], 1.0, [("B", pp)], [(vk, ti)])
                    for h in range(NKV):
                        S.emit("sp", lambda: nc.sync.dma_start(out=V_d[g, h, :, tc * 4:(tc + 1) * 4, :], in_=vs[:, :, h * 64:(h + 1) * 64]),
                               reads=[(vk, ti) for ti in range(4)], writes=[("scr", 2, g, h, tc)])
        else:
            w_in_bf = C.sb(pfx + "w_in_bf", [128, 8, 416], BF16, p1)
            w_uq_bf = C.sb(pfx + "w_uq_bf", [128, 2, 768], BF16, p1)
            w_uk_bf = C.sb(pfx + "w_uk_bf", [128, 512], BF16, p1)
            w_uv_bf = C.sb(pfx + "w_uv_bf", [128, 512], BF16, p1)
            qn_b = C.sb(pfx + "qn_b", [128, 256], F32, p1)
            kvn_b = C.sb(pfx + "kvn_b", [128, 128], F32, p1)
            cos_sb = C.sb(pfx + "cos_sb", [128, NBLK, 16], F32, p1)
            sin_sb = C.sb(pfx + "sin_sb", [128, NBLK, 16], F32, p1)
            for c in range(8):
                S.emit("poolq", lambda: nc.gpsimd.dma_start(out=w_in_bf[:, c, :], in_=w_in[c * 128:(c + 1) * 128, :]), writes=[("w", c)])
            for c in range(2):
                S.emit("poolq", lambda: nc.gpsimd.dma_start(out=w_uq_bf[:, c, :], in_=w_uq[c * 128:(c + 1) * 128, :]), writes=[("wuq", c)])
            S.emit("poolq", lambda: nc.gpsimd.dma_start(out=w_uk_bf[:], in_=w_uk[:, :]), writes=["wuk"])
            S.emit("poolq", lambda: nc.gpsimd.dma_start(out=w_uv_bf[:], in_=w_uv[:, :]), writes=["wuv"])
            S.emit("sp", lambda: nc.sync.dma_start(out=qn_b[:], in_=qn[:, :]), writes=["qn"])
            S.emit("sp", lambda: nc.sync.dma_start(out=kvn_b[:], in_=kvn[:, :]), writes=["kvn"])
            S.emit("sp", lambda: nc.sync.dma_start(out=cos_sb[:], in_=cosd.rearrange("(n p) j -> p n j", p=128)), writes=["cos"])
            S.emit("sp", lambda: nc.sync.dma_start(out=sin_sb[:], in_=sind.rearrange("(n p) j -> p n j", p=128)), writes=["sin"])
            xin = [C.sb(pfx + f"xin{i}", [128, 1024], F32, p1) for i in range(2)]
            xTt = C.sb(pfx + "xTt", [128, 8, 128], BF16, p1)
            c_sb = C.sb(pfx + "c_sb", [128, 416], F32, p1)
            junk = C.sb(pfx + "junkm", [128, 256], F32, p1)
            rst = C.sb(pfx + "rst", [128, 8], F32, p1)
            cqn = C.sb(pfx + "cqn", [128, 256], F32, p1)
            ckvn = C.sb(pfx + "ckvn", [128, 128], F32, p1)
            krr = C.sb(pfx + "krr", [128, 32], F32, p1)
            rt = C.sb(pfx + "rt", [128, 4, 16], F32, p1)
            cqnT = C.sb(pfx + "cqnT", [128, 2, 128], BF16, p1)
            ckvnT = C.sb(pfx + "ckvnT", [128, 128], BF16, p1)
            krT = C.sb(pfx + "krT", [32, 128], BF16, p1)
            q_sb = C.sb(pfx + "q_sb", [128, 8, 96], F32, p1)
            q_bf = C.sb(pfx + "q_bf", [128, 8, 96], F32, p1)
            qrt = C.sb(pfx + "qrt", [128, 4, 8, 16], F32, p1)
            qTs = C.sb(pfx + "qTs", [96, 8, 128], BF16, p1)
            kTs = C.sb(pfx + "kTs", [64, 8, 128], BF16, p1)
            vs = C.sb(pfx + "vs", [128, 512], BF16, p1)
            QSC = 96 ** -0.5
            import os as _os2
            _MSTOP = int(_os2.environ.get('MLA_STOP', '99'))
            _MSUB = int(_os2.environ.get('MLA_SUB', '0'))
            for t in range(NBLK):
                xb = xin[t % 2]
                xk = ("xin", t % 2)
                S.emit("sp", lambda: nc.sync.dma_start(out=xb[:], in_=x[t * 128:(t + 1) * 128, :]), writes=[xk])
                pst = PSA[:, 0:2, :].rearrange("p b (c t) -> p (b c) t", t=128)
                for c in range(8):
                    S.emit("pe", lambda: nc.tensor.transpose(out=pst[:, c, :], in_=xb[:, c * 128:(c + 1) * 128], identity=ident_sb[:]),
                           reads=[xk, "ident"], writes=[("A", c // 4)])
                S.emit("act", lambda: nc.scalar.copy(out=xTt[:], in_=pst), reads=[("A", 0), ("A", 1)], writes=["xTt"])
                for c in range(8):
                    S.emit("pe", lambda: nc.tensor.matmul(out=PSA[:, 2, 0:416], lhsT=xTt[:, c, :], rhs=w_in_bf[:, c, :],
                                                          start=(c == 0), stop=(c == 7)),
                           reads=["xTt", ("w", c)], writes=[("A", 2)])
                S.emit("dve", lambda: nc.vector.tensor_copy(out=c_sb[:], in_=PSA[:, 2, 0:416]), reads=[("A", 2)], writes=["c"])
                if _MSTOP <= 2:
                    continue
                S.emit("dve", lambda: nc.vector.scalar_tensor_tensor(out=junk[:], in0=c_sb[:, 0:256], scalar=1.0 / 256, in1=c_sb[:, 0:256],
                                                                      op0=ALU.mult, op1=ALU.mult, accum_out=rst[:, 0:1]),
                       reads=["c"], writes=["junkm", "rst0"])
                S.emit("dve", lambda: nc.vector.scalar_tensor_tensor(out=junk[:, 0:128], in0=c_sb[:, 256:384], scalar=1.0 / 128, in1=c_sb[:, 256:384],
                                                                      op0=ALU.mult, op1=ALU.mult, accum_out=rst[:, 1:2]),
                       reads=["c"], writes=["junkm", "rst1"])
                S.emit("dve", lambda: nc.vector.tensor_scalar(out=rst[:, 2:4], in0=rst[:, 0:2], scalar1=RMS_EPS, scalar2=None, op0=ALU.add),
                       reads=["rst0", "rst1"], writes=["rst23"])
                S.emit("act", lambda: nc.scalar.activation(out=rst[:, 4:6], in_=rst[:, 2:4], func=AF.Sqrt), reads=["rst23"], writes=["rst45"])
                S.emit("dve", lambda: nc.vector.reciprocal(out=rst[:, 6:8], in_=rst[:, 4:6]), reads=["rst45"], writes=["rst67"])
                S.emit("dve", lambda: nc.vector.scalar_tensor_tensor(out=cqn[:], in0=c_sb[:, 0:256], scalar=rst[:, 6:7], in1=qn_b[:],
                                                                      op0=ALU.mult, op1=ALU.mult), reads=["c", "rst67", "qn"], writes=["cqn"])
                S.emit("dve", lambda: nc.vector.scalar_tensor_tensor(out=ckvn[:], in0=c_sb[:, 256:384], scalar=rst[:, 7:8], in1=kvn_b[:],
                                                                      op0=ALU.mult, op1=ALU.mult), reads=["c", "rst67", "kvn"], writes=["ckvn"])
                if _MSTOP <= 3:
                    continue
                k1, k2 = c_sb[:, 384:400], c_sb[:, 400:416]
                cs, sn = cos_sb[:, t, :], sin_sb[:, t, :]
                S.emit("dve", lambda: nc.vector.tensor_tensor(out=rt[:, 0, :], in0=k1, in1=cs, op=ALU.mult), reads=["c", "cos"], writes=["rt0"])
                S.emit("dve", lambda: nc.vector.tensor_tensor(out=rt[:, 1, :], in0=k2, in1=sn, op=ALU.mult), reads=["c", "sin"], writes=["rt1"])
                S.emit("dve", lambda: nc.vector.tensor_tensor(out=rt[:, 2, :], in0=k1, in1=sn, op=ALU.mult), reads=["c", "sin"], writes=["rt2"])
                S.emit("dve", lambda: nc.vector.tensor_tensor(out=rt[:, 3, :], in0=k2, in1=cs, op=ALU.mult), reads=["c", "cos"], writes=["rt3"])
                S.emit("dve", lambda: nc.vector.tensor_tensor(out=krr[:, 0:16], in0=rt[:, 0, :], in1=rt[:, 1, :], op=ALU.subtract),
                       reads=["rt0", "rt1"], writes=["krr0"])
                S.emit("dve", lambda: nc.vector.tensor_tensor(out=krr[:, 16:32], in0=rt[:, 2, :], in1=rt[:, 3, :], op=ALU.add),
                       reads=["rt2", "rt3"], writes=["krr1"])
                if _MSTOP <= 4:
                    continue
                PS3 = PSA[:, 3, :]
                for c in range(2):
                    S.emit("pe", lambda: nc.tensor.transpose(out=PS3[:, c * 128:(c + 1) * 128], in_=cqn[:, c * 128:(c + 1) * 128], identity=ident_sb[:]),
                           reads=["cqn", "ident"], writes=[("A", 3)])
                S.emit("pe", lambda: nc.tensor.transpose(out=PS3[:, 256:384], in_=ckvn[:], identity=ident_sb[:]),
                       reads=["ckvn", "ident"], writes=[("A", 3)])
                if _MSUB != 1:
                    S.emit("pe", lambda: nc.tensor.transpose(out=PS3[0:32, 384:512], in_=krr[:], identity=ident_sb[:]),
                           reads=["krr0", "krr1", "ident"], writes=[("A", 3)])
                S.emit("act", lambda: nc.scalar.copy(out=cqnT[:].rearrange("p c t -> p (c t)"), in_=PS3[:, 0:256]), reads=[("A", 3)], writes=["cqnT"])
                S.emit("act", lambda: nc.scalar.copy(out=ckvnT[:], in_=PS3[:, 256:384]), reads=[("A", 3)], writes=["ckvnT"])
                S.emit("act", lambda: nc.scalar.copy(out=krT[:], in_=PS3[0:32, 384:512]), reads=[("A", 3)], writes=["krT"])
                if _MSUB != 3:
                    S.emit("sp", lambda: nc.sync.dma_start(out=KR_d[:, t * 128:(t + 1) * 128], in_=krT[:]), reads=["krT"], writes=[("scr", "kr", t)])
                if _MSTOP <= 5:
                    continue
                for half in range(2):
                    for c in range(2):
                        S.emit("pe", lambda: nc.tensor.matmul(out=PSB[:, half, 0:384], lhsT=cqnT[:, c, :],
                                                              rhs=w_uq_bf[:, c, half * 384:(half + 1) * 384], start=(c == 0), stop=(c == 1)),
                               reads=["cqnT", ("wuq", c)], writes=[("B", half)])
                    S.emit("act", lambda: nc.scalar.activation(out=q_sb[:, half * 4:(half + 1) * 4, :].rearrange("p h d -> p (h d)"),
                                                               in_=PSB[:, half, 0:384], func=AF.Copy, scale=QSC),
                           reads=[("B", half)], writes=[("q_sb", half)])
                qk = [("q_sb", 0), ("q_sb", 1)]
                q1, q2 = q_sb[:, :, 64:80], q_sb[:, :, 80:96]
                csb = cos_sb[:, t, :].unsqueeze(1).to_broadcast([128, 8, 16])
                snb = sin_sb[:, t, :].unsqueeze(1).to_broadcast([128, 8, 16])
                S.emit("dve", lambda: nc.vector.tensor_tensor(out=qrt[:, 0, :, :], in0=q1, in1=csb, op=ALU.mult), reads=qk + ["cos"], writes=["qrt0"])
                S.emit("dve", lambda: nc.vector.tensor_tensor(out=qrt[:, 1, :, :], in0=q2, in1=snb, op=ALU.mult), reads=qk + ["sin"], writes=["qrt1"])
                S.emit("dve", lambda: nc.vector.tensor_tensor(out=qrt[:, 2, :, :], in0=q1, in1=snb, op=ALU.mult), reads=qk + ["sin"], writes=["qrt2"])
                S.emit("dve", lambda: nc.vector.tensor_tensor(out=qrt[:, 3, :, :], in0=q2, in1=csb, op=ALU.mult), reads=qk + ["cos"], writes=["qrt3"])
                S.emit("dve", lambda: nc.vector.tensor_copy(out=q_bf[:, :, 0:64], in_=q_sb[:, :, 0:64]), reads=qk, writes=["q_bf0"])
                S.emit("dve", lambda: nc.vector.tensor_tensor(out=q_bf[:, :, 64:80], in0=qrt[:, 0, :, :], in1=qrt[:, 1, :, :], op=ALU.subtract),
                       reads=["qrt0", "qrt1"], writes=["q_bf1"])
                S.emit("dve", lambda: nc.vector.tensor_tensor(out=q_bf[:, :, 80:96], in0=qrt[:, 2, :, :], in1=qrt[:, 3, :, :], op=ALU.add),
                       reads=["qrt2", "qrt3"], writes=["q_bf2"])
                if _MSTOP <= 6:
                    continue
                PQT = PSA[:, 0:2, :].rearrange("p b (h t) -> p (b h) t", t=128)
                for h in range(8):
                    S.emit("pe", lambda: nc.tensor.transpose(out=PQT[0:96, h, :], in_=q_bf[:, h, :], identity=ident_sb[:]),
                           reads=["q_bf0", "q_bf1", "q_bf2", "ident"], writes=[("A", h // 4)])
                S.emit("act", lambda: nc.scalar.copy(out=qTs[:], in_=PQT[0:96, :, :]), reads=[("A", 0), ("A", 1)], writes=["qTs"])
                S.emit("sp", lambda: nc.sync.dma_start(out=QT_d[0, :, :, t * 128:(t + 1) * 128].rearrange("h d t -> d h t"), in_=qTs[:]),
                       reads=["qTs"], writes=[("scr", "q", t)])
                if _MSTOP <= 7:
                    continue
                for h in range(8):
                    dst = PSA[0:64, 3, (h % 4) * 128:(h % 4 + 1) * 128] if h < 4 else PSB[0:64, 2, (h % 4) * 128:(h % 4 + 1) * 128]
                    S.emit("pe", lambda: nc.tensor.matmul(out=dst, lhsT=w_uk_bf[:, h * 64:(h + 1) * 64], rhs=ckvnT[:], start=True, stop=True),
                           reads=["wuk", "ckvnT"], writes=[("A", 3) if h < 4 else ("B", 2)])
                S.emit("dve", lambda: nc.vector.tensor_copy(out=kTs[:, 0:4, :].rearrange("p h t -> p (h t)"), in_=PSA[0:64, 3, :]),
                       reads=[("A", 3)], writes=["kTs0"])
                S.emit("act", lambda: nc.scalar.copy(out=kTs[:, 4:8, :].rearrange("p h t -> p (h t)"), in_=PSB[0:64, 2, :]),
                       reads=[("B", 2)], writes=["kTs1"])
                S.emit("sp", lambda: nc.sync.dma_start(out=KT_d[0, :, :, t * 128:(t + 1) * 128].rearrange("h d t -> d h t"), in_=kTs[:]),
                       reads=["kTs0", "kTs1"], writes=[("scr", "k", t)])
                S.emit("pe", lambda: nc.tensor.matmul(out=PSA[:, 2, :], lhsT=ckvnT[:], rhs=w_uv_bf[:], start=True, stop=True),
                       reads=["wuv", "ckvnT"], writes=[("A", 2)])
                S.emit("dve", lambda: nc.vector.tensor_copy(out=vs[:], in_=PSA[:, 2, :]), reads=[("A", 2)], writes=["vs"])
                S.emit("sp", lambda: nc.sync.dma_start(out=V_d[0, :, :, t, :].rearrange("h p d -> p h d"),
                                                       in_=vs[:].rearrange("p (h d) -> p h d", d=64)),
                       reads=["vs"], writes=[("scr", "v", t)])
        S.barrier()

    QT_u = C.sb(pfx + "QT_u", [DH, SEQ], BF16, stk)
    KT_u = C.sb(pfx + "KT_u", [DH, SEQ], BF16, stk)
    V_u = C.sb(pfx + "V_u", [128, NBLK, 65], BF16, stk)
    Oacc = C.sb(pfx + "Oacc", [128, NBLK, 65], F32, stk)
    o_st = C.sb(pfx + "o_st", [128, NBLK, 64], F32, stk)
    den = C.sb(pfx + "den", [128, NBLK], F32, stk)
    NPT = 4
    PT = [C.sb(pfx + f"PT{i}", [128, 4, 128], BF16, stk) for i in range(NPT)]
    S.emit("dve", lambda: nc.vector.memset(V_u[:, :, 64:65], 1.0), writes=["V_ones"])
    SPv = [PSA[:, i, :].rearrange("p (a q) -> p a q", q=128) for i in range(3)]
    OPv = [PSB[:, i, :] for i in range(3)]

    grp_ctr = 0
    blk_ctr = 0
    import os as _os
    for hl in range(int(_os.environ.get('ATT_HEADS', '8'))):
        for g in range(G):
            kvh = 0 if kind == "swa" else hl
            S.emit("sp", lambda: nc.sync.dma_start(out=QT_u[:], in_=QT_d[g, hl]), writes=["QT_u"])
            S.emit("sp", lambda: nc.sync.dma_start(out=KT_u[0:64, :], in_=KT_d[g, kvh]), writes=["KT_u"])
            if kind == "mla":
                S.emit("sp", lambda: nc.sync.dma_start(out=KT_u[64:96, :], in_=KR_d[:, :]), writes=["KT_u2"])
            S.emit("sp", lambda: nc.sync.dma_start(out=V_u[:, :, 0:64], in_=V_d[g, kvh]), writes=["V_u"])
            kt_keys = ["KT_u", "KT_u2"] if kind == "mla" else ["KT_u"]
            dil = DILS[g] if kind == "dil" else 1
            nd = ND[g]
            work = []
            for n in range(NBLK):
                if kind == "mla":
                    kbs = [(kb, n - kb) for kb in range(0, n + 1)]
                else:
                    kbs = [(n - d, d) for d in range(nd - 1, -1, -1) if n - d >= 0]
                groups = [kbs[i:i + 4] for i in range(0, len(kbs), 4)]
                for gi, grp in enumerate(groups):
                    work.append((n, gi, len(groups), grp))

            def emit_qk(item, sp_i):
                n, gi, ng, grp = item
                for i, (kb, d) in enumerate(grp):
                    if kind == "mla":
                        has_b = (d == 0)
                        bt = cm_bf[:, :] if has_b else None
                    else:
                        has_b = True
                        bt = Bt[g][:, d, hl, :]
                    S.emit("pe", lambda: nc.tensor.matmul(out=SPv[sp_i][:, i, :], lhsT=KT_u[:, kb * 128:(kb + 1) * 128],
                                                          rhs=QT_u[:, n * 128:(n + 1) * 128], start=True, stop=(not has_b)),
                           reads=["QT_u"] + kt_keys, writes=[("SP", sp_i)])
                    if has_b:
                        S.emit("pe", lambda: nc.tensor.matmul(out=SPv[sp_i][:, i, :], lhsT=ident_bf[:], rhs=bt, start=False, stop=True),
                               reads=["identbf", "bias"], writes=[("SP", sp_i)])

            def emit_pv(item, sp_i, pt_i):
                n, gi, ng, grp = item
                L = len(grp)
                S.emit("act", lambda: nc.scalar.activation(out=PT[pt_i][:, 0:L, :], in_=SPv[sp_i][:, 0:L, :], func=AF.Exp),
                       reads=[("SP", sp_i)], writes=[("PT", pt_i)])
                slot = n % 3
                for i, (kb, d) in enumerate(grp):
                    S.emit("pe", lambda: nc.tensor.matmul(out=OPv[slot][:, 0:65], lhsT=PT[pt_i][:, i, :], rhs=V_u[:, kb, :],
                                                          start=(gi == 0 and i == 0), stop=(gi == ng - 1 and i == L - 1)),
                           reads=[("PT", pt_i), "V_u", "V_ones"], writes=[("OP", slot)])
                if gi == ng - 1:
                    if g == 0:
                        S.emit("dve", lambda: nc.vector.tensor_copy(out=Oacc[:, n, :], in_=OPv[slot][:, 0:65]),
                               reads=[("OP", slot)], writes=[("Oacc", n)])
                    else:
                        S.emit("dve", lambda: nc.vector.tensor_tensor(out=Oacc[:, n, :], in0=Oacc[:, n, :], in1=OPv[slot][:, 0:65], op=ALU.add),
                               reads=[("OP", slot), ("Oacc", n)], writes=[("Oacc", n)])

            LAG = 2
            pend = []
            for item in work:
                sp_i = grp_ctr % 3
                pt_i = grp_ctr % NPT
                grp_ctr += 1
                emit_qk(item, sp_i)
                pend.append((item, sp_i, pt_i))
                if len(pend) > LAG:
                    emit_pv(*pend.pop(0))
            while pend:
                emit_pv(*pend.pop(0))
            if g == G - 1:
                ok_keys = [("Oacc", n) for n in range(NBLK)]
                if kind == "swa":
                    S.emit("dve", lambda: nc.vector.tensor_scalar(out=den[:], in0=Oacc[:, :, 64], scalar1=esink[:, hl:hl + 1], scalar2=None,
                                                                   op0=ALU.add), reads=ok_keys + ["esink"], writes=["den"])
                    S.emit("dve", lambda: nc.vector.reciprocal(out=den[:], in_=den[:]), reads=["den"], writes=["den"])
                else:
                    S.emit("dve", lambda: nc.vector.reciprocal(out=den[:], in_=Oacc[:, :, 64]), reads=ok_keys, writes=["den"])
                S.emit("dve", lambda: nc.vector.tensor_tensor(out=o_st[:], in0=Oacc[:, :, 0:64],
                                                              in1=den[:].unsqueeze(2).to_broadcast([128, NBLK, 64]), op=ALU.mult),
                       reads=ok_keys + ["den"], writes=["o_st"])
                S.emit("sp", lambda: nc.sync.dma_start(out=o[:, hl * 64:(hl + 1) * 64].rearrange("(n p) d -> p n d", p=128), in_=o_st[:]),
                       reads=["o_st"], writes=[("o", hl)])
    S.barrier()


def emit_post(C, pfx, x, oin, y, NT, ident_sb, iota_sb, PS8, stk, tokidx=None):
    nc, S = C.nc, C.S
    w_out = C.din(pfx + "w_out", [1024, 1024])
    wq = C.din(pfx + "wq", [1024, 2048])
    keysT = C.din(pfx + "keysT", [128, 2048])
    u = C.din(pfx + "u", [16384, 1024])
    v = C.din(pfx + "v", [16384, 1024])
    lnw = C.din(pfx + "lnw", [4, 128, 1024])

    def sb(name, shape, dt):
        return C.sb(pfx + name, shape, dt, stk)

    wq_bf = sb("wq_bf", [128, 8, 2048], BF16)
    wo_bf = sb("wo_bf", [128, 8, 1024], BF16)
    keysT_bf = sb("keysT_bf", [128, 16, 128], BF16)
    ln_sb = sb("ln_sb", [128, 4, 1024], F32)
    xa = [sb(f"xa{i}", [128, 1024], F32) for i in range(2)]
    oa = [sb(f"oa{i}", [128, 1024], F32) for i in range(2)]
    x1 = sb("x1", [128, 1024], F32)
    oT_bf = sb("oT_bf", [128, 8, 128], BF16)
    xT_bf = sb("xT_bf", [128, 8, 128], BF16)
    qT_bf = sb("qT_bf", [128, 16, 128], BF16)
    s_sb = sb("s_sb", [128, 16, 128], F32)
    s_wk = sb("s_wk", [128, 16, 128], F32)
    oh = s_sb[:].rearrange("p a b -> p (a b)").rearrange("p (h k a) -> p h k a", h=8, k=16)
    stop = sb("stop", [128, 8, 2, 16], F32)
    itop = sb("itop", [128, 8, 2, 16], U32)
    itopf = sb("itopf", [128, 8, 2, 16], F32)
    cand = sb("cand", [128, 8, 16, 16], F32)
    best = sb("best", [128, 8, 16], F32)
    sel = sb("sel", [128, 8, 16], U32)
    selA = sb("selA", [128, 8, 16], U32)
    selB = sb("selB", [128, 8, 16], U32)
    selAf = sb("selAf", [128, 8, 16], F32)
    selBf = sb("selBf", [128, 8, 16], F32)
    i1sel = sb("i1sel", [128, 8, 16], F32)
    i2sel = sb("i2sel", [128, 8, 16], F32)
    idxf = sb("idxf", [128, 128], F32)
    idx = sb("idx", [128, 128], U32)
    gd = sb("gd", [128, 8, 16], F32)
    gz = sb("gz", [128, 8], F32)
    gate = sb("gate", [128, 128], F32)
    hacc = sb("hacc", [128, 128], F32)
    gh = sb("gh", [128, 128], F32)
    NB = 6
    gbuf = [sb(f"gbuf{i}", [128, 1024], F32) for i in range(NB)]
    yacc = sb("yacc", [128, 1024], F32)
    junk = sb("junk", [128, 1024], F32)
    lnst = sb("lnst", [128, 8], F32)
    zt = sb("zt", [128, 1024], F32)
    ot = [sb(f"ot{i}", [128, 1024], F32) for i in range(2)]
    psA = PS8[:, 0:4, :].rearrange("p b (c t) -> p (b c) t", t=128)
    psB = PS8[:, 4:6, :].rearrange("p b (c t) -> p (b c) t", t=128)
    psY = PS8[:, 6:8, :]

    for c in range(8):
        S.emit("poolq", lambda: nc.gpsimd.dma_start(out=wq_bf[:, c, :], in_=wq[c * 128:(c + 1) * 128, :]), writes=[("wq", c)])
        S.emit("poolq", lambda: nc.gpsimd.dma_start(out=wo_bf[:, c, :], in_=w_out[c * 128:(c + 1) * 128, :]), writes=[("wo", c)])
    S.emit("poolq", lambda: nc.gpsimd.dma_start(out=keysT_bf[:].rearrange("p a b -> p (a b)"), in_=keysT[:, :]), writes=["keysT"])
    for i in range(4):
        S.emit("sp", lambda: nc.sync.dma_start(out=ln_sb[:, i, :], in_=lnw[i]), writes=["lnw"] if i == 3 else [("lnw", i)])
    lnw_all = [("lnw", 0), ("lnw", 1), ("lnw", 2), "lnw"]

    def prefetch(t):
        if tokidx is None:
            S.emit("sp", lambda: nc.sync.dma_start(out=xa[t % 2][:], in_=x[t * 128:(t + 1) * 128, :]), writes=[("xa", t % 2)])
            S.emit("sp", lambda: nc.sync.dma_start(out=oa[t % 2][:], in_=oin[t * 128:(t + 1) * 128, :]), writes=[("oa", t % 2)])
        else:
            S.emit("poolq", lambda: nc.gpsimd.indirect_dma_start(
                out=xa[t % 2][:], out_offset=None, in_=x, in_offset=bass.IndirectOffsetOnAxis(ap=tokidx[:, t:t + 1], axis=0)),
                reads=["tokidx"], writes=[("xa", t % 2)])
            S.emit("poolq", lambda: nc.gpsimd.indirect_dma_start(
                out=oa[t % 2][:], out_offset=None, in_=oin, in_offset=bass.IndirectOffsetOnAxis(ap=tokidx[:, t:t + 1], axis=0)),
                reads=["tokidx"], writes=[("oa", t % 2)])

    prefetch(0)
    gi = 0
    for t in range(NT):
        if t + 1 < NT:
            prefetch(t + 1)
        xb, ob = xa[t % 2], oa[t % 2]
        xk, okk = ("xa", t % 2), ("oa", t % 2)
        for c in range(8):
            S.emit("pe", lambda: nc.tensor.transpose(out=psB[:, c, :], in_=ob[:, c * 128:(c + 1) * 128], identity=ident_sb[:]),
                   reads=[okk, "ident"], writes=[("psB", c)])
        S.emit("act", lambda: nc.scalar.copy(out=oT_bf[:], in_=psB), reads=[("psB", c) for c in range(8)], writes=["oT"])
        for half in range(2):
            for c in range(8):
                S.emit("pe", lambda: nc.tensor.matmul(out=psY[:, half, :], lhsT=oT_bf[:, c, :], rhs=wo_bf[:, c, half * 512:(half + 1) * 512],
                                                      start=(c == 0), stop=(c == 7)),
                       reads=["oT", ("wo", c)], writes=[("psY", half)])
        S.emit("dve", lambda: nc.vector.scalar_tensor_tensor(out=zt[:], in0=xb[:], scalar=DN_ALPHA, in1=psY.rearrange("p a b -> p (a b)"),
                                                              op0=ALU.mult, op1=ALU.add),
               reads=[xk, ("psY", 0), ("psY", 1)], writes=["zt"])
        if t == 0:
            S.emit("dve", lambda: nc.vector.tensor_copy(out=lnst[:, 0:1], in_=ln_sb[:, 0, 0:1]), reads=lnw_all, writes=["lnst0"])
        emit_ln(C, zt[:], "zt", x1[:], "x1", ln_sb[:, 0, :], ln_sb[:, 1, :], lnst, junk)
        for c in range(8):
            S.emit("pe", lambda: nc.tensor.transpose(out=psB[:, c, :], in_=x1[:, c * 128:(c + 1) * 128], identity=ident_sb[:]),
                   reads=["x1", "ident"], writes=[("psB", c)])
        S.emit("act", lambda: nc.scalar.copy(out=xT_bf[:], in_=psB), reads=[("psB", c) for c in range(8)], writes=["xT"])
        for blk in range(16):
            for c in range(8):
                S.emit("pe", lambda: nc.tensor.matmul(out=psA[:, blk, :], lhsT=wq_bf[:, c, blk * 128:(blk + 1) * 128], rhs=xT_bf[:, c, :],
                                                      start=(c == 0), stop=(c == 7)),
                       reads=[("wq", c), "xT"], writes=[("psA", blk)])
        S.emit("act", lambda: nc.scalar.copy(out=qT_bf[:], in_=psA), reads=[("psA", b) for b in range(16)], writes=["qT"])
        for hp in range(16):
            S.emit("pe", lambda: nc.tensor.matmul(out=psA[:, hp, :], lhsT=qT_bf[:, hp, :], rhs=keysT_bf[:, hp, :], start=True, stop=True),
                   reads=["qT", "keysT"], writes=[("psA", hp)])
        S.emit("dve", lambda: nc.vector.tensor_copy(out=s_sb[:], in_=psA), reads=[("psA", b) for b in range(16)], writes=["s"])
        for hp in range(16):
            h_, p_ = hp // 2, hp % 2
            sv = s_sb[:, hp, :]
            sw = s_wk[:, hp, :]
            S.emit("dve", lambda: nc.vector.max(out=stop[:, h_, p_, 0:8], in_=sv), reads=["s"], writes=[("stop", hp, 0)])
            S.emit("dve", lambda: nc.vector.max_index(out=itop[:, h_, p_, 0:8], in_max=stop[:, h_, p_, 0:8], in_values=sv),
                   reads=["s", ("stop", hp, 0)], writes=[("itop", hp, 0)])
            S.emit("dve", lambda: nc.vector.match_replace(out=sw, in_to_replace=stop[:, h_, p_, 0:8], in_values=sv, imm_value=-1e30),
                   reads=["s", ("stop", hp, 0)], writes=[("swk", hp)])
            S.emit("dve", lambda: nc.vector.max(out=stop[:, h_, p_, 8:16], in_=sw), reads=[("swk", hp)], writes=[("stop", hp, 1)])
            S.emit("dve", lambda: nc.vector.max_index(out=itop[:, h_, p_, 8:16], in_max=stop[:, h_, p_, 8:16], in_values=sw),
                   reads=[("swk", hp), ("stop", hp, 1)], writes=[("itop", hp, 1)])
        stop_keys = [("stop", hp, i) for hp in range(16) for i in range(2)]
        itop_keys = [("itop", hp, i) for hp in range(16) for i in range(2)]
        S.emit("dve", lambda: nc.vector.tensor_tensor(
            out=cand[:], in0=stop[:, :, 0, :].unsqueeze(3).to_broadcast([128, 8, 16, 16]),
            in1=stop[:, :, 1, :].unsqueeze(2).to_broadcast([128, 8, 16, 16]), op=ALU.add), reads=stop_keys, writes=["cand"])
        for h_ in range(8):
            cv = cand[:, h_, :, :].rearrange("p a b -> p (a b)")
            cw = s_wk[:, 2 * h_:2 * h_ + 2, :].rearrange("p a b -> p (a b)")
            cwk = [("swk", 2 * h_), ("swk", 2 * h_ + 1)]
            S.emit("dve", lambda: nc.vector.max(out=best[:, h_, 0:8], in_=cv), reads=["cand"], writes=[("best", h_, 0)])
            S.emit("dve", lambda: nc.vector.max_index(out=sel[:, h_, 0:8], in_max=best[:, h_, 0:8], in_values=cv),
                   reads=["cand", ("best", h_, 0)], writes=[("sel", h_, 0)])
            S.emit("dve", lambda: nc.vector.match_replace(out=cw, in_to_replace=best[:, h_, 0:8], in_values=cv, imm_value=-1e30),
                   reads=["cand", ("best", h_, 0)], writes=cwk)
            S.emit("dve", lambda: nc.vector.max(out=best[:, h_, 8:16], in_=cw), reads=cwk, writes=[("best", h_, 1)])
            S.emit("dve", lambda: nc.vector.max_index(out=sel[:, h_, 8:16], in_max=best[:, h_, 8:16], in_values=cw),
                   reads=cwk + [("best", h_, 1)], writes=[("sel", h_, 1)])
        best_keys = [("best", h_, i) for h_ in range(8) for i in range(2)]
        sel_keys = [("sel", h_, i) for h_ in range(8) for i in range(2)]
        S.emit("dve", lambda: nc.vector.tensor_single_scalar(out=selA[:], in_=sel[:], scalar=4, op=ALU.logical_shift_right),
               reads=sel_keys, writes=["selA"])
        S.emit("dve", lambda: nc.vector.tensor_single_scalar(out=selB[:], in_=sel[:], scalar=15, op=ALU.bitwise_and),
               reads=sel_keys, writes=["selB"])
        S.emit("dve", lambda: nc.vector.tensor_copy(out=selAf[:], in_=selA[:]), reads=["selA"], writes=["selAf"])
        S.emit("dve", lambda: nc.vector.tensor_copy(out=selBf[:], in_=selB[:]), reads=["selB"], writes=["selBf"])
        S.emit("dve", lambda: nc.vector.tensor_copy(out=itopf[:], in_=itop[:]), reads=itop_keys, writes=["itopf"])
        iota_b = iota_sb[:].unsqueeze(1).unsqueeze(1).to_broadcast([128, 8, 16, 16])
        for (self_, pidx, dst, dkey, skey) in ((selAf, 0, i1sel, "i1sel", "selAf"), (selBf, 1, i2sel, "i2sel", "selBf")):
            S.emit("dve", lambda: nc.vector.tensor_tensor(out=oh, in0=iota_b, in1=self_[:].unsqueeze(3).to_broadcast([128, 8, 16, 16]),
                                                          op=ALU.is_equal), reads=["iota", skey], writes=["s"])
            S.emit("dve", lambda: nc.vector.tensor_tensor(out=oh, in0=oh, in1=itopf[:, :, pidx, :].unsqueeze(2).to_broadcast([128, 8, 16, 16]),
                                                          op=ALU.mult), reads=["s", "itopf"], writes=["s"])
            S.emit("dve", lambda: nc.vector.tensor_reduce(out=dst[:], in_=oh, axis=AX.X, op=ALU.add), reads=["s"], writes=[dkey])
        S.emit("dve", lambda: nc.vector.scalar_tensor_tensor(
            out=idxf[:], in0=i1sel[:].rearrange("p a b -> p (a b)"), scalar=128.0, in1=i2sel[:].rearrange("p a b -> p (a b)"),
            op0=ALU.mult, op1=ALU.add), reads=["i1sel", "i2sel"], writes=["idxf"])
        S.emit("dve", lambda: nc.vector.tensor_copy(out=idx[:], in_=idxf[:]), reads=["idxf"], writes=["idx"])
        S.emit("dve", lambda: nc.vector.tensor_tensor(out=gd[:], in0=best[:], in1=best[:, :, 0:1].to_broadcast([128, 8, 16]),
                                                      op=ALU.subtract), reads=best_keys, writes=["gd"])
        S.emit("act", lambda: nc.scalar.activation(out=gd[:], in_=gd[:], func=AF.Exp), reads=["gd"], writes=["gd"])
        S.emit("dve", lambda: nc.vector.tensor_reduce(out=gz[:], in_=gd[:], axis=AX.X, op=ALU.add), reads=["gd"], writes=["gz"])
        S.emit("dve", lambda: nc.vector.reciprocal(out=gz[:], in_=gz[:]), reads=["gz"], writes=["gz"])
        S.emit("dve", lambda: nc.vector.tensor_tensor(out=gate[:].rearrange("p (a b) -> p a b", b=16), in0=gd[:],
                                                      in1=gz[:].unsqueeze(2).to_broadcast([128, 8, 16]), op=ALU.mult),
               reads=["gd", "gz"], writes=["gate"])
        for j in range(128):
            b = gi % NB
            gi += 1
            S.emit("poolq", lambda: nc.gpsimd.indirect_dma_start(
                out=gbuf[b][:], out_offset=None, in_=u[:, :], in_offset=bass.IndirectOffsetOnAxis(ap=idx[:, j:j + 1], axis=0)),
                reads=["idx"], writes=[("gbuf", b)])
            S.emit("dve", lambda: nc.vector.scalar_tensor_tensor(out=junk[:], in0=gbuf[b][:], scalar=1.0, in1=x1[:], op0=ALU.mult,
                                                                  op1=ALU.mult, accum_out=hacc[:, j:j + 1]),
                   reads=[("gbuf", b), "x1"], writes=["junk", ("hacc", j)])
        S.emit("act", lambda: nc.scalar.activation(out=gh[:], in_=hacc[:], func=AF.Gelu), reads=[("hacc", j) for j in range(128)], writes=["gh"])
        S.emit("dve", lambda: nc.vector.tensor_tensor(out=gh[:], in0=gh[:], in1=gate[:], op=ALU.mult), reads=["gh", "gate"], writes=["gh"])
        for j in range(128):
            b = gi % NB
            gi += 1
            S.emit("poolq", lambda: nc.gpsimd.indirect_dma_start(
                out=gbuf[b][:], out_offset=None, in_=v[:, :], in_offset=bass.IndirectOffsetOnAxis(ap=idx[:, j:j + 1], axis=0)),
                reads=["idx"], writes=[("gbuf", b)])
            if j == 0:
                S.emit("dve", lambda: nc.vector.tensor_scalar(out=yacc[:], in0=gbuf[b][:], scalar1=gh[:, 0:1], scalar2=None, op0=ALU.mult),
                       reads=[("gbuf", b), "gh"], writes=["yacc"])
            else:
                S.emit("dve", lambda: nc.vector.scalar_tensor_tensor(out=yacc[:], in0=gbuf[b][:], scalar=gh[:, j:j + 1], in1=yacc[:],
                                                                      op0=ALU.mult, op1=ALU.add),
                       reads=[("gbuf", b), "gh", "yacc"], writes=["yacc"])
        S.emit("dve", lambda: nc.vector.scalar_tensor_tensor(out=zt[:], in0=x1[:], scalar=DN_ALPHA, in1=yacc[:], op0=ALU.mult, op1=ALU.add),
               reads=["x1", "yacc"], writes=["zt"])
        obuf = ot[t % 2]
        obk = ("ot", t % 2)
        emit_ln(C, zt[:], "zt", obuf[:], obk, ln_sb[:, 2, :], ln_sb[:, 3, :], lnst, junk)
        S.emit("sp", lambda: nc.sync.dma_start(out=y[t * 128:(t + 1) * 128, :], in_=obuf[:]), reads=[obk], writes=[("y", t)])
    S.barrier()


def _rel_bucket(dist):
    n = np.maximum(dist, 0)
    nf = np.maximum(n, 1).astype(np.float32)
    large = 16 + (np.log(nf / np.float32(16)) / np.float32(math.log(2048 / 16)) * np.float32(16)).astype(np.int32)
    large = np.minimum(large, 31)
    return np.where(n < 16, n, large)


def _bias_tiles(rel_bias, heads, dil, max_m, nd):
    k = np.arange(128)[:, None]
    q = np.arange(128)[None, :]
    bias = np.zeros((nd, 128, len(heads), 128), np.float32)
    mask = np.zeros((nd, 128, 128), np.float32)
    for d in range(nd):
        dist = q + d * 128 - k
        valid = (dist >= 0) & (dist % dil == 0) & (dist // dil <= max_m)
        bk = _rel_bucket(np.where(valid, dist, 0))
        bias[d] = np.transpose(rel_bias[bk][:, :, heads], (0, 2, 1))
        mask[d] = np.where(valid, np.float32(0), np.float32(NEGM))
    return bias, mask


_IDENT = np.eye(128, dtype=np.float32)
_IOTA16 = np.ascontiguousarray(np.broadcast_to(np.arange(16, dtype=np.float32)[None, :], (128, 16)))
_PROG = {}
KINDS = ("swa", "dil", "mla")


def _bc(vec, n=128):
    return np.ascontiguousarray(np.broadcast_to(np.asarray(vec, np.float32)[None, :], (n, vec.shape[0])))


def build_fused():
    C = Ctx()
    nc, S = C.nc, C.S
    x_in = C.din("x", [SEQ, 1024])
    ident = C.din("ident", [128, 128])
    iota = C.din("iota16", [128, 16])
    NT_LAST = SEQ // 256
    tok = C.din("tokidx", [128, NT_LAST], U32)
    y = C.dout("y", [SEQ // 2, 1024])
    ident_sb = C.sb("ident_sb", [128, 128], F32)
    ident_bf = C.sb("ident_bf", [128, 128], BF16)
    iota_sb = C.sb("iota_sb", [128, 16], F32)
    tok_sb = C.sb("tok_sb", [128, NT_LAST], U32)
    PS8 = C.ps("PS8", [128, 8, 512], F32)
    S.emit("sp", lambda: nc.sync.dma_start(out=ident_sb[:], in_=ident[:, :]), writes=["ident"])
    S.emit("dve", lambda: nc.vector.tensor_copy(out=ident_bf[:], in_=ident_sb[:]), reads=["ident"], writes=["identbf"])
    S.emit("sp", lambda: nc.sync.dma_start(out=iota_sb[:], in_=iota[:, :]), writes=["iota"])
    S.emit("sp", lambda: nc.sync.dma_start(out=tok_sb[:], in_=tok[:, :]), writes=["tokidx"])
    xbuf = [C.dscratch(f"xs{i}", [SEQ, 1024], F32) for i in range(2)]
    o_d = C.dscratch("o_d", [SEQ, 1024], F32)
    cur = x_in
    for i in range(DEPTH):
        kind = KINDS[i % 3]
        for hh in range(2):
            with ExitStack() as stk:
                emit_att(C, kind, f"L{i}h{hh}_", cur, o_d[:, hh * 512:(hh + 1) * 512], ident_sb, ident_bf, PS8, stk)
        with ExitStack() as stk:
            if i < DEPTH - 1:
                emit_post(C, f"L{i}p_", cur, o_d, xbuf[i % 2], SEQ // 128, ident_sb, iota_sb, PS8, stk)
            else:
                emit_post(C, f"L{i}p_", cur, o_d, y, NT_LAST, ident_sb, iota_sb, PS8, stk, tokidx=tok_sb)
        cur = xbuf[i % 2]
    S.finish()
    print("fused instr:", {k: v for k, v in S.cnt.items() if v and not k[-1].isdigit()},
          "dma:", sum(v for k, v in S.cnt.items() if k[-1].isdigit()), "sems", S.nsem, "waits", S.nwait)
    C.stack.close()
    return nc


def _post_inputs(w_out, g1, b1, w_q, keys, u, v, g2, b2):
    return {
        "w_out": np.ascontiguousarray(w_out), "wq": np.ascontiguousarray(w_q),
        "keysT": np.ascontiguousarray(np.transpose(keys, (3, 0, 1, 2)).reshape(128, 2048)),
        "u": np.ascontiguousarray(u), "v": np.ascontiguousarray(v),
        "lnw": np.stack([_bc(g1), _bc(b1), _bc(g2), _bc(b2)], 0),
    }


def _swa_inputs(w_in, sinks, rel_bias):
    def per_core(hh):
        heads = list(range(hh * 8, hh * 8 + 8))
        bias, mask = _bias_tiles(rel_bias, heads, 1, 127, 2)
        return {
            "wq": np.ascontiguousarray(w_in[None, :, hh * 512:(hh + 1) * 512]),
            "wk": np.ascontiguousarray(w_in[None, :, 1024 + hh * 64:1024 + (hh + 1) * 64]),
            "wv": np.ascontiguousarray(w_in[None, :, 1152 + hh * 64:1152 + (hh + 1) * 64]),
            "sinks": _bc(sinks[hh * 8:(hh + 1) * 8]),
            "bias0": bias, "mask0": mask,
        }
    return per_core


def _dil_inputs(w_in, rel_bias):
    w = w_in.reshape(1024, 3, 3, 16, 64)

    def per_core(hh):
        heads = list(range(hh * 8, hh * 8 + 8))
        m = {}
        for j, nm in enumerate(("wq", "wk", "wv")):
            m[nm] = np.ascontiguousarray(np.transpose(w[:, :, j, hh * 8:(hh + 1) * 8, :], (1, 0, 2, 3)).reshape(3, 1024, 512))
        for g, dil in enumerate(DILS):
            bias, mask = _bias_tiles(rel_bias, heads, dil, 128, dil + 1)
            m[f"bias{g}"] = bias
            m[f"mask{g}"] = mask
        return m
    return per_core


def _mla_inputs(w_in, q_norm, w_uq, kv_norm, w_ukv):
    pos = np.arange(SEQ, dtype=np.float32)
    freq = (np.float32(10000.0) ** (-np.arange(16, dtype=np.float32) / np.float32(16))).astype(np.float32)
    ang = (pos[:, None] * freq[None, :]).astype(np.float32)
    cos, sin = np.cos(ang).astype(np.float32), np.sin(ang).astype(np.float32)
    k = np.arange(128)[:, None]
    q = np.arange(128)[None, :]
    cm = np.where(k <= q, np.float32(0), np.float32(NEGM))[None].astype(np.float32)
    wkv = w_ukv.reshape(128, 16, 2, 64)

    def per_core(hh):
        return {
            "w_in": np.ascontiguousarray(w_in), "qn": _bc(q_norm), "kvn": _bc(kv_norm),
            "w_uq": np.ascontiguousarray(w_uq[:, hh * 768:(hh + 1) * 768]),
            "w_uk": np.ascontiguousarray(wkv[:, hh * 8:(hh + 1) * 8, 0, :].reshape(128, 512)),
            "w_uv": np.ascontiguousarray(wkv[:, hh * 8:(hh + 1) * 8, 1, :].reshape(128, 512)),
            "cos": cos, "sin": sin, "mask": cm,
        }
    return per_core


def _shared_inputs(rel_bias, ln_g, ln_b, swa_w_in, swa_sinks, swa_w_out, dil_w_in, dil_w_out,
                   mla_w_in, mla_q_norm, mla_w_uq, mla_kv_norm, mla_w_ukv, mla_w_out,
                   peer_w_q, peer_keys, peer_u, peer_v):
    m = {"ident": _IDENT, "iota16": _IOTA16}
    for i in range(DEPTH):
        kind, j = i % 3, i // 3
        if kind == 0:
            pc, w_out = _swa_inputs(swa_w_in[j], swa_sinks[j], rel_bias), swa_w_out[j]
        elif kind == 1:
            pc, w_out = _dil_inputs(dil_w_in[j], rel_bias), dil_w_out[j]
        else:
            pc, w_out = _mla_inputs(mla_w_in[j], mla_q_norm[j], mla_w_uq[j], mla_kv_norm[j], mla_w_ukv[j]), mla_w_out[j]
        for hh in range(2):
            for k_, v_ in pc(hh).items():
                m[f"L{i}h{hh}_{k_}"] = v_
        for k_, v_ in _post_inputs(w_out, ln_g[i, 0], ln_b[i, 0], peer_w_q[i], peer_keys[i], peer_u[i], peer_v[i],
                                   ln_g[i, 1], ln_b[i, 1]).items():
            m[f"L{i}p_{k_}"] = v_
    return m


def kernel(x, rel_bias, ln_g, ln_b, swa_w_in, swa_sinks, swa_w_out, dil_w_in, dil_w_out,
           mla_w_in, mla_q_norm, mla_w_uq, mla_kv_norm, mla_w_ukv, mla_w_out,
           peer_w_q, peer_keys, peer_u, peer_v):
    f = lambda a: np.asarray(a, dtype=np.float32)
    x = f(x)
    shared = _shared_inputs(f(rel_bias), f(ln_g), f(ln_b), f(swa_w_in), f(swa_sinks), f(swa_w_out), f(dil_w_in), f(dil_w_out),
                            f(mla_w_in), f(mla_q_norm), f(mla_w_uq), f(mla_kv_norm), f(mla_w_ukv), f(mla_w_out),
                            f(peer_w_q), f(peer_keys), f(peer_u), f(peer_v))
    if "fused" not in _PROG:
        _PROG["fused"] = build_fused()
    nc = _PROG["fused"]
    half_tok = SEQ // 2
    in_maps = []
    for c in range(8):
        b, half = c // 2, c % 2
        m = dict(shared)
        m["x"] = np.ascontiguousarray(x[b])
        m["tokidx"] = (half * half_tok + np.arange(half_tok, dtype=np.uint32).reshape(-1, 128).T).astype(np.uint32).copy()
        in_maps.append(m)
    res = run_bass_kernel_spmd(nc, in_maps, core_ids=list(range(8)))
    out = np.empty((4, SEQ, 1024), np.float32)
    for c in range(8):
        b, half = c // 2, c % 2
        out[b, half * half_tok:(half + 1) * half_tok] = res.results[c]["y"]
    return out
```

```python
import math
from contextlib import ExitStack

import numpy as np
import concourse.bass as bass
import concourse.mybir as mybir
from concourse.bass_utils import run_bass_kernel_spmd

F32 = mybir.dt.float32
BF16 = mybir.dt.bfloat16
U32 = mybir.dt.uint32
ALU = mybir.AluOpType
AF = mybir.ActivationFunctionType
AX = mybir.AxisListType

DEPTH = 4
DN_ALPHA = (2 * DEPTH) ** 0.25
LN_EPS = 1e-5
RMS_EPS = 1e-6
NEGM = -30000.0
SEQ = 8192
NBLK = 64
DILS = (1, 4, 16)


class Sched:
    KDMA = 12

    def __init__(self, nc, stack):
        self.nc = nc
        self.stack = stack
        self.eng = {"pe": nc.tensor, "dve": nc.vector, "act": nc.scalar, "pool": nc.gpsimd}
        self.stream = {"pe": "pe", "dve": "dve", "act": "act", "pool": "pool"}
        self.inc = {"pe": 1, "dve": 1, "act": 1, "pool": 1}
        self.EPOCH = {"pe": 30000, "dve": 30000, "act": 30000, "pool": 30000}
        self.rr = {"sp": 0, "poolq": 0, "actq": 0}
        for base, e, st in (("sp", nc.sync, "sp"), ("poolq", nc.gpsimd, "pool"), ("actq", nc.scalar, "act")):
            for k in range(self.KDMA):
                nm = f"{base}{k}"
                self.eng[nm] = e
                self.stream[nm] = st
                self.inc[nm] = 16
                self.EPOCH[nm] = 1800
        self.cnt = {k: 0 for k in self.eng}
        self.sems = {k: [] for k in self.eng}
        self.waited = {}
        self.last_w = {}
        self.readers = {}
        self.nsem = 0
        self.nwait = 0

    def _sem(self, q, epoch):
        while len(self.sems[q]) <= epoch:
            s = self.stack.enter_context(self.nc.semaphore(f"s_{q}_{len(self.sems[q])}"))
            self.sems[q].append(s)
            self.nsem += 1
        return self.sems[q][epoch]

    def _wait(self, q_issuer, dep):
        dq, dn = dep
        st = self.stream[q_issuer]
        ep = (dn - 1) // self.EPOCH[dq]
        local = dn - ep * self.EPOCH[dq]
        key = (st, dq, ep)
        if self.waited.get(key, 0) >= local:
            return
        self.waited[key] = local
        self.nwait += 1
        self.eng[q_issuer].wait_ge(self._sem(dq, ep), local * self.inc[dq])

    SAME_ENG_WAITS = True

    def emit(self, q, fn, reads=(), writes=(), chain=True):
        if q in self.rr:
            k = self.rr[q] % self.KDMA
            self.rr[q] += 1
            q = f"{q}{k}"
        deps = set()
        for b in reads:
            if b in self.last_w:
                deps.add(self.last_w[b])
        for b in writes:
            if b in self.last_w:
                deps.add(self.last_w[b])
            for r in self.readers.get(b, ()):
                deps.add(r)
        if chain and q[-1].isdigit() and self.cnt[q] > 0:
            deps.add((q, self.cnt[q]))
        for d in sorted(deps):
            if d[0] == q and q == "pe":
                continue
            if d[0] == q and q in ("dve", "act", "pool") and not self.SAME_ENG_WAITS:
                continue
            self._wait(q, d)
        ins = fn()
        self.cnt[q] += 1
        n = self.cnt[q]
        ep = (n - 1) // self.EPOCH[q]
        ins.then_inc(self._sem(q, ep), self.inc[q])
        for b in writes:
            self.last_w[b] = (q, n)
            self.readers[b] = []
        for b in reads:
            lst = self.readers.setdefault(b, [])
            lst.append((q, n))
            if len(lst) > 12:
                latest = {}
                for (rq, rn) in lst:
                    latest[rq] = max(latest.get(rq, 0), rn)
                self.readers[b] = list(latest.items())
        return ins

    def barrier(self):
        for rep in ("pe", "dve", "act", "pool", "sp0"):
            for q in self.cnt:
                if self.cnt[q] > 0:
                    self._wait(rep, (q, self.cnt[q]))

    def finish(self):
        for q in self.cnt:
            if self.cnt[q] > 0:
                self._wait("sp0", (q, self.cnt[q]))


class Ctx:
    def __init__(self):
        self.nc = bass.Bass("TRN2", target_bir_lowering=False)
        self.stack = ExitStack()
        self.S = Sched(self.nc, self.stack)

    def din(self, name, shape, dt=F32):
        return self.nc.dram_tensor(name, list(shape), dt, kind="ExternalInput").ap()

    def dout(self, name, shape, dt=F32):
        return self.nc.dram_tensor(name, list(shape), dt, kind="ExternalOutput").ap()

    def dscratch(self, name, shape, dt):
        return self.nc.dram_tensor(name, list(shape), dt).ap()

    def sb(self, name, shape, dt, stack=None):
        return (stack or self.stack).enter_context(self.nc.sbuf_tensor(name, list(shape), dt))

    def ps(self, name, shape, dt, stack=None):
        return (stack or self.stack).enter_context(self.nc.psum_tensor(name, list(shape), dt))


def emit_ln(C, z, zk, out_ap, ok, g_b, b_b, st, junk):
    S, nc = C.S, C.nc
    S.emit("dve", lambda: nc.vector.tensor_scalar(out=junk[:], in0=z, scalar1=1.0, scalar2=0.0, op0=ALU.mult,
                                                   op1=ALU.add, accum_out=st[:, 0:1]),
           reads=[zk], writes=["junk", "lnst0"])
    S.emit("dve", lambda: nc.vector.scalar_tensor_tensor(out=junk[:], in0=z, scalar=1.0, in1=z, op0=ALU.mult,
                                                          op1=ALU.mult, accum_out=st[:, 1:2]),
           reads=[zk], writes=["junk", "lnst1"])
    S.emit("dve", lambda: nc.vector.tensor_scalar(out=st[:, 2:4], in0=st[:, 0:2], scalar1=1.0 / 1024, scalar2=None,
                                                   op0=ALU.mult), reads=["lnst0", "lnst1"], writes=["lnst23"])
    S.emit("dve", lambda: nc.vector.tensor_tensor(out=st[:, 4:5], in0=st[:, 2:3], in1=st[:, 2:3], op=ALU.mult),
           reads=["lnst23"], writes=["lnst4"])
    S.emit("dve", lambda: nc.vector.tensor_tensor(out=st[:, 5:6], in0=st[:, 3:4], in1=st[:, 4:5], op=ALU.subtract),
           reads=["lnst23", "lnst4"], writes=["lnst5"])
    S.emit("dve", lambda: nc.vector.tensor_scalar(out=st[:, 5:6], in0=st[:, 5:6], scalar1=LN_EPS, scalar2=None,
                                                   op0=ALU.add), reads=["lnst5"], writes=["lnst5"])
    S.emit("act", lambda: nc.scalar.activation(out=st[:, 6:7], in_=st[:, 5:6], func=AF.Sqrt),
           reads=["lnst5"], writes=["lnst6"])
    S.emit("dve", lambda: nc.vector.reciprocal(out=st[:, 7:8], in_=st[:, 6:7]), reads=["lnst6"], writes=["lnst7"])
    S.emit("dve", lambda: nc.vector.tensor_scalar(out=z, in0=z, scalar1=st[:, 2:3], scalar2=st[:, 7:8],
                                                   op0=ALU.subtract, op1=ALU.mult),
           reads=[zk, "lnst23", "lnst7"], writes=[zk])
    S.emit("dve", lambda: nc.vector.tensor_tensor(out=z, in0=z, in1=g_b, op=ALU.mult), reads=[zk, "lnw"], writes=[zk])
    S.emit("dve", lambda: nc.vector.tensor_tensor(out=out_ap, in0=z, in1=b_b, op=ALU.add), reads=[zk, "lnw"], writes=[ok])


def emit_att(C, kind, pfx, x, o, ident_sb, ident_bf, PS8, stk):
    nc, S = C.nc, C.S
    G = 3 if kind == "dil" else 1
    DH = 96 if kind == "mla" else 64
    NKV = 1 if kind == "swa" else 8
    QT_d = C.dscratch(pfx + "QT_d", [G, 8, DH, SEQ], BF16)
    KT_d = C.dscratch(pfx + "KT_d", [G, NKV, 64, SEQ], BF16)
    V_d = C.dscratch(pfx + "V_d", [G, NKV, 128, NBLK, 64], BF16)
    if kind == "mla":
        KR_d = C.dscratch(pfx + "KR_d", [32, SEQ], BF16)
        w_in = C.din(pfx + "w_in", [1024, 416])
        qn = C.din(pfx + "qn", [128, 256])
        kvn = C.din(pfx + "kvn", [128, 128])
        w_uq = C.din(pfx + "w_uq", [256, 768])
        w_uk = C.din(pfx + "w_uk", [128, 512])
        w_uv = C.din(pfx + "w_uv", [128, 512])
        cosd = C.din(pfx + "cos", [SEQ, 16])
        sind = C.din(pfx + "sin", [SEQ, 16])
        maskd = C.din(pfx + "mask", [1, 128, 128])
        ND = [1]
    else:
        wq = C.din(pfx + "wq", [G, 1024, 512])
        wk = C.din(pfx + "wk", [G, 1024, NKV * 64])
        wv = C.din(pfx + "wv", [G, 1024, NKV * 64])
        if kind == "swa":
            ND = [2]
            sinkd = C.din(pfx + "sinks", [128, 8])
        else:
            ND = [d + 1 for d in DILS]
        biasd = [C.din(pfx + f"bias{g}", [ND[g], 128, 8, 128]) for g in range(G)]
        maskd = [C.din(pfx + f"mask{g}", [ND[g], 128, 128]) for g in range(G)]

    PSA = PS8[:, 0:4, :]
    PSB = PS8[:, 4:7, :]
    Bt = []
    if kind == "mla":
        cm_bf = C.sb(pfx + "cm_bf", [128, 128], BF16, stk)
        S.emit("poolq", lambda: nc.gpsimd.dma_start(out=cm_bf[:], in_=maskd[0]), writes=["bias"])
    else:
        for g in range(G):
            Bt.append(C.sb(pfx + f"B{g}", [128, ND[g], 8, 128], BF16, stk))
        if kind == "swa":
            esink = C.sb(pfx + "esink", [128, 8], F32, stk)
            S.emit("sp", lambda: nc.sync.dma_start(out=esink[:], in_=sinkd[:, :]), writes=["esink"])
            S.emit("act", lambda: nc.scalar.activation(out=esink[:], in_=esink[:], func=AF.Exp), reads=["esink"], writes=["esink"])

    with ExitStack() as p1:
        if kind != "mla":
            btmp = C.sb(pfx + "btmp", [128, 8, 128], F32, p1)
            mtmp = C.sb(pfx + "mtmp", [128, 128], F32, p1)
            for g in range(G):
                for d in range(ND[g]):
                    S.emit("sp", lambda: nc.sync.dma_start(out=btmp[:], in_=biasd[g][d]), writes=["btmp"])
                    S.emit("sp", lambda: nc.sync.dma_start(out=mtmp[:], in_=maskd[g][d]), writes=["mtmp"])
                    S.emit("dve", lambda: nc.vector.tensor_tensor(
                        out=Bt[g][:, d, :, :], in0=btmp[:], in1=mtmp[:].unsqueeze(1).to_broadcast([128, 8, 128]), op=ALU.add),
                        reads=["btmp", "mtmp"], writes=["bias"])
            wq_bf = C.sb(pfx + "wq_bf", [128, G, 8, 512], BF16, p1)
            wk_bf = C.sb(pfx + "wk_bf", [128, G, 8, NKV * 64], BF16, p1)
            wv_bf = C.sb(pfx + "wv_bf", [128, G, 8, NKV * 64], BF16, p1)
            for g in range(G):
                for c in range(8):
                    S.emit("poolq", lambda: nc.gpsimd.dma_start(out=wq_bf[:, g, c, :], in_=wq[g, c * 128:(c + 1) * 128, :]), writes=[("w", g, c, 0)])
                    S.emit("poolq", lambda: nc.gpsimd.dma_start(out=wk_bf[:, g, c, :], in_=wk[g, c * 128:(c + 1) * 128, :]), writes=[("w", g, c, 1)])
                    S.emit("poolq", lambda: nc.gpsimd.dma_start(out=wv_bf[:, g, c, :], in_=wv[g, c * 128:(c + 1) * 128, :]), writes=[("w", g, c, 2)])
            xin = [C.sb(pfx + f"xin{i}", [128, 4, 1024], F32, p1) for i in range(2)]
            xT = [C.sb(pfx + f"xTb{i}", [128, 8, 512], BF16, p1) for i in range(2)]
            NST = 4
            stg = [C.sb(pfx + f"stg{i}", [128, 512], BF16, p1) for i in range(NST)]
            vst = [C.sb(pfx + f"vst{i}", [128, 4, NKV * 64], BF16, p1) for i in range(2)]
            sti = 0
            ppi = 0
            evi = 0

            def evac(out_ap, in_ap, scale, reads, writes):
                nonlocal evi
                evi += 1
                if evi % 2 == 0:
                    S.emit("act", lambda: nc.scalar.activation(out=out_ap, in_=in_ap, func=AF.Copy, scale=float(scale)),
                           reads=reads, writes=writes)
                else:
                    S.emit("dve", lambda: nc.vector.tensor_scalar(out=out_ap, in0=in_ap, scalar1=float(scale), scalar2=None,
                                                                   op0=ALU.mult), reads=reads, writes=writes)

            for tc in range(SEQ // 512):
                xb = xin[tc % 2]
                xk = ("xin", tc % 2)
                S.emit("sp", lambda: nc.sync.dma_start(out=xb[:], in_=x[tc * 512:(tc + 1) * 512, :].rearrange("(n p) d -> p n d", p=128)),
                       writes=[xk])
                xt = xT[tc % 2]
                xtk = ("xT", tc % 2)
                for ti in range(4):
                    pb = (tc * 4 + ti) % 2
                    pst = PSA[:, 2 * pb:2 * pb + 2, :].rearrange("p b (c t) -> p (b c) t", t=128)
                    for c in range(8):
                        S.emit("pe", lambda: nc.tensor.transpose(out=pst[:, c, :], in_=xb[:, ti, c * 128:(c + 1) * 128], identity=ident_sb[:]),
                               reads=[xk, "ident"], writes=[("A", 2 * pb + c // 4)])
                    evac(xt[:, :, ti * 128:(ti + 1) * 128], pst, 1.0, [("A", 2 * pb), ("A", 2 * pb + 1)], [(xtk, ti)])
                xt_keys = [(xtk, ti) for ti in range(4)]
                for g in range(G):
                    for (wbf, dst_d, nh, scale, wi) in ((wq_bf, QT_d, 8, 0.125, 0), (wk_bf, KT_d, NKV, 1.0, 1)):
                        for h in range(nh):
                            pp = ppi % 3
                            ppi += 1
                            for c in range(8):
                                S.emit("pe", lambda: nc.tensor.matmul(out=PSB[0:64, pp, :], lhsT=wbf[:, g, c, h * 64:(h + 1) * 64],
                                                                      rhs=xt[:, c, :], start=(c == 0), stop=(c == 7)),
                                       reads=xt_keys + [("w", g, c, wi)], writes=[("B", pp)])
                            sg = stg[sti % NST]
                            sk = ("stg", sti % NST)
                            sti += 1
                            evac(sg[0:64, :], PSB[0:64, pp, :], scale, [("B", pp)], [sk])
                            S.emit("sp", lambda: nc.sync.dma_start(out=dst_d[g, h, :, tc * 512:(tc + 1) * 512], in_=sg[0:64, :]),
                                   reads=[sk], writes=[("scr", wi, g, h, tc)])
                    vs = vst[(tc * G + g) % 2]
                    vk = ("vst", (tc * G + g) % 2)
                    for ti in range(4):
                        pp = ppi % 3
                        ppi += 1
                        for c in range(8):
                            S.emit("pe", lambda: nc.tensor.matmul(out=PSB[:, pp, 0:NKV * 64], lhsT=xt[:, c, ti * 128:(ti + 1) * 128],
                                                                  rhs=wv_bf[:, g, c, :], start=(c == 0), stop=(c == 7)),
                                   reads=xt_keys + [("w", g, c, 2)], writes=[("B", pp)])
                        evac(vs[:, ti, :], PSB[:, pp, 0:NKV * 64], 1.0, [("B", pp)], [(vk, ti)])
                    for h in range(NKV):
                        S.emit("sp", lambda: nc.sync.dma_start(out=V_d[g, h, :, tc * 4:(tc + 1) * 4, :], in_=vs[:, :, h * 64:(h + 1) * 64]),
                               reads=[(vk, ti) for ti in range(4)], writes=[("scr", 2, g, h, tc)])
        else:
            w_in_bf = C.sb(pfx + "w_in_bf", [128, 8, 416], BF16, p1)
            w_uq_bf = C.sb(pfx + "w_uq_bf", [128, 2, 768], BF16, p1)
            w_uk_bf = C.sb(pfx + "w_uk_bf", [128, 512], BF16, p1)
            w_uv_bf = C.sb(pfx + "w_uv_bf", [128, 512], BF16, p1)
            qn_b = C.sb(pfx + "qn_b", [128, 256], F32, p1)
            kvn_b = C.sb(pfx + "kvn_b", [128, 128], F32, p1)
            cos_sb = C.sb(pfx + "cos_sb", [128, NBLK, 16], F32, p1)
            sin_sb = C.sb(pfx + "sin_sb", [128, NBLK, 16], F32, p1)
            for c in range(8):
                S.emit("poolq", lambda: nc.gpsimd.dma_start(out=w_in_bf[:, c, :], in_=w_in[c * 128:(c + 1) * 128, :]), writes=[("w", c)])
            for c in range(2):
                S.emit("poolq", lambda: nc.gpsimd.dma_start(out=w_uq_bf[:, c, :], in_=w_uq[c * 128:(c + 1) * 128, :]), writes=[("wuq", c)])
            S.emit("poolq", lambda: nc.gpsimd.dma_start(out=w_uk_bf[:], in_=w_uk[:, :]), writes=["wuk"])
            S.emit("poolq", lambda: nc.gpsimd.dma_start(out=w_uv_bf[:], in_=w_uv[:, :]), writes=["wuv"])
            S.emit("sp", lambda: nc.sync.dma_start(out=qn_b[:], in_=qn[:, :]), writes=["qn"])
            S.emit("sp", lambda: nc.sync.dma_start(out=kvn_b[:], in_=kvn[:, :]), writes=["kvn"])
            S.emit("sp", lambda: nc.sync.dma_start(out=cos_sb[:], in_=cosd.rearrange("(n p) j -> p n j", p=128)), writes=["cos"])
            S.emit("sp", lambda: nc.sync.dma_start(out=sin_sb[:], in_=sind.rearrange("(n p) j -> p n j", p=128)), writes=["sin"])
            xin = [C.sb(pfx + f"xin{i}", [128, 1024], F32, p1) for i in range(2)]
            xTt = C.sb(pfx + "xTt", [128, 8, 128], BF16, p1)
            c_sb = C.sb(pfx + "c_sb", [128, 416], F32, p1)
            junk = C.sb(pfx + "junkm", [128, 256], F32, p1)
            rst = C.sb(pfx + "rst", [128, 8], F32, p1)
            cqn = C.sb(pfx + "cqn", [128, 256], F32, p1)
            ckvn = C.sb(pfx + "ckvn", [128, 128], F32, p1)
            krr = C.sb(pfx + "krr", [128, 32], F32, p1)
            rt = C.sb(pfx + "rt", [128, 4, 16], F32, p1)
            cqnT = C.sb(pfx + "cqnT", [128, 2, 128], BF16, p1)
            ckvnT = C.sb(pfx + "ckvnT", [128, 128], BF16, p1)
            krT = C.sb(pfx + "krT", [32, 128], BF16, p1)
            q_sb = C.sb(pfx + "q_sb", [128, 8, 96], F32, p1)
            q_bf = C.sb(pfx + "q_bf", [128, 8, 96], F32, p1)
            qrt = C.sb(pfx + "qrt", [128, 4, 8, 16], F32, p1)
            qTs = C.sb(pfx + "qTs", [96, 8, 128], BF16, p1)
            kTs = C.sb(pfx + "kTs", [64, 8, 128], BF16, p1)
            vs = C.sb(pfx + "vs", [128, 512], BF16, p1)
            QSC = 96 ** -0.5
            import os as _os2
            _MSTOP = int(_os2.environ.get('MLA_STOP', '99'))
            _MSUB = int(_os2.environ.get('MLA_SUB', '0'))
            for t in range(NBLK):
                xb = xin[t % 2]
                xk = ("xin", t % 2)
                S.emit("sp", lambda: nc.sync.dma_start(out=xb[:], in_=x[t * 128:(t + 1) * 128, :]), writes=[xk])
                pst = PSA[:, 0:2, :].rearrange("p b (c t) -> p (b c) t", t=128)
                for c in range(8):
                    S.emit("pe", lambda: nc.tensor.transpose(out=pst[:, c, :], in_=xb[:, c * 128:(c + 1) * 128], identity=ident_sb[:]),
                           reads=[xk, "ident"], writes=[("A", c // 4)])
                S.emit("act", lambda: nc.scalar.copy(out=xTt[:], in_=pst), reads=[("A", 0), ("A", 1)], writes=["xTt"])
                for c in range(8):
                    S.emit("pe", lambda: nc.tensor.matmul(out=PSA[:, 2, 0:416], lhsT=xTt[:, c, :], rhs=w_in_bf[:, c, :],
                                                          start=(c == 0), stop=(c == 7)),
                           reads=["xTt", ("w", c)], writes=[("A", 2)])
                S.emit("dve", lambda: nc.vector.tensor_copy(out=c_sb[:], in_=PSA[:, 2, 0:416]), reads=[("A", 2)], writes=["c"])
                if _MSTOP <= 2:
                    continue
                S.emit("dve", lambda: nc.vector.scalar_tensor_tensor(out=junk[:], in0=c_sb[:, 0:256], scalar=1.0 / 256, in1=c_sb[:, 0:256],
                                                                      op0=ALU.mult, op1=ALU.mult, accum_out=rst[:, 0:1]),
                       reads=["c"], writes=["junkm", "rst0"])
                S.emit("dve", lambda: nc.vector.scalar_tensor_tensor(out=junk[:, 0:128], in0=c_sb[:, 256:384], scalar=1.0 / 128, in1=c_sb[:, 256:384],
                                                                      op0=ALU.mult, op1=ALU.mult, accum_out=rst[:, 1:2]),
                       reads=["c"], writes=["junkm", "rst1"])
                S.emit("dve", lambda: nc.vector.tensor_scalar(out=rst[:, 2:4], in0=rst[:, 0:2], scalar1=RMS_EPS, scalar2=None, op0=ALU.add),
                       reads=["rst0", "rst1"], writes=["rst23"])
                S.emit("act", lambda: nc.scalar.activation(out=rst[:, 4:6], in_=rst[:, 2:4], func=AF.Sqrt), reads=["rst23"], writes=["rst45"])
                S.emit("dve", lambda: nc.vector.reciprocal(out=rst[:, 6:8], in_=rst[:, 4:6]), reads=["rst45"], writes=["rst67"])
                S.emit("dve", lambda: nc.vector.scalar_tensor_tensor(out=cqn[:], in0=c_sb[:, 0:256], scalar=rst[:, 6:7], in1=qn_b[:],
                                                                      op0=ALU.mult, op1=ALU.mult), reads=["c", "rst67", "qn"], writes=["cqn"])
                S.emit("dve", lambda: nc.vector.scalar_tensor_tensor(out=ckvn[:], in0=c_sb[:, 256:384], scalar=rst[:, 7:8], in1=kvn_b[:],
                                                                      op0=ALU.mult, op1=ALU.mult), reads=["c", "rst67", "kvn"], writes=["ckvn"])
                if _MSTOP <= 3:
                    continue
                k1, k2 = c_sb[:, 384:400], c_sb[:, 400:416]
                cs, sn = cos_sb[:, t, :], sin_sb[:, t, :]
                S.emit("dve", lambda: nc.vector.tensor_tensor(out=rt[:, 0, :], in0=k1, in1=cs, op=ALU.mult), reads=["c", "cos"], writes=["rt0"])
                S.emit("dve", lambda: nc.vector.tensor_tensor(out=rt[:, 1, :], in0=k2, in1=sn, op=ALU.mult), reads=["c", "sin"], writes=["rt1"])
                S.emit("dve", lambda: nc.vector.tensor_tensor(out=rt[:, 2, :], in0=k1, in1=sn, op=ALU.mult), reads=["c", "sin"], writes=["rt2"])
                S.emit("dve", lambda: nc.vector.tensor_tensor(out=rt[:, 3, :], in0=k2, in1=cs, op=ALU.mult), reads=["c", "cos"], writes=["rt3"])
                S.emit("dve", lambda: nc.vector.tensor_tensor(out=krr[:, 0:16], in0=rt[:, 0, :], in1=rt[:, 1, :], op=ALU.subtract),
                       reads=["rt0", "rt1"], writes=["krr0"])
                S.emit("dve", lambda: nc.vector.tensor_tensor(out=krr[:, 16:32], in0=rt[:, 2, :], in1=rt[:, 3, :], op=ALU.add),
                       reads=["rt2", "rt3"], writes=["krr1"])
                if _MSTOP <= 4:
                    continue
                PS3 = PSA[:, 3, :]
                for c in range(2):
                    S.emit("pe", lambda: nc.tensor.transpose(out=PS3[:, c * 128:(c + 1) * 128], in_=cqn[:, c * 128:(c + 1) * 128], identity=ident_sb[:]),
                           reads=["cqn", "ident"], writes=[("A", 3)])
                S.emit("pe", lambda: nc.tensor.transpose(out=PS3[:, 256:384], in_=ckvn[:], identity=ident_sb[:]),
                       reads=["ckvn", "ident"], writes=[("A", 3)])
                if _MSUB != 1:
                    S.emit("pe", lambda: nc.tensor.transpose(out=PS3[0:32, 384:512], in_=krr[:], identity=ident_sb[:]),
                           reads=["krr0", "krr1", "ident"], writes=[("A", 3)])
                S.emit("act", lambda: nc.scalar.copy(out=cqnT[:].rearrange("p c t -> p (c t)"), in_=PS3[:, 0:256]), reads=[("A", 3)], writes=["cqnT"])
                S.emit("act", lambda: nc.scalar.copy(out=ckvnT[:], in_=PS3[:, 256:384]), reads=[("A", 3)], writes=["ckvnT"])
                S.emit("act", lambda: nc.scalar.copy(out=krT[:], in_=PS3[0:32, 384:512]), reads=[("A", 3)], writes=["krT"])
                if _MSUB != 3:
                    S.emit("sp", lambda: nc.sync.dma_start(out=KR_d[:, t * 128:(t + 1) * 128], in_=krT[:]), reads=["krT"], writes=[("scr", "kr", t)])
                if _MSTOP <= 5:
                    continue
                for half in range(2):
                    for c in range(2):
                        S.emit("pe", lambda: nc.tensor.matmul(out=PSB[:, half, 0:384], lhsT=cqnT[:, c, :],
                                                              rhs=w_uq_bf[:, c, half * 384:(half + 1) * 384], start=(c == 0), stop=(c == 1)),
                               reads=["cqnT", ("wuq", c)], writes=[("B", half)])
                    S.emit("act", lambda: nc.scalar.activation(out=q_sb[:, half * 4:(half + 1) * 4, :].rearrange("p h d -> p (h d)"),
                                                               in_=PSB[:, half, 0:384], func=AF.Copy, scale=QSC),
                           reads=[("B", half)], writes=[("q_sb", half)])
                qk = [("q_sb", 0), ("q_sb", 1)]
                q1, q2 = q_sb[:, :, 64:80], q_sb[:, :, 80:96]
                csb = cos_sb[:, t, :].unsqueeze(1).to_broadcast([128, 8, 16])
                snb = sin_sb[:, t, :].unsqueeze(1).to_broadcast([128, 8, 16])
                S.emit("dve", lambda: nc.vector.tensor_tensor(out=qrt[:, 0, :, :], in0=q1, in1=csb, op=ALU.mult), reads=qk + ["cos"], writes=["qrt0"])
                S.emit("dve", lambda: nc.vector.tensor_tensor(out=qrt[:, 1, :, :], in0=q2, in1=snb, op=ALU.mult), reads=qk + ["sin"], writes=["qrt1"])
                S.emit("dve", lambda: nc.vector.tensor_tensor(out=qrt[:, 2, :, :], in0=q1, in1=snb, op=ALU.mult), reads=qk + ["sin"], writes=["qrt2"])
                S.emit("dve", lambda: nc.vector.tensor_tensor(out=qrt[:, 3, :, :], in0=q2, in1=csb, op=ALU.mult), reads=qk + ["cos"], writes=["qrt3"])
                S.emit("dve", lambda: nc.vector.tensor_copy(out=q_bf[:, :, 0:64], in_=q_sb[:, :, 0:64]), reads=qk, writes=["q_bf0"])
                S.emit("dve", lambda: nc.vector.tensor_tensor(out=q_bf[:, :, 64:80], in0=qrt[:, 0, :, :], in1=qrt[:, 1, :, :], op=ALU.subtract),
                       reads=["qrt0", "qrt1"], writes=["q_bf1"])
                S.emit("dve", lambda: nc.vector.tensor_tensor(out=q_bf[:, :, 80:96], in0=qrt[:, 2, :, :], in1=qrt[:, 3, :, :], op=ALU.add),
                       reads=["qrt2", "qrt3"], writes=["q_bf2"])
                if _MSTOP <= 6:
                    continue
                PQT = PSA[:, 0:2, :].rearrange("p b (h t) -> p (b h) t", t=128)
                for h in range(8):
                    S.emit("pe", lambda: nc.tensor.transpose(out=PQT[0:96, h, :], in_=q_bf[:, h, :], identity=ident_sb[:]),
                           reads=["q_bf0", "q_bf1", "q_bf2", "ident"], writes=[("A", h // 4)])
                S.emit("act", lambda: nc.scalar.copy(out=qTs[:], in_=PQT[0:96, :, :]), reads=[("A", 0), ("A", 1)], writes=["qTs"])
                S.emit("sp", lambda: nc.sync.dma_start(out=QT_d[0, :, :, t * 128:(t + 1) * 128].rearrange("h d t -> d h t"), in_=qTs[:]),
                       reads=["qTs"], writes=[("scr", "q", t)])
                if _MSTOP <= 7:
                    continue
                for h in range(8):
                    dst = PSA[0:64, 3, (h % 4) * 128:(h % 4 + 1) * 128] if h < 4 else PSB[0:64, 2, (h % 4) * 128:(h % 4 + 1) * 128]
                    S.emit("pe", lambda: nc.tensor.matmul(out=dst, lhsT=w_uk_bf[:, h * 64:(h + 1) * 64], rhs=ckvnT[:], start=True, stop=True),
                           reads=["wuk", "ckvnT"], writes=[("A", 3) if h < 4 else ("B", 2)])
                S.emit("dve", lambda: nc.vector.tensor_copy(out=kTs[:, 0:4, :].rearrange("p h t -> p (h t)"), in_=PSA[0:64, 3, :]),
                       reads=[("A", 3)], writes=["kTs0"])
                S.emit("act", lambda: nc.scalar.copy(out=kTs[:, 4:8, :].rearrange("p h t -> p (h t)"), in_=PSB[0:64, 2, :]),
                       reads=[("B", 2)], writes=["kTs1"])
                S.emit("sp", lambda: nc.sync.dma_start(out=KT_d[0, :, :, t * 128:(t + 1) * 128].rearrange("h d t -> d h t"), in_=kTs[:]),
                       reads=["kTs0", "kTs1"], writes=[("scr", "k", t)])
                S.emit("pe", lambda: nc.tensor.matmul(out=PSA[:, 2, :], lhsT=ckvnT[:], rhs=w_uv_bf[:], start=True, stop=True),
                       reads=["wuv", "ckvnT"], writes=[("A", 2)])
                S.emit("dve", lambda: nc.vector.tensor_copy(out=vs[:], in_=PSA[:, 2, :]), reads=[("A", 2)], writes=["vs"])
                S.emit("sp", lambda: nc.sync.dma_start(out=V_d[0, :, :, t, :].rearrange("h p d -> p h d"),
                                                       in_=vs[:].rearrange("p (h d) -> p h d", d=64)),
                       reads=["vs"], writes=[("scr", "v", t)])
        S.barrier()

    QT_u = C.sb(pfx + "QT_u", [DH, SEQ], BF16, stk)
    KT_u = C.sb(pfx + "KT_u", [DH, SEQ], BF16, stk)
    V_u = C.sb(pfx + "V_u", [128, NBLK, 65], BF16, stk)
    Oacc = C.sb(pfx + "Oacc", [128, NBLK, 65], F32, stk)
    o_st = C.sb(pfx + "o_st", [128, NBLK, 64], F32, stk)
    den = C.sb(pfx + "den", [128, NBLK], F32, stk)
    NPT = 4
    PT = [C.sb(pfx + f"PT{i}", [128, 4, 128], BF16, stk) for i in range(NPT)]
    S.emit("dve", lambda: nc.vector.memset(V_u[:, :, 64:65], 1.0), writes=["V_ones"])
    SPv = [PSA[:, i, :].rearrange("p (a q) -> p a q", q=128) for i in range(3)]
    OPv = [PSB[:, i, :] for i in range(3)]

    grp_ctr = 0
    blk_ctr = 0
    import os as _os
    for hl in range(int(_os.environ.get('ATT_HEADS', '8'))):
        for g in range(G):
            kvh = 0 if kind == "swa" else hl
            S.emit("sp", lambda: nc.sync.dma_start(out=QT_u[:], in_=QT_d[g, hl]), writes=["QT_u"])
            S.emit("sp", lambda: nc.sync.dma_start(out=KT_u[0:64, :], in_=KT_d[g, kvh]), writes=["KT_u"])
            if kind == "mla":
                S.emit("sp", lambda: nc.sync.dma_start(out=KT_u[64:96, :], in_=KR_d[:, :]), writes=["KT_u2"])
            S.emit("sp", lambda: nc.sync.dma_start(out=V_u[:, :, 0:64], in_=V_d[g, kvh]), writes=["V_u"])
            kt_keys = ["KT_u", "KT_u2"] if kind == "mla" else ["KT_u"]
            dil = DILS[g] if kind == "dil" else 1
            nd = ND[g]
            work = []
            for n in range(NBLK):
                if kind == "mla":
                    kbs = [(kb, n - kb) for kb in range(0, n + 1)]
                else:
                    kbs = [(n - d, d) for d in range(nd - 1, -1, -1) if n - d >= 0]
                groups = [kbs[i:i + 4] for i in range(0, len(kbs), 4)]
                for gi, grp in enumerate(groups):
                    work.append((n, gi, len(groups), grp))

            def emit_qk(item, sp_i):
                n, gi, ng, grp = item
                for i, (kb, d) in enumerate(grp):
                    if kind == "mla":
                        has_b = (d == 0)
                        bt = cm_bf[:, :] if has_b else None
                    else:
                        has_b = True
                        bt = Bt[g][:, d, hl, :]
                    S.emit("pe", lambda: nc.tensor.matmul(out=SPv[sp_i][:, i, :], lhsT=KT_u[:, kb * 128:(kb + 1) * 128],
                                                          rhs=QT_u[:, n * 128:(n + 1) * 128], start=True, stop=(not has_b)),
                           reads=["QT_u"] + kt_keys, writes=[("SP", sp_i)])
                    if has_b:
                        S.emit("pe", lambda: nc.tensor.matmul(out=SPv[sp_i][:, i, :], lhsT=ident_bf[:], rhs=bt, start=False, stop=True),
                               reads=["identbf", "bias"], writes=[("SP", sp_i)])

            def emit_pv(item, sp_i, pt_i):
                n, gi, ng, grp = item
                L = len(grp)
                S.emit("act", lambda: nc.scalar.activation(out=PT[pt_i][:, 0:L, :], in_=SPv[sp_i][:, 0:L, :], func=AF.Exp),
                       reads=[("SP", sp_i)], writes=[("PT", pt_i)])
                slot = n % 3
                for i, (kb, d) in enumerate(grp):
                    S.emit("pe", lambda: nc.tensor.matmul(out=OPv[slot][:, 0:65], lhsT=PT[pt_i][:, i, :], rhs=V_u[:, kb, :],
                                                          start=(gi == 0 and i == 0), stop=(gi == ng - 1 and i == L - 1)),
                           reads=[("PT", pt_i), "V_u", "V_ones"], writes=[("OP", slot)])
                if gi == ng - 1:
                    if g == 0:
                        S.emit("dve", lambda: nc.vector.tensor_copy(out=Oacc[:, n, :], in_=OPv[slot][:, 0:65]),
                               reads=[("OP", slot)], writes=[("Oacc", n)])
                    else:
                        S.emit("dve", lambda: nc.vector.tensor_tensor(out=Oacc[:, n, :], in0=Oacc[:, n, :], in1=OPv[slot][:, 0:65], op=ALU.add),
                               reads=[("OP", slot), ("Oacc", n)], writes=[("Oacc", n)])

            LAG = 2
            pend = []
            for item in work:
                sp_i = grp_ctr % 3
                pt_i = grp_ctr % NPT
                grp_ctr += 1
                emit_qk(item, sp_i)
                pend.append((item, sp_i, pt_i))
                if len(pend) > LAG:
                    emit_pv(*pend.pop(0))
            while pend:
                emit_pv(*pend.pop(0))
            if g == G - 1:
                ok_keys = [("Oacc", n) for n in range(NBLK)]
                if kind == "swa":
                    S.emit("dve", lambda: nc.vector.tensor_scalar(out=den[:], in0=Oacc[:, :, 64], scalar1=esink[:, hl:hl + 1], scalar2=None,
                                                                   op0=ALU.add), reads=ok_keys + ["esink"], writes=["den"])
                    S.emit("dve", lambda: nc.vector.reciprocal(out=den[:], in_=den[:]), reads=["den"], writes=["den"])
                else:
                    S.emit("dve", lambda: nc.vector.reciprocal(out=den[:], in_=Oacc[:, :, 64]), reads=ok_keys, writes=["den"])
                S.emit("dve", lambda: nc.vector.tensor_tensor(out=o_st[:], in0=Oacc[:, :, 0:64],
                                                              in1=den[:].unsqueeze(2).to_broadcast([128, NBLK, 64]), op=ALU.mult),
                       reads=ok_keys + ["den"], writes=["o_st"])
                S.emit("sp", lambda: nc.sync.dma_start(out=o[:, hl * 64:(hl + 1) * 64].rearrange("(n p) d -> p n d", p=128), in_=o_st[:]),
                       reads=["o_st"], writes=[("o", hl)])
    S.barrier()


def emit_post(C, pfx, x, oin, y, NT, ident_sb, iota_sb, PS8, stk, tokidx=None):
    nc, S = C.nc, C.S
    w_out = C.din(pfx + "w_out", [1024, 1024])
    wq = C.din(pfx + "wq", [1024, 2048])
    keysT = C.din(pfx + "keysT", [128, 2048])
    u = C.din(pfx + "u", [16384, 1024])
    v = C.din(pfx + "v", [16384, 1024])
    lnw = C.din(pfx + "lnw", [4, 128, 1024])

    def sb(name, shape, dt):
        return C.sb(pfx + name, shape, dt, stk)

    wq_bf = sb("wq_bf", [128, 8, 2048], BF16)
    wo_bf = sb("wo_bf", [128, 8, 1024], BF16)
    keysT_bf = sb("keysT_bf", [128, 16, 128], BF16)
    ln_sb = sb("ln_sb", [128, 4, 1024], F32)
    xa = [sb(f"xa{i}", [128, 1024], F32) for i in range(2)]
    oa = [sb(f"oa{i}", [128, 1024], F32) for i in range(2)]
    x1 = sb("x1", [128, 1024], F32)
    oT_bf = sb("oT_bf", [128, 8, 128], BF16)
    xT_bf = sb("xT_bf", [128, 8, 128], BF16)
    qT_bf = sb("qT_bf", [128, 16, 128], BF16)
    s_sb = sb("s_sb", [128, 16, 128], F32)
    s_wk = sb("s_wk", [128, 16, 128], F32)
    oh = s_sb[:].rearrange("p a b -> p (a b)").rearrange("p (h k a) -> p h k a", h=8, k=16)
    stop = sb("stop", [128, 8, 2, 16], F32)
    itop = sb("itop", [128, 8, 2, 16], U32)
    itopf = sb("itopf", [128, 8, 2, 16], F32)
    cand = sb("cand", [128, 8, 16, 16], F32)
    best = sb("best", [128, 8, 16], F32)
    sel = sb("sel", [128, 8, 16], U32)
    selA = sb("selA", [128, 8, 16], U32)
    selB = sb("selB", [128, 8, 16], U32)
    selAf = sb("selAf", [128, 8, 16], F32)
    selBf = sb("selBf", [128, 8, 16], F32)
    i1sel = sb("i1sel", [128, 8, 16], F32)
    i2sel = sb("i2sel", [128, 8, 16], F32)
    idxf = sb("idxf", [128, 128], F32)
    idx = sb("idx", [128, 128], U32)
    gd = sb("gd", [128, 8, 16], F32)
    gz = sb("gz", [128, 8], F32)
    gate = sb("gate", [128, 128], F32)
    hacc = sb("hacc", [128, 128], F32)
    gh = sb("gh", [128, 128], F32)
    NB = 8
    gbuf = [sb(f"gbuf{i}", [128, 1024], BF16) for i in range(NB)]
    ub = C.dscratch(pfx + "u_bf", [16384, 1024], BF16)
    vb = C.dscratch(pfx + "v_bf", [16384, 1024], BF16)
    cst = [sb(f"cst{i}", [128, 8, 1024], BF16) for i in range(2)]
    ci = 0
    for (src_t, dst_t, nm) in ((u, ub, "ubf"), (v, vb, "vbf")):
        for blk in range(16):
            cb = cst[ci % 2]
            ck = ("cst", ci % 2)
            ci += 1
            S.emit("poolq", lambda: nc.gpsimd.dma_start(out=cb[:], in_=src_t[blk * 1024:(blk + 1) * 1024, :].rearrange("(p r) d -> p r d", p=128)),
                   writes=[ck])
            S.emit("sp", lambda: nc.sync.dma_start(out=dst_t[blk * 1024:(blk + 1) * 1024, :].rearrange("(p r) d -> p r d", p=128), in_=cb[:]),
                   reads=[ck], writes=[(nm, blk)])
    ubf_keys = [("ubf", blk) for blk in range(16)]
    vbf_keys = [("vbf", blk) for blk in range(16)]
    yacc = sb("yacc", [128, 1024], F32)
    junk = sb("junk", [128, 1024], F32)
    lnst = sb("lnst", [128, 8], F32)
    zt = sb("zt", [128, 1024], F32)
    ot = [sb(f"ot{i}", [128, 1024], F32) for i in range(2)]
    psA = PS8[:, 0:4, :].rearrange("p b (c t) -> p (b c) t", t=128)
    psB = PS8[:, 4:6, :].rearrange("p b (c t) -> p (b c) t", t=128)
    psY = PS8[:, 6:8, :]

    for c in range(8):
        S.emit("poolq", lambda: nc.gpsimd.dma_start(out=wq_bf[:, c, :], in_=wq[c * 128:(c + 1) * 128, :]), writes=[("wq", c)])
        S.emit("poolq", lambda: nc.gpsimd.dma_start(out=wo_bf[:, c, :], in_=w_out[c * 128:(c + 1) * 128, :]), writes=[("wo", c)])
    S.emit("poolq", lambda: nc.gpsimd.dma_start(out=keysT_bf[:].rearrange("p a b -> p (a b)"), in_=keysT[:, :]), writes=["keysT"])
    for i in range(4):
        S.emit("sp", lambda: nc.sync.dma_start(out=ln_sb[:, i, :], in_=lnw[i]), writes=["lnw"] if i == 3 else [("lnw", i)])
    lnw_all = [("lnw", 0), ("lnw", 1), ("lnw", 2), "lnw"]

    def prefetch(t):
        if tokidx is None:
            S.emit("sp", lambda: nc.sync.dma_start(out=xa[t % 2][:], in_=x[t * 128:(t + 1) * 128, :]), writes=[("xa", t % 2)])
            S.emit("sp", lambda: nc.sync.dma_start(out=oa[t % 2][:], in_=oin[t * 128:(t + 1) * 128, :]), writes=[("oa", t % 2)])
        else:
            S.emit("poolq", lambda: nc.gpsimd.indirect_dma_start(
                out=xa[t % 2][:], out_offset=None, in_=x, in_offset=bass.IndirectOffsetOnAxis(ap=tokidx[:, t:t + 1], axis=0)),
                reads=["tokidx"], writes=[("xa", t % 2)])
            S.emit("poolq", lambda: nc.gpsimd.indirect_dma_start(
                out=oa[t % 2][:], out_offset=None, in_=oin, in_offset=bass.IndirectOffsetOnAxis(ap=tokidx[:, t:t + 1], axis=0)),
                reads=["tokidx"], writes=[("oa", t % 2)])

    prefetch(0)
    gi = 0
    for t in range(NT):
        if t + 1 < NT:
            prefetch(t + 1)
        xb, ob = xa[t % 2], oa[t % 2]
        xk, okk = ("xa", t % 2), ("oa", t % 2)
        for c in range(8):
            S.emit("pe", lambda: nc.tensor.transpose(out=psB[:, c, :], in_=ob[:, c * 128:(c + 1) * 128], identity=ident_sb[:]),
                   reads=[okk, "ident"], writes=[("psB", c)])
        S.emit("act", lambda: nc.scalar.copy(out=oT_bf[:], in_=psB), reads=[("psB", c) for c in range(8)], writes=["oT"])
        for half in range(2):
            for c in range(8):
                S.emit("pe", lambda: nc.tensor.matmul(out=psY[:, half, :], lhsT=oT_bf[:, c, :], rhs=wo_bf[:, c, half * 512:(half + 1) * 512],
                                                      start=(c == 0), stop=(c == 7)),
                       reads=["oT", ("wo", c)], writes=[("psY", half)])
        S.emit("dve", lambda: nc.vector.scalar_tensor_tensor(out=zt[:], in0=xb[:], scalar=DN_ALPHA, in1=psY.rearrange("p a b -> p (a b)"),
                                                              op0=ALU.mult, op1=ALU.add),
               reads=[xk, ("psY", 0), ("psY", 1)], writes=["zt"])
        if t == 0:
            S.emit("dve", lambda: nc.vector.tensor_copy(out=lnst[:, 0:1], in_=ln_sb[:, 0, 0:1]), reads=lnw_all, writes=["lnst0"])
        emit_ln(C, zt[:], "zt", x1[:], "x1", ln_sb[:, 0, :], ln_sb[:, 1, :], lnst, junk)
        for c in range(8):
            S.emit("pe", lambda: nc.tensor.transpose(out=psB[:, c, :], in_=x1[:, c * 128:(c + 1) * 128], identity=ident_sb[:]),
                   reads=["x1", "ident"], writes=[("psB", c)])
        S.emit("act", lambda: nc.scalar.copy(out=xT_bf[:], in_=psB), reads=[("psB", c) for c in range(8)], writes=["xT"])
        for blk in range(16):
            for c in range(8):
                S.emit("pe", lambda: nc.tensor.matmul(out=psA[:, blk, :], lhsT=wq_bf[:, c, blk * 128:(blk + 1) * 128], rhs=xT_bf[:, c, :],
                                                      start=(c == 0), stop=(c == 7)),
                       reads=[("wq", c), "xT"], writes=[("psA", blk)])
        S.emit("act", lambda: nc.scalar.copy(out=qT_bf[:], in_=psA), reads=[("psA", b) for b in range(16)], writes=["qT"])
        for hp in range(16):
            S.emit("pe", lambda: nc.tensor.matmul(out=psA[:, hp, :], lhsT=qT_bf[:, hp, :], rhs=keysT_bf[:, hp, :], start=True, stop=True),
                   reads=["qT", "keysT"], writes=[("psA", hp)])
        S.emit("dve", lambda: nc.vector.tensor_copy(out=s_sb[:], in_=psA), reads=[("psA", b) for b in range(16)], writes=["s"])
        for hp in range(16):
            h_, p_ = hp // 2, hp % 2
            sv = s_sb[:, hp, :]
            sw = s_wk[:, hp, :]
            S.emit("dve", lambda: nc.vector.max(out=stop[:, h_, p_, 0:8], in_=sv), reads=["s"], writes=[("stop", hp, 0)])
            S.emit("dve", lambda: nc.vector.max_index(out=itop[:, h_, p_, 0:8], in_max=stop[:, h_, p_, 0:8], in_values=sv),
                   reads=["s", ("stop", hp, 0)], writes=[("itop", hp, 0)])
            S.emit("dve", lambda: nc.vector.match_replace(out=sw, in_to_replace=stop[:, h_, p_, 0:8], in_values=sv, imm_value=-1e30),
                   reads=["s", ("stop", hp, 0)], writes=[("swk", hp)])
            S.emit("dve", lambda: nc.vector.max(out=stop[:, h_, p_, 8:16], in_=sw), reads=[("swk", hp)], writes=[("stop", hp, 1)])
            S.emit("dve", lambda: nc.vector.max_index(out=itop[:, h_, p_, 8:16], in_max=stop[:, h_, p_, 8:16], in_values=sw),
                   reads=[("swk", hp), ("stop", hp, 1)], writes=[("itop", hp, 1)])
        stop_keys = [("stop", hp, i) for hp in range(16) for i in range(2)]
        itop_keys = [("itop", hp, i) for hp in range(16) for i in range(2)]
        S.emit("dve", lambda: nc.vector.tensor_tensor(
            out=cand[:], in0=stop[:, :, 0, :].unsqueeze(3).to_broadcast([128, 8, 16, 16]),
            in1=stop[:, :, 1, :].unsqueeze(2).to_broadcast([128, 8, 16, 16]), op=ALU.add), reads=stop_keys, writes=["cand"])
        for h_ in range(8):
            cv = cand[:, h_, :, :].rearrange("p a b -> p (a b)")
            cw = s_wk[:, 2 * h_:2 * h_ + 2, :].rearrange("p a b -> p (a b)")
            cwk = [("swk", 2 * h_), ("swk", 2 * h_ + 1)]
            S.emit("dve", lambda: nc.vector.max(out=best[:, h_, 0:8], in_=cv), reads=["cand"], writes=[("best", h_, 0)])
            S.emit("dve", lambda: nc.vector.max_index(out=sel[:, h_, 0:8], in_max=best[:, h_, 0:8], in_values=cv),
                   reads=["cand", ("best", h_, 0)], writes=[("sel", h_, 0)])
            S.emit("dve", lambda: nc.vector.match_replace(out=cw, in_to_replace=best[:, h_, 0:8], in_values=cv, imm_value=-1e30),
                   reads=["cand", ("best", h_, 0)], writes=cwk)
            S.emit("dve", lambda: nc.vector.max(out=best[:, h_, 8:16], in_=cw), reads=cwk, writes=[("best", h_, 1)])
            S.emit("dve", lambda: nc.vector.max_index(out=sel[:, h_, 8:16], in_max=best[:, h_, 8:16], in_values=cw),
                   reads=cwk + [("best", h_, 1)], writes=[("sel", h_, 1)])
        best_keys = [("best", h_, i) for h_ in range(8) for i in range(2)]
        sel_keys = [("sel", h_, i) for h_ in range(8) for i in range(2)]
        S.emit("dve", lambda: nc.vector.tensor_single_scalar(out=selA[:], in_=sel[:], scalar=4, op=ALU.logical_shift_right),
               reads=sel_keys, writes=["selA"])
        S.emit("dve", lambda: nc.vector.tensor_single_scalar(out=selB[:], in_=sel[:], scalar=15, op=ALU.bitwise_and),
               reads=sel_keys, writes=["selB"])
        S.emit("dve", lambda: nc.vector.tensor_copy(out=selAf[:], in_=selA[:]), reads=["selA"], writes=["selAf"])
        S.emit("dve", lambda: nc.vector.tensor_copy(out=selBf[:], in_=selB[:]), reads=["selB"], writes=["selBf"])
        S.emit("dve", lambda: nc.vector.tensor_copy(out=itopf[:], in_=itop[:]), reads=itop_keys, writes=["itopf"])
        iota_b = iota_sb[:].unsqueeze(1).unsqueeze(1).to_broadcast([128, 8, 16, 16])
        for (self_, pidx, dst, dkey, skey) in ((selAf, 0, i1sel, "i1sel", "selAf"), (selBf, 1, i2sel, "i2sel", "selBf")):
            S.emit("dve", lambda: nc.vector.tensor_tensor(out=oh, in0=iota_b, in1=self_[:].unsqueeze(3).to_broadcast([128, 8, 16, 16]),
                                                          op=ALU.is_equal), reads=["iota", skey], writes=["s"])
            S.emit("dve", lambda: nc.vector.tensor_tensor(out=oh, in0=oh, in1=itopf[:, :, pidx, :].unsqueeze(2).to_broadcast([128, 8, 16, 16]),
                                                          op=ALU.mult), reads=["s", "itopf"], writes=["s"])
            S.emit("dve", lambda: nc.vector.tensor_reduce(out=dst[:], in_=oh, axis=AX.X, op=ALU.add), reads=["s"], writes=[dkey])
        S.emit("dve", lambda: nc.vector.scalar_tensor_tensor(
            out=idxf[:], in0=i1sel[:].rearrange("p a b -> p (a b)"), scalar=128.0, in1=i2sel[:].rearrange("p a b -> p (a b)"),
            op0=ALU.mult, op1=ALU.add), reads=["i1sel", "i2sel"], writes=["idxf"])
        S.emit("dve", lambda: nc.vector.tensor_copy(out=idx[:], in_=idxf[:]), reads=["idxf"], writes=["idx"])
        S.emit("dve", lambda: nc.vector.tensor_tensor(out=gd[:], in0=best[:], in1=best[:, :, 0:1].to_broadcast([128, 8, 16]),
                                                      op=ALU.subtract), reads=best_keys, writes=["gd"])
        S.emit("act", lambda: nc.scalar.activation(out=gd[:], in_=gd[:], func=AF.Exp), reads=["gd"], writes=["gd"])
        S.emit("dve", lambda: nc.vector.tensor_reduce(out=gz[:], in_=gd[:], axis=AX.X, op=ALU.add), reads=["gd"], writes=["gz"])
        S.emit("dve", lambda: nc.vector.reciprocal(out=gz[:], in_=gz[:]), reads=["gz"], writes=["gz"])
        S.emit("dve", lambda: nc.vector.tensor_tensor(out=gate[:].rearrange("p (a b) -> p a b", b=16), in0=gd[:],
                                                      in1=gz[:].unsqueeze(2).to_broadcast([128, 8, 16]), op=ALU.mult),
               reads=["gd", "gz"], writes=["gate"])
        for j in range(128):
            b = gi % NB
            gi += 1
            S.emit("poolq", lambda: nc.gpsimd.indirect_dma_start(
                out=gbuf[b][:], out_offset=None, in_=ub[:, :], in_offset=bass.IndirectOffsetOnAxis(ap=idx[:, j:j + 1], axis=0)),
                reads=["idx"] + ubf_keys, writes=[("gbuf", b)])
            S.emit("dve", lambda: nc.vector.scalar_tensor_tensor(out=junk[:], in0=gbuf[b][:], scalar=1.0, in1=x1[:], op0=ALU.mult,
                                                                  op1=ALU.mult, accum_out=hacc[:, j:j + 1]),
                   reads=[("gbuf", b), "x1"], writes=["junk", ("hacc", j)])
        S.emit("act", lambda: nc.scalar.activation(out=gh[:], in_=hacc[:], func=AF.Gelu), reads=[("hacc", j) for j in range(128)], writes=["gh"])
        S.emit("dve", lambda: nc.vector.tensor_tensor(out=gh[:], in0=gh[:], in1=gate[:], op=ALU.mult), reads=["gh", "gate"], writes=["gh"])
        for j in range(128):
            b = gi % NB
            gi += 1
            S.emit("poolq", lambda: nc.gpsimd.indirect_dma_start(
                out=gbuf[b][:], out_offset=None, in_=vb[:, :], in_offset=bass.IndirectOffsetOnAxis(ap=idx[:, j:j + 1], axis=0)),
                reads=["idx"] + vbf_keys, writes=[("gbuf", b)])
            if j == 0:
                S.emit("dve", lambda: nc.vector.tensor_scalar(out=yacc[:], in0=gbuf[b][:], scalar1=gh[:, 0:1], scalar2=None, op0=ALU.mult),
                       reads=[("gbuf", b), "gh"], writes=["yacc"])
            else:
                S.emit("dve", lambda: nc.vector.scalar_tensor_tensor(out=yacc[:], in0=gbuf[b][:], scalar=gh[:, j:j + 1], in1=yacc[:],
                                                                      op0=ALU.mult, op1=ALU.add),
                       reads=[("gbuf", b), "gh", "yacc"], writes=["yacc"])
        S.emit("dve", lambda: nc.vector.scalar_tensor_tensor(out=zt[:], in0=x1[:], scalar=DN_ALPHA, in1=yacc[:], op0=ALU.mult, op1=ALU.add),
               reads=["x1", "yacc"], writes=["zt"])
        obuf = ot[t % 2]
        obk = ("ot", t % 2)
        emit_ln(C, zt[:], "zt", obuf[:], obk, ln_sb[:, 2, :], ln_sb[:, 3, :], lnst, junk)
        S.emit("sp", lambda: nc.sync.dma_start(out=y[t * 128:(t + 1) * 128, :], in_=obuf[:]), reads=[obk], writes=[("y", t)])
    S.barrier()


def _rel_bucket(dist):
    n = np.maximum(dist, 0)
    nf = np.maximum(n, 1).astype(np.float32)
    large = 16 + (np.log(nf / np.float32(16)) / np.float32(math.log(2048 / 16)) * np.float32(16)).astype(np.int32)
    large = np.minimum(large, 31)
    return np.where(n < 16, n, large)


def _bias_tiles(rel_bias, heads, dil, max_m, nd):
    k = np.arange(128)[:, None]
    q = np.arange(128)[None, :]
    bias = np.zeros((nd, 128, len(heads), 128), np.float32)
    mask = np.zeros((nd, 128, 128), np.float32)
    for d in range(nd):
        dist = q + d * 128 - k
        valid = (dist >= 0) & (dist % dil == 0) & (dist // dil <= max_m)
        bk = _rel_bucket(np.where(valid, dist, 0))
        bias[d] = np.transpose(rel_bias[bk][:, :, heads], (0, 2, 1))
        mask[d] = np.where(valid, np.float32(0), np.float32(NEGM))
    return bias, mask


_IDENT = np.eye(128, dtype=np.float32)
_IOTA16 = np.ascontiguousarray(np.broadcast_to(np.arange(16, dtype=np.float32)[None, :], (128, 16)))
_PROG = {}
KINDS = ("swa", "dil", "mla")


def _bc(vec, n=128):
    return np.ascontiguousarray(np.broadcast_to(np.asarray(vec, np.float32)[None, :], (n, vec.shape[0])))


def build_fused():
    C = Ctx()
    nc, S = C.nc, C.S
    x_in = C.din("x", [SEQ, 1024])
    ident = C.din("ident", [128, 128])
    iota = C.din("iota16", [128, 16])
    NT_LAST = SEQ // 256
    tok = C.din("tokidx", [128, NT_LAST], U32)
    y = C.dout("y", [SEQ // 2, 1024])
    ident_sb = C.sb("ident_sb", [128, 128], F32)
    ident_bf = C.sb("ident_bf", [128, 128], BF16)
    iota_sb = C.sb("iota_sb", [128, 16], F32)
    tok_sb = C.sb("tok_sb", [128, NT_LAST], U32)
    PS8 = C.ps("PS8", [128, 8, 512], F32)
    S.emit("sp", lambda: nc.sync.dma_start(out=ident_sb[:], in_=ident[:, :]), writes=["ident"])
    S.emit("dve", lambda: nc.vector.tensor_copy(out=ident_bf[:], in_=ident_sb[:]), reads=["ident"], writes=["identbf"])
    S.emit("sp", lambda: nc.sync.dma_start(out=iota_sb[:], in_=iota[:, :]), writes=["iota"])
    S.emit("sp", lambda: nc.sync.dma_start(out=tok_sb[:], in_=tok[:, :]), writes=["tokidx"])
    xbuf = [C.dscratch(f"xs{i}", [SEQ, 1024], F32) for i in range(2)]
    o_d = C.dscratch("o_d", [SEQ, 1024], F32)
    cur = x_in
    for i in range(DEPTH):
        kind = KINDS[i % 3]
        for hh in range(2):
            with ExitStack() as stk:
                emit_att(C, kind, f"L{i}h{hh}_", cur, o_d[:, hh * 512:(hh + 1) * 512], ident_sb, ident_bf, PS8, stk)
        with ExitStack() as stk:
            if i < DEPTH - 1:
                emit_post(C, f"L{i}p_", cur, o_d, xbuf[i % 2], SEQ // 128, ident_sb, iota_sb, PS8, stk)
            else:
                emit_post(C, f"L{i}p_", cur, o_d, y, NT_LAST, ident_sb, iota_sb, PS8, stk, tokidx=tok_sb)
        cur = xbuf[i % 2]
    S.finish()
    print("fused instr:", {k: v for k, v in S.cnt.items() if v and not k[-1].isdigit()},
          "dma:", sum(v for k, v in S.cnt.items() if k[-1].isdigit()), "sems", S.nsem, "waits", S.nwait)
    C.stack.close()
    return nc


def _post_inputs(w_out, g1, b1, w_q, keys, u, v, g2, b2):
    return {
        "w_out": np.ascontiguousarray(w_out), "wq": np.ascontiguousarray(w_q),
        "keysT": np.ascontiguousarray(np.transpose(keys, (3, 0, 1, 2)).reshape(128, 2048)),
        "u": np.ascontiguousarray(u), "v": np.ascontiguousarray(v),
        "lnw": np.stack([_bc(g1), _bc(b1), _bc(g2), _bc(b2)], 0),
    }


def _swa_inputs(w_in, sinks, rel_bias):
    def per_core(hh):
        heads = list(range(hh * 8, hh * 8 + 8))
        bias, mask = _bias_tiles(rel_bias, heads, 1, 127, 2)
        return {
            "wq": np.ascontiguousarray(w_in[None, :, hh * 512:(hh + 1) * 512]),
            "wk": np.ascontiguousarray(w_in[None, :, 1024 + hh * 64:1024 + (hh + 1) * 64]),
            "wv": np.ascontiguousarray(w_in[None, :, 1152 + hh * 64:1152 + (hh + 1) * 64]),
            "sinks": _bc(sinks[hh * 8:(hh + 1) * 8]),
            "bias0": bias, "mask0": mask,
        }
    return per_core


def _dil_inputs(w_in, rel_bias):
    w = w_in.reshape(1024, 3, 3, 16, 64)

    def per_core(hh):
        heads = list(range(hh * 8, hh * 8 + 8))
        m = {}
        for j, nm in enumerate(("wq", "wk", "wv")):
            m[nm] = np.ascontiguousarray(np.transpose(w[:, :, j, hh * 8:(hh + 1) * 8, :], (1, 0, 2, 3)).reshape(3, 1024, 512))
        for g, dil in enumerate(DILS):
            bias, mask = _bias_tiles(rel_bias, heads, dil, 128, dil + 1)
            m[f"bias{g}"] = bias
            m[f"mask{g}"] = mask
        return m
    return per_core


def _mla_inputs(w_in, q_norm, w_uq, kv_norm, w_ukv):
    pos = np.arange(SEQ, dtype=np.float32)
    freq = (np.float32(10000.0) ** (-np.arange(16, dtype=np.float32) / np.float32(16))).astype(np.float32)
    ang = (pos[:, None] * freq[None, :]).astype(np.float32)
    cos, sin = np.cos(ang).astype(np.float32), np.sin(ang).astype(np.float32)
    k = np.arange(128)[:, None]
    q = np.arange(128)[None, :]
    cm = np.where(k <= q, np.float32(0), np.float32(NEGM))[None].astype(np.float32)
    wkv = w_ukv.reshape(128, 16, 2, 64)

    def per_core(hh):
        return {
            "w_in": np.ascontiguousarray(w_in), "qn": _bc(q_norm), "kvn": _bc(kv_norm),
            "w_uq": np.ascontiguousarray(w_uq[:, hh * 768:(hh + 1) * 768]),
            "w_uk": np.ascontiguousarray(wkv[:, hh * 8:(hh + 1) * 8, 0, :].reshape(128, 512)),
            "w_uv": np.ascontiguousarray(wkv[:, hh * 8:(hh + 1) * 8, 1, :].reshape(128, 512)),
            "cos": cos, "sin": sin, "mask": cm,
        }
    return per_core


def _shared_inputs(rel_bias, ln_g, ln_b, swa_w_in, swa_sinks, swa_w_out, dil_w_in, dil_w_out,
                   mla_w_in, mla_q_norm, mla_w_uq, mla_kv_norm, mla_w_ukv, mla_w_out,
                   peer_w_q, peer_keys, peer_u, peer_v):
    m = {"ident": _IDENT, "iota16": _IOTA16}
    for i in range(DEPTH):
        kind, j = i % 3, i // 3
        if kind == 0:
            pc, w_out = _swa_inputs(swa_w_in[j], swa_sinks[j], rel_bias), swa_w_out[j]
        elif kind == 1:
            pc, w_out = _dil_inputs(dil_w_in[j], rel_bias), dil_w_out[j]
        else:
            pc, w_out = _mla_inputs(mla_w_in[j], mla_q_norm[j], mla_w_uq[j], mla_kv_norm[j], mla_w_ukv[j]), mla_w_out[j]
        for hh in range(2):
            for k_, v_ in pc(hh).items():
                m[f"L{i}h{hh}_{k_}"] = v_
        for k_, v_ in _post_inputs(w_out, ln_g[i, 0], ln_b[i, 0], peer_w_q[i], peer_keys[i], peer_u[i], peer_v[i],
                                   ln_g[i, 1], ln_b[i, 1]).items():
            m[f"L{i}p_{k_}"] = v_
    return m


def kernel(x, rel_bias, ln_g, ln_b, swa_w_in, swa_sinks, swa_w_out, dil_w_in, dil_w_out,
           mla_w_in, mla_q_norm, mla_w_uq, mla_kv_norm, mla_w_ukv, mla_w_out,
           peer_w_q, peer_keys, peer_u, peer_v):
    f = lambda a: np.asarray(a, dtype=np.float32)
    x = f(x)
    shared = _shared_inputs(f(rel_bias), f(ln_g), f(ln_b), f(swa_w_in), f(swa_sinks), f(swa_w_out), f(dil_w_in), f(dil_w_out),
                            f(mla_w_in), f(mla_q_norm), f(mla_w_uq), f(mla_kv_norm), f(mla_w_ukv), f(mla_w_out),
                            f(peer_w_q), f(peer_keys), f(peer_u), f(peer_v))
    if "fused" not in _PROG:
        _PROG["fused"] = build_fused()
    nc = _PROG["fused"]
    half_tok = SEQ // 2
    in_maps = []
    for c in range(8):
        b, half = c // 2, c % 2
        m = dict(shared)
        m["x"] = np.ascontiguousarray(x[b])
        m["tokidx"] = (half * half_tok + np.arange(half_tok, dtype=np.uint32).reshape(-1, 128).T).astype(np.uint32).copy()
        in_maps.append(m)
    res = run_bass_kernel_spmd(nc, in_maps, core_ids=list(range(8)))
    out = np.empty((4, SEQ, 1024), np.float32)
    for c in range(8):
        b, half = c // 2, c % 2
        out[b, half * half_tok:(half + 1) * half_tok] = res.results[c]["y"]
    return out
```

```python
import math
from contextlib import ExitStack

import numpy as np
import concourse.bass as bass
import concourse.mybir as mybir
from concourse.bass_utils import run_bass_kernel_spmd

F32 = mybir.dt.float32
BF16 = mybir.dt.bfloat16
U32 = mybir.dt.uint32
ALU = mybir.AluOpType
AF = mybir.ActivationFunctionType
AX = mybir.AxisListType

DEPTH = 4
DN_ALPHA = (2 * DEPTH) ** 0.25
LN_EPS = 1e-5
RMS_EPS = 1e-6
NEGM = -30000.0
SEQ = 8192
NBLK = 64
DILS = (1, 4, 16)


class Sched:
    KDMA = 12

    def __init__(self, nc, stack):
        self.nc = nc
        self.stack = stack
        self.eng = {"pe": nc.tensor, "dve": nc.vector, "act": nc.scalar, "pool": nc.gpsimd}
        self.stream = {"pe": "pe", "dve": "dve", "act": "act", "pool": "pool"}
        self.inc = {"pe": 1, "dve": 1, "act": 1, "pool": 1}
        self.EPOCH = {"pe": 30000, "dve": 30000, "act": 30000, "pool": 30000}
        self.rr = {"sp": 0, "poolq": 0, "actq": 0}
        for base, e, st in (("sp", nc.sync, "sp"), ("poolq", nc.gpsimd, "pool"), ("actq", nc.scalar, "act")):
            for k in range(self.KDMA):
                nm = f"{base}{k}"
                self.eng[nm] = e
                self.stream[nm] = st
                self.inc[nm] = 16
                self.EPOCH[nm] = 1800
        self.cnt = {k: 0 for k in self.eng}
        self.sems = {k: [] for k in self.eng}
        self.waited = {}
        self.last_w = {}
        self.readers = {}
        self.nsem = 0
        self.nwait = 0

    def _sem(self, q, epoch):
        while len(self.sems[q]) <= epoch:
            s = self.stack.enter_context(self.nc.semaphore(f"s_{q}_{len(self.sems[q])}"))
            self.sems[q].append(s)
            self.nsem += 1
        return self.sems[q][epoch]

    def _wait(self, q_issuer, dep):
        dq, dn = dep
        st = self.stream[q_issuer]
        ep = (dn - 1) // self.EPOCH[dq]
        local = dn - ep * self.EPOCH[dq]
        key = (st, dq, ep)
        if self.waited.get(key, 0) >= local:
            return
        self.waited[key] = local
        self.nwait += 1
        self.eng[q_issuer].wait_ge(self._sem(dq, ep), local * self.inc[dq])

    SAME_ENG_WAITS = True

    def emit(self, q, fn, reads=(), writes=(), chain=True):
        if q in self.rr:
            k = self.rr[q] % self.KDMA
            self.rr[q] += 1
            q = f"{q}{k}"
        deps = set()
        for b in reads:
            if b in self.last_w:
                deps.add(self.last_w[b])
        for b in writes:
            if b in self.last_w:
                deps.add(self.last_w[b])
            for r in self.readers.get(b, ()):
                deps.add(r)
        if chain and q[-1].isdigit() and self.cnt[q] > 0:
            deps.add((q, self.cnt[q]))
        for d in sorted(deps):
            if d[0] == q and q == "pe":
                continue
            if d[0] == q and q in ("dve", "act", "pool") and not self.SAME_ENG_WAITS:
                continue
            self._wait(q, d)
        ins = fn()
        self.cnt[q] += 1
        n = self.cnt[q]
        ep = (n - 1) // self.EPOCH[q]
        ins.then_inc(self._sem(q, ep), self.inc[q])
        for b in writes:
            self.last_w[b] = (q, n)
            self.readers[b] = []
        for b in reads:
            lst = self.readers.setdefault(b, [])
            lst.append((q, n))
            if len(lst) > 12:
                latest = {}
                for (rq, rn) in lst:
                    latest[rq] = max(latest.get(rq, 0), rn)
                self.readers[b] = list(latest.items())
        return ins

    def barrier(self):
        for rep in ("pe", "dve", "act", "pool", "sp0"):
            for q in self.cnt:
                if self.cnt[q] > 0:
                    self._wait(rep, (q, self.cnt[q]))

    def finish(self):
        for q in self.cnt:
            if self.cnt[q] > 0:
                self._wait("sp0", (q, self.cnt[q]))


class Ctx:
    def __init__(self):
        self.nc = bass.Bass("TRN2", target_bir_lowering=False)
        self.stack = ExitStack()
        self.S = Sched(self.nc, self.stack)

    def din(self, name, shape, dt=F32):
        return self.nc.dram_tensor(name, list(shape), dt, kind="ExternalInput").ap()

    def dout(self, name, shape, dt=F32):
        return self.nc.dram_tensor(name, list(shape), dt, kind="ExternalOutput").ap()

    def dscratch(self, name, shape, dt):
        return self.nc.dram_tensor(name, list(shape), dt).ap()

    def sb(self, name, shape, dt, stack=None):
        return (stack or self.stack).enter_context(self.nc.sbuf_tensor(name, list(shape), dt))

    def ps(self, name, shape, dt, stack=None):
        return (stack or self.stack).enter_context(self.nc.psum_tensor(name, list(shape), dt))


def emit_ln(C, z, zk, out_ap, ok, g_b, b_b, st, junk):
    S, nc = C.S, C.nc
    S.emit("dve", lambda: nc.vector.tensor_scalar(out=junk[:], in0=z, scalar1=1.0, scalar2=0.0, op0=ALU.mult,
                                                   op1=ALU.add, accum_out=st[:, 0:1]),
           reads=[zk], writes=["junk", "lnst0"])
    S.emit("dve", lambda: nc.vector.scalar_tensor_tensor(out=junk[:], in0=z, scalar=1.0, in1=z, op0=ALU.mult,
                                                          op1=ALU.mult, accum_out=st[:, 1:2]),
           reads=[zk], writes=["junk", "lnst1"])
    S.emit("dve", lambda: nc.vector.tensor_scalar(out=st[:, 2:4], in0=st[:, 0:2], scalar1=1.0 / 1024, scalar2=None,
                                                   op0=ALU.mult), reads=["lnst0", "lnst1"], writes=["lnst23"])
    S.emit("dve", lambda: nc.vector.tensor_tensor(out=st[:, 4:5], in0=st[:, 2:3], in1=st[:, 2:3], op=ALU.mult),
           reads=["lnst23"], writes=["lnst4"])
    S.emit("dve", lambda: nc.vector.tensor_tensor(out=st[:, 5:6], in0=st[:, 3:4], in1=st[:, 4:5], op=ALU.subtract),
           reads=["lnst23", "lnst4"], writes=["lnst5"])
    S.emit("dve", lambda: nc.vector.tensor_scalar(out=st[:, 5:6], in0=st[:, 5:6], scalar1=LN_EPS, scalar2=None,
                                                   op0=ALU.add), reads=["lnst5"], writes=["lnst5"])
    S.emit("act", lambda: nc.scalar.activation(out=st[:, 6:7], in_=st[:, 5:6], func=AF.Sqrt),
           reads=["lnst5"], writes=["lnst6"])
    S.emit("dve", lambda: nc.vector.reciprocal(out=st[:, 7:8], in_=st[:, 6:7]), reads=["lnst6"], writes=["lnst7"])
    S.emit("dve", lambda: nc.vector.tensor_scalar(out=z, in0=z, scalar1=st[:, 2:3], scalar2=st[:, 7:8],
                                                   op0=ALU.subtract, op1=ALU.mult),
           reads=[zk, "lnst23", "lnst7"], writes=[zk])
    S.emit("dve", lambda: nc.vector.tensor_tensor(out=z, in0=z, in1=g_b, op=ALU.mult), reads=[zk, "lnw"], writes=[zk])
    S.emit("dve", lambda: nc.vector.tensor_tensor(out=out_ap, in0=z, in1=b_b, op=ALU.add), reads=[zk, "lnw"], writes=[ok])


def emit_att(C, kind, pfx, x, o, ident_sb, ident_bf, PS8, stk):
    nc, S = C.nc, C.S
    G = 3 if kind == "dil" else 1
    DH = 96 if kind == "mla" else 64
    NKV = 1 if kind == "swa" else 8
    QT_d = C.dscratch(pfx + "QT_d", [G, 8, DH, SEQ], BF16)
    KT_d = C.dscratch(pfx + "KT_d", [G, NKV, 64, SEQ], BF16)
    V_d = C.dscratch(pfx + "V_d", [G, NKV, 128, NBLK, 64], BF16)
    if kind == "mla":
        KR_d = C.dscratch(pfx + "KR_d", [32, SEQ], BF16)
        w_in = C.din(pfx + "w_in", [1024, 416])
        qn = C.din(pfx + "qn", [128, 256])
        kvn = C.din(pfx + "kvn", [128, 128])
        w_uq = C.din(pfx + "w_uq", [256, 768])
        w_uk = C.din(pfx + "w_uk", [128, 512])
        w_uv = C.din(pfx + "w_uv", [128, 512])
        cosd = C.din(pfx + "cos", [SEQ, 16])
        sind = C.din(pfx + "sin", [SEQ, 16])
        maskd = C.din(pfx + "mask", [1, 128, 128])
        ND = [1]
    else:
        wq = C.din(pfx + "wq", [G, 1024, 512])
        wk = C.din(pfx + "wk", [G, 1024, NKV * 64])
        wv = C.din(pfx + "wv", [G, 1024, NKV * 64])
        if kind == "swa":
            ND = [2]
            sinkd = C.din(pfx + "sinks", [128, 8])
        else:
            ND = [d + 1 for d in DILS]
        biasd = [C.din(pfx + f"bias{g}", [ND[g], 128, 8, 128]) for g in range(G)]
        maskd = [C.din(pfx + f"mask{g}", [ND[g], 128, 128]) for g in range(G)]

    PSA = PS8[:, 0:4, :]
    PSB = PS8[:, 4:7, :]
    Bt = []
    if kind == "mla":
        cm_bf = C.sb(pfx + "cm_bf", [128, 128], BF16, stk)
        S.emit("poolq", lambda: nc.gpsimd.dma_start(out=cm_bf[:], in_=maskd[0]), writes=["bias"])
    else:
        for g in range(G):
            Bt.append(C.sb(pfx + f"B{g}", [128, ND[g], 8, 128], BF16, stk))
        if kind == "swa":
            esink = C.sb(pfx + "esink", [128, 8], F32, stk)
            S.emit("sp", lambda: nc.sync.dma_start(out=esink[:], in_=sinkd[:, :]), writes=["esink"])
            S.emit("act", lambda: nc.scalar.activation(out=esink[:], in_=esink[:], func=AF.Exp), reads=["esink"], writes=["esink"])

    with ExitStack() as p1:
        if kind != "mla":
            btmp = C.sb(pfx + "btmp", [128, 8, 128], F32, p1)
            mtmp = C.sb(pfx + "mtmp", [128, 128], F32, p1)
            for g in range(G):
                for d in range(ND[g]):
                    S.emit("sp", lambda: nc.sync.dma_start(out=btmp[:], in_=biasd[g][d]), writes=["btmp"])
                    S.emit("sp", lambda: nc.sync.dma_start(out=mtmp[:], in_=maskd[g][d]), writes=["mtmp"])
                    S.emit("dve", lambda: nc.vector.tensor_tensor(
                        out=Bt[g][:, d, :, :], in0=btmp[:], in1=mtmp[:].unsqueeze(1).to_broadcast([128, 8, 128]), op=ALU.add),
                        reads=["btmp", "mtmp"], writes=["bias"])
            wq_bf = C.sb(pfx + "wq_bf", [128, G, 8, 512], BF16, p1)
            wk_bf = C.sb(pfx + "wk_bf", [128, G, 8, NKV * 64], BF16, p1)
            wv_bf = C.sb(pfx + "wv_bf", [128, G, 8, NKV * 64], BF16, p1)
            for g in range(G):
                for c in range(8):
                    S.emit("poolq", lambda: nc.gpsimd.dma_start(out=wq_bf[:, g, c, :], in_=wq[g, c * 128:(c + 1) * 128, :]), writes=[("w", g, c, 0)])
                    S.emit("poolq", lambda: nc.gpsimd.dma_start(out=wk_bf[:, g, c, :], in_=wk[g, c * 128:(c + 1) * 128, :]), writes=[("w", g, c, 1)])
                    S.emit("poolq", lambda: nc.gpsimd.dma_start(out=wv_bf[:, g, c, :], in_=wv[g, c * 128:(c + 1) * 128, :]), writes=[("w", g, c, 2)])
            xin = [C.sb(pfx + f"xin{i}", [128, 4, 1024], F32, p1) for i in range(2)]
            xT = [C.sb(pfx + f"xTb{i}", [128, 8, 512], BF16, p1) for i in range(2)]
            NST = 4
            stg = [C.sb(pfx + f"stg{i}", [128, 512], BF16, p1) for i in range(NST)]
            vst = [C.sb(pfx + f"vst{i}", [128, 4, NKV * 64], BF16, p1) for i in range(2)]
            sti = 0
            ppi = 0
            evi = 0

            def evac(out_ap, in_ap, scale, reads, writes):
                nonlocal evi
                evi += 1
                if evi % 2 == 0:
                    S.emit("act", lambda: nc.scalar.activation(out=out_ap, in_=in_ap, func=AF.Copy, scale=float(scale)),
                           reads=reads, writes=writes)
                else:
                    S.emit("dve", lambda: nc.vector.tensor_scalar(out=out_ap, in0=in_ap, scalar1=float(scale), scalar2=None,
                                                                   op0=ALU.mult), reads=reads, writes=writes)

            for tc in range(SEQ // 512):
                xb = xin[tc % 2]
                xk = ("xin", tc % 2)
                S.emit("sp", lambda: nc.sync.dma_start(out=xb[:], in_=x[tc * 512:(tc + 1) * 512, :].rearrange("(n p) d -> p n d", p=128)),
                       writes=[xk])
                xt = xT[tc % 2]
                xtk = ("xT", tc % 2)
                for ti in range(4):
                    pb = (tc * 4 + ti) % 2
                    pst = PSA[:, 2 * pb:2 * pb + 2, :].rearrange("p b (c t) -> p (b c) t", t=128)
                    for c in range(8):
                        S.emit("pe", lambda: nc.tensor.transpose(out=pst[:, c, :], in_=xb[:, ti, c * 128:(c + 1) * 128], identity=ident_sb[:]),
                               reads=[xk, "ident"], writes=[("A", 2 * pb + c // 4)])
                    evac(xt[:, :, ti * 128:(ti + 1) * 128], pst, 1.0, [("A", 2 * pb), ("A", 2 * pb + 1)], [(xtk, ti)])
                xt_keys = [(xtk, ti) for ti in range(4)]
                for g in range(G):
                    for (wbf, dst_d, nh, scale, wi) in ((wq_bf, QT_d, 8, 0.125, 0), (wk_bf, KT_d, NKV, 1.0, 1)):
                        for h in range(nh):
                            pp = ppi % 3
                            ppi += 1
                            for c in range(8):
                                S.emit("pe", lambda: nc.tensor.matmul(out=PSB[0:64, pp, :], lhsT=wbf[:, g, c, h * 64:(h + 1) * 64],
                                                                      rhs=xt[:, c, :], start=(c == 0), stop=(c == 7)),
                                       reads=xt_keys + [("w", g, c, wi)], writes=[("B", pp)])
                            sg = stg[sti % NST]
                            sk = ("stg", sti % NST)
                            sti += 1
                            evac(sg[0:64, :], PSB[0:64, pp, :], scale, [("B", pp)], [sk])
                            S.emit("sp", lambda: nc.sync.dma_start(out=dst_d[g, h, :, tc * 512:(tc + 1) * 512], in_=sg[0:64, :]),
                                   reads=[sk], writes=[("scr", wi, g, h, tc)])
                    vs = vst[(tc * G + g) % 2]
                    vk = ("vst", (tc * G + g) % 2)
                    for ti in range(4):
                        pp = ppi % 3
                        ppi += 1
                        for c in range(8):
                            S.emit("pe", lambda: nc.tensor.matmul(out=PSB[:, pp, 0:NKV * 64], lhsT=xt[:, c, ti * 128:(ti + 1) * 128],
                                                                  rhs=wv_bf[:, g, c, :], start=(c == 0), stop=(c == 7)),
                                   reads=xt_keys + [("w", g, c, 2)], writes=[("B", pp)])
                        evac(vs[:, ti, :], PSB[:, pp, 0:NKV * 64], 1.0, [("B", pp)], [(vk, ti)])
                    for h in range(NKV):
                        S.emit("sp", lambda: nc.sync.dma_start(out=V_d[g, h, :, tc * 4:(tc + 1) * 4, :], in_=vs[:, :, h * 64:(h + 1) * 64]),
                               reads=[(vk, ti) for ti in range(4)], writes=[("scr", 2, g, h, tc)])
        else:
            w_in_bf = C.sb(pfx + "w_in_bf", [128, 8, 416], BF16, p1)
            w_uq_bf = C.sb(pfx + "w_uq_bf", [128, 2, 768], BF16, p1)
            w_uk_bf = C.sb(pfx + "w_uk_bf", [128, 512], BF16, p1)
            w_uv_bf = C.sb(pfx + "w_uv_bf", [128, 512], BF16, p1)
            qn_b = C.sb(pfx + "qn_b", [128, 256], F32, p1)
            kvn_b = C.sb(pfx + "kvn_b", [128, 128], F32, p1)
            cos_sb = C.sb(pfx + "cos_sb", [128, NBLK, 16], F32, p1)
            sin_sb = C.sb(pfx + "sin_sb", [128, NBLK, 16], F32, p1)
            for c in range(8):
                S.emit("poolq", lambda: nc.gpsimd.dma_start(out=w_in_bf[:, c, :], in_=w_in[c * 128:(c + 1) * 128, :]), writes=[("w", c)])
            for c in range(2):
                S.emit("poolq", lambda: nc.gpsimd.dma_start(out=w_uq_bf[:, c, :], in_=w_uq[c * 128:(c + 1) * 128, :]), writes=[("wuq", c)])
            S.emit("poolq", lambda: nc.gpsimd.dma_start(out=w_uk_bf[:], in_=w_uk[:, :]), writes=["wuk"])
            S.emit("poolq", lambda: nc.gpsimd.dma_start(out=w_uv_bf[:], in_=w_uv[:, :]), writes=["wuv"])
            S.emit("sp", lambda: nc.sync.dma_start(out=qn_b[:], in_=qn[:, :]), writes=["qn"])
            S.emit("sp", lambda: nc.sync.dma_start(out=kvn_b[:], in_=kvn[:, :]), writes=["kvn"])
            S.emit("sp", lambda: nc.sync.dma_start(out=cos_sb[:], in_=cosd.rearrange("(n p) j -> p n j", p=128)), writes=["cos"])
            S.emit("sp", lambda: nc.sync.dma_start(out=sin_sb[:], in_=sind.rearrange("(n p) j -> p n j", p=128)), writes=["sin"])
            xin = [C.sb(pfx + f"xin{i}", [128, 1024], F32, p1) for i in range(2)]
            xTt = C.sb(pfx + "xTt", [128, 8, 128], BF16, p1)
            c_sb = C.sb(pfx + "c_sb", [128, 416], F32, p1)
            junk = C.sb(pfx + "junkm", [128, 256], F32, p1)
            rst = C.sb(pfx + "rst", [128, 8], F32, p1)
            cqn = C.sb(pfx + "cqn", [128, 256], F32, p1)
            ckvn = C.sb(pfx + "ckvn", [128, 128], F32, p1)
            krr = C.sb(pfx + "krr", [128, 32], F32, p1)
            rt = C.sb(pfx + "rt", [128, 4, 16], F32, p1)
            cqnT = C.sb(pfx + "cqnT", [128, 2, 128], BF16, p1)
            ckvnT = C.sb(pfx + "ckvnT", [128, 128], BF16, p1)
            krT = C.sb(pfx + "krT", [32, 128], BF16, p1)
            q_sb = C.sb(pfx + "q_sb", [128, 8, 96], F32, p1)
            q_bf = C.sb(pfx + "q_bf", [128, 8, 96], F32, p1)
            qrt = C.sb(pfx + "qrt", [128, 4, 8, 16], F32, p1)
            qTs = C.sb(pfx + "qTs", [96, 8, 128], BF16, p1)
            kTs = C.sb(pfx + "kTs", [64, 8, 128], BF16, p1)
            vs = C.sb(pfx + "vs", [128, 512], BF16, p1)
            QSC = 96 ** -0.5
            import os as _os2
            _MSTOP = int(_os2.environ.get('MLA_STOP', '99'))
            _MSUB = int(_os2.environ.get('MLA_SUB', '0'))
            for t in range(NBLK):
                xb = xin[t % 2]
                xk = ("xin", t % 2)
                S.emit("sp", lambda: nc.sync.dma_start(out=xb[:], in_=x[t * 128:(t + 1) * 128, :]), writes=[xk])
                pst = PSA[:, 0:2, :].rearrange("p b (c t) -> p (b c) t", t=128)
                for c in range(8):
                    S.emit("pe", lambda: nc.tensor.transpose(out=pst[:, c, :], in_=xb[:, c * 128:(c + 1) * 128], identity=ident_sb[:]),
                           reads=[xk, "ident"], writes=[("A", c // 4)])
                S.emit("act", lambda: nc.scalar.copy(out=xTt[:], in_=pst), reads=[("A", 0), ("A", 1)], writes=["xTt"])
                for c in range(8):
                    S.emit("pe", lambda: nc.tensor.matmul(out=PSA[:, 2, 0:416], lhsT=xTt[:, c, :], rhs=w_in_bf[:, c, :],
                                                          start=(c == 0), stop=(c == 7)),
                           reads=["xTt", ("w", c)], writes=[("A", 2)])
                S.emit("dve", lambda: nc.vector.tensor_copy(out=c_sb[:], in_=PSA[:, 2, 0:416]), reads=[("A", 2)], writes=["c"])
                if _MSTOP <= 2:
                    continue
                S.emit("dve", lambda: nc.vector.scalar_tensor_tensor(out=junk[:], in0=c_sb[:, 0:256], scalar=1.0 / 256, in1=c_sb[:, 0:256],
                                                                      op0=ALU.mult, op1=ALU.mult, accum_out=rst[:, 0:1]),
                       reads=["c"], writes=["junkm", "rst0"])
                S.emit("dve", lambda: nc.vector.scalar_tensor_tensor(out=junk[:, 0:128], in0=c_sb[:, 256:384], scalar=1.0 / 128, in1=c_sb[:, 256:384],
                                                                      op0=ALU.mult, op1=ALU.mult, accum_out=rst[:, 1:2]),
                       reads=["c"], writes=["junkm", "rst1"])
                S.emit("dve", lambda: nc.vector.tensor_scalar(out=rst[:, 2:4], in0=rst[:, 0:2], scalar1=RMS_EPS, scalar2=None, op0=ALU.add),
                       reads=["rst0", "rst1"], writes=["rst23"])
                S.emit("act", lambda: nc.scalar.activation(out=rst[:, 4:6], in_=rst[:, 2:4], func=AF.Sqrt), reads=["rst23"], writes=["rst45"])
                S.emit("dve", lambda: nc.vector.reciprocal(out=rst[:, 6:8], in_=rst[:, 4:6]), reads=["rst45"], writes=["rst67"])
                S.emit("dve", lambda: nc.vector.scalar_tensor_tensor(out=cqn[:], in0=c_sb[:, 0:256], scalar=rst[:, 6:7], in1=qn_b[:],
                                                                      op0=ALU.mult, op1=ALU.mult), reads=["c", "rst67", "qn"], writes=["cqn"])
                S.emit("dve", lambda: nc.vector.scalar_tensor_tensor(out=ckvn[:], in0=c_sb[:, 256:384], scalar=rst[:, 7:8], in1=kvn_b[:],
                                                                      op0=ALU.mult, op1=ALU.mult), reads=["c", "rst67", "kvn"], writes=["ckvn"])
                if _MSTOP <= 3:
                    continue
                k1, k2 = c_sb[:, 384:400], c_sb[:, 400:416]
                cs, sn = cos_sb[:, t, :], sin_sb[:, t, :]
                S.emit("dve", lambda: nc.vector.tensor_tensor(out=rt[:, 0, :], in0=k1, in1=cs, op=ALU.mult), reads=["c", "cos"], writes=["rt0"])
                S.emit("dve", lambda: nc.vector.tensor_tensor(out=rt[:, 1, :], in0=k2, in1=sn, op=ALU.mult), reads=["c", "sin"], writes=["rt1"])
                S.emit("dve", lambda: nc.vector.tensor_tensor(out=rt[:, 2, :], in0=k1, in1=sn, op=ALU.mult), reads=["c", "sin"], writes=["rt2"])
                S.emit("dve", lambda: nc.vector.tensor_tensor(out=rt[:, 3, :], in0=k2, in1=cs, op=ALU.mult), reads=["c", "cos"], writes=["rt3"])
                S.emit("dve", lambda: nc.vector.tensor_tensor(out=krr[:, 0:16], in0=rt[:, 0, :], in1=rt[:, 1, :], op=ALU.subtract),
                       reads=["rt0", "rt1"], writes=["krr0"])
                S.emit("dve", lambda: nc.vector.tensor_tensor(out=krr[:, 16:32], in0=rt[:, 2, :], in1=rt[:, 3, :], op=ALU.add),
                       reads=["rt2", "rt3"], writes=["krr1"])
                if _MSTOP <= 4:
                    continue
                PS3 = PSA[:, 3, :]
                for c in range(2):
                    S.emit("pe", lambda: nc.tensor.transpose(out=PS3[:, c * 128:(c + 1) * 128], in_=cqn[:, c * 128:(c + 1) * 128], identity=ident_sb[:]),
                           reads=["cqn", "ident"], writes=[("A", 3)])
                S.emit("pe", lambda: nc.tensor.transpose(out=PS3[:, 256:384], in_=ckvn[:], identity=ident_sb[:]),
                       reads=["ckvn", "ident"], writes=[("A", 3)])
                if _MSUB != 1:
                    S.emit("pe", lambda: nc.tensor.transpose(out=PS3[0:32, 384:512], in_=krr[:], identity=ident_sb[:]),
                           reads=["krr0", "krr1", "ident"], writes=[("A", 3)])
                S.emit("act", lambda: nc.scalar.copy(out=cqnT[:].rearrange("p c t -> p (c t)"), in_=PS3[:, 0:256]), reads=[("A", 3)], writes=["cqnT"])
                S.emit("act", lambda: nc.scalar.copy(out=ckvnT[:], in_=PS3[:, 256:384]), reads=[("A", 3)], writes=["ckvnT"])
                S.emit("act", lambda: nc.scalar.copy(out=krT[:], in_=PS3[0:32, 384:512]), reads=[("A", 3)], writes=["krT"])
                if _MSUB != 3:
                    S.emit("sp", lambda: nc.sync.dma_start(out=KR_d[:, t * 128:(t + 1) * 128], in_=krT[:]), reads=["krT"], writes=[("scr", "kr", t)])
                if _MSTOP <= 5:
                    continue
                for half in range(2):
                    for c in range(2):
                        S.emit("pe", lambda: nc.tensor.matmul(out=PSB[:, half, 0:384], lhsT=cqnT[:, c, :],
                                                              rhs=w_uq_bf[:, c, half * 384:(half + 1) * 384], start=(c == 0), stop=(c == 1)),
                               reads=["cqnT", ("wuq", c)], writes=[("B", half)])
                    S.emit("act", lambda: nc.scalar.activation(out=q_sb[:, half * 4:(half + 1) * 4, :].rearrange("p h d -> p (h d)"),
                                                               in_=PSB[:, half, 0:384], func=AF.Copy, scale=QSC),
                           reads=[("B", half)], writes=[("q_sb", half)])
                qk = [("q_sb", 0), ("q_sb", 1)]
                q1, q2 = q_sb[:, :, 64:80], q_sb[:, :, 80:96]
                csb = cos_sb[:, t, :].unsqueeze(1).to_broadcast([128, 8, 16])
                snb = sin_sb[:, t, :].unsqueeze(1).to_broadcast([128, 8, 16])
                S.emit("dve", lambda: nc.vector.tensor_tensor(out=qrt[:, 0, :, :], in0=q1, in1=csb, op=ALU.mult), reads=qk + ["cos"], writes=["qrt0"])
                S.emit("dve", lambda: nc.vector.tensor_tensor(out=qrt[:, 1, :, :], in0=q2, in1=snb, op=ALU.mult), reads=qk + ["sin"], writes=["qrt1"])
                S.emit("dve", lambda: nc.vector.tensor_tensor(out=qrt[:, 2, :, :], in0=q1, in1=snb, op=ALU.mult), reads=qk + ["sin"], writes=["qrt2"])
                S.emit("dve", lambda: nc.vector.tensor_tensor(out=qrt[:, 3, :, :], in0=q2, in1=csb, op=ALU.mult), reads=qk + ["cos"], writes=["qrt3"])
                S.emit("dve", lambda: nc.vector.tensor_copy(out=q_bf[:, :, 0:64], in_=q_sb[:, :, 0:64]), reads=qk, writes=["q_bf0"])
                S.emit("dve", lambda: nc.vector.tensor_tensor(out=q_bf[:, :, 64:80], in0=qrt[:, 0, :, :], in1=qrt[:, 1, :, :], op=ALU.subtract),
                       reads=["qrt0", "qrt1"], writes=["q_bf1"])
                S.emit("dve", lambda: nc.vector.tensor_tensor(out=q_bf[:, :, 80:96], in0=qrt[:, 2, :, :], in1=qrt[:, 3, :, :], op=ALU.add),
                       reads=["qrt2", "qrt3"], writes=["q_bf2"])
                if _MSTOP <= 6:
                    continue
                PQT = PSA[:, 0:2, :].rearrange("p b (h t) -> p (b h) t", t=128)
                for h in range(8):
                    S.emit("pe", lambda: nc.tensor.transpose(out=PQT[0:96, h, :], in_=q_bf[:, h, :], identity=ident_sb[:]),
                           reads=["q_bf0", "q_bf1", "q_bf2", "ident"], writes=[("A", h // 4)])
                S.emit("act", lambda: nc.scalar.copy(out=qTs[:], in_=PQT[0:96, :, :]), reads=[("A", 0), ("A", 1)], writes=["qTs"])
                S.emit("sp", lambda: nc.sync.dma_start(out=QT_d[0, :, :, t * 128:(t + 1) * 128].rearrange("h d t -> d h t"), in_=qTs[:]),
                       reads=["qTs"], writes=[("scr", "q", t)])
                if _MSTOP <= 7:
                    continue
                for h in range(8):
                    dst = PSA[0:64, 3, (h % 4) * 128:(h % 4 + 1) * 128] if h < 4 else PSB[0:64, 2, (h % 4) * 128:(h % 4 + 1) * 128]
                    S.emit("pe", lambda: nc.tensor.matmul(out=dst, lhsT=w_uk_bf[:, h * 64:(h + 1) * 64], rhs=ckvnT[:], start=True, stop=True),
                           reads=["wuk", "ckvnT"], writes=[("A", 3) if h < 4 else ("B", 2)])
                S.emit("dve", lambda: nc.vector.tensor_copy(out=kTs[:, 0:4, :].rearrange("p h t -> p (h t)"), in_=PSA[0:64, 3, :]),
                       reads=[("A", 3)], writes=["kTs0"])
                S.emit("act", lambda: nc.scalar.copy(out=kTs[:, 4:8, :].rearrange("p h t -> p (h t)"), in_=PSB[0:64, 2, :]),
                       reads=[("B", 2)], writes=["kTs1"])
                S.emit("sp", lambda: nc.sync.dma_start(out=KT_d[0, :, :, t * 128:(t + 1) * 128].rearrange("h d t -> d h t"), in_=kTs[:]),
                       reads=["kTs0", "kTs1"], writes=[("scr", "k", t)])
                S.emit("pe", lambda: nc.tensor.matmul(out=PSA[:, 2, :], lhsT=ckvnT[:], rhs=w_uv_bf[:], start=True, stop=True),
                       reads=["wuv", "ckvnT"], writes=[("A", 2)])
                S.emit("dve", lambda: nc.vector.tensor_copy(out=vs[:], in_=PSA[:, 2, :]), reads=[("A", 2)], writes=["vs"])
                S.emit("sp", lambda: nc.sync.dma_start(out=V_d[0, :, :, t, :].rearrange("h p d -> p h d"),
                                                       in_=vs[:].rearrange("p (h d) -> p h d", d=64)),
                       reads=["vs"], writes=[("scr", "v", t)])
        S.barrier()

    QT_u = C.sb(pfx + "QT_u", [DH, SEQ], BF16, stk)
    KT_u = C.sb(pfx + "KT_u", [DH, SEQ], BF16, stk)
    V_u = C.sb(pfx + "V_u", [128, NBLK, 65], BF16, stk)
    Oacc = C.sb(pfx + "Oacc", [128, NBLK, 65], F32, stk)
    o_st = C.sb(pfx + "o_st", [128, NBLK, 64], F32, stk)
    den = C.sb(pfx + "den", [128, NBLK], F32, stk)
    NPT = 4
    PT = [C.sb(pfx + f"PT{i}", [128, 4, 128], BF16, stk) for i in range(NPT)]
    S.emit("dve", lambda: nc.vector.memset(V_u[:, :, 64:65], 1.0), writes=["V_ones"])
    SPv = [PSA[:, i, :].rearrange("p (a q) -> p a q", q=128) for i in range(3)]
    OPv = [PSB[:, i, :] for i in range(3)]

    grp_ctr = 0
    blk_ctr = 0
    import os as _os
    for hl in range(int(_os.environ.get('ATT_HEADS', '8'))):
        for g in range(G):
            kvh = 0 if kind == "swa" else hl
            S.emit("sp", lambda: nc.sync.dma_start(out=QT_u[:], in_=QT_d[g, hl]), writes=["QT_u"])
            S.emit("sp", lambda: nc.sync.dma_start(out=KT_u[0:64, :], in_=KT_d[g, kvh]), writes=["KT_u"])
            if kind == "mla":
                S.emit("sp", lambda: nc.sync.dma_start(out=KT_u[64:96, :], in_=KR_d[:, :]), writes=["KT_u2"])
            S.emit("sp", lambda: nc.sync.dma_start(out=V_u[:, :, 0:64], in_=V_d[g, kvh]), writes=["V_u"])
            kt_keys = ["KT_u", "KT_u2"] if kind == "mla" else ["KT_u"]
            dil = DILS[g] if kind == "dil" else 1
            nd = ND[g]
            work = []
            for n in range(NBLK):
                if kind == "mla":
                    kbs = [(kb, n - kb) for kb in range(0, n + 1)]
                else:
                    kbs = [(n - d, d) for d in range(nd - 1, -1, -1) if n - d >= 0]
                groups = [kbs[i:i + 4] for i in range(0, len(kbs), 4)]
                for gi, grp in enumerate(groups):
                    work.append((n, gi, len(groups), grp))

            def emit_qk(item, sp_i):
                n, gi, ng, grp = item
                for i, (kb, d) in enumerate(grp):
                    if kind == "mla":
                        has_b = (d == 0)
                        bt = cm_bf[:, :] if has_b else None
                    else:
                        has_b = True
                        bt = Bt[g][:, d, hl, :]
                    S.emit("pe", lambda: nc.tensor.matmul(out=SPv[sp_i][:, i, :], lhsT=KT_u[:, kb * 128:(kb + 1) * 128],
                                                          rhs=QT_u[:, n * 128:(n + 1) * 128], start=True, stop=(not has_b)),
                           reads=["QT_u"] + kt_keys, writes=[("SP", sp_i)])
                    if has_b:
                        S.emit("pe", lambda: nc.tensor.matmul(out=SPv[sp_i][:, i, :], lhsT=ident_bf[:], rhs=bt, start=False, stop=True),
                               reads=["identbf", "bias"], writes=[("SP", sp_i)])

            def emit_pv(item, sp_i, pt_i):
                n, gi, ng, grp = item
                L = len(grp)
                S.emit("act", lambda: nc.scalar.activation(out=PT[pt_i][:, 0:L, :], in_=SPv[sp_i][:, 0:L, :], func=AF.Exp),
                       reads=[("SP", sp_i)], writes=[("PT", pt_i)])
                slot = n % 3
                for i, (kb, d) in enumerate(grp):
                    S.emit("pe", lambda: nc.tensor.matmul(out=OPv[slot][:, 0:65], lhsT=PT[pt_i][:, i, :], rhs=V_u[:, kb, :],
                                                          start=(gi == 0 and i == 0), stop=(gi == ng - 1 and i == L - 1)),
                           reads=[("PT", pt_i), "V_u", "V_ones"], writes=[("OP", slot)])
                if gi == ng - 1:
                    if g == 0:
                        S.emit("dve", lambda: nc.vector.tensor_copy(out=Oacc[:, n, :], in_=OPv[slot][:, 0:65]),
                               reads=[("OP", slot)], writes=[("Oacc", n)])
                    else:
                        S.emit("dve", lambda: nc.vector.tensor_tensor(out=Oacc[:, n, :], in0=Oacc[:, n, :], in1=OPv[slot][:, 0:65], op=ALU.add),
                               reads=[("OP", slot), ("Oacc", n)], writes=[("Oacc", n)])

            LAG = 2
            pend = []
            for item in work:
                sp_i = grp_ctr % 3
                pt_i = grp_ctr % NPT
                grp_ctr += 1
                emit_qk(item, sp_i)
                pend.append((item, sp_i, pt_i))
                if len(pend) > LAG:
                    emit_pv(*pend.pop(0))
            while pend:
                emit_pv(*pend.pop(0))
            if g == G - 1:
                ok_keys = [("Oacc", n) for n in range(NBLK)]
                if kind == "swa":
                    S.emit("dve", lambda: nc.vector.tensor_scalar(out=den[:], in0=Oacc[:, :, 64], scalar1=esink[:, hl:hl + 1], scalar2=None,
                                                                   op0=ALU.add), reads=ok_keys + ["esink"], writes=["den"])
                    S.emit("dve", lambda: nc.vector.reciprocal(out=den[:], in_=den[:]), reads=["den"], writes=["den"])
                else:
                    S.emit("dve", lambda: nc.vector.reciprocal(out=den[:], in_=Oacc[:, :, 64]), reads=ok_keys, writes=["den"])
                S.emit("dve", lambda: nc.vector.tensor_tensor(out=o_st[:], in0=Oacc[:, :, 0:64],
                                                              in1=den[:].unsqueeze(2).to_broadcast([128, NBLK, 64]), op=ALU.mult),
                       reads=ok_keys + ["den"], writes=["o_st"])
                S.emit("sp", lambda: nc.sync.dma_start(out=o[:, hl * 64:(hl + 1) * 64].rearrange("(n p) d -> p n d", p=128), in_=o_st[:]),
                       reads=["o_st"], writes=[("o", hl)])
    S.barrier()


def emit_post(C, pfx, x, oin, y, NT, ident_sb, iota_sb, PS8, stk, tokidx=None):
    nc, S = C.nc, C.S
    w_out = C.din(pfx + "w_out", [1024, 1024])
    wq = C.din(pfx + "wq", [1024, 2048])
    keysT = C.din(pfx + "keysT", [128, 2048])
    u = C.din(pfx + "u", [16384, 1024])
    v = C.din(pfx + "v", [16384, 1024])
    lnw = C.din(pfx + "lnw", [4, 128, 1024])

    def sb(name, shape, dt):
        return C.sb(pfx + name, shape, dt, stk)

    wq_bf = sb("wq_bf", [128, 8, 2048], BF16)
    wo_bf = sb("wo_bf", [128, 8, 1024], BF16)
    keysT_bf = sb("keysT_bf", [128, 16, 128], BF16)
    ln_sb = sb("ln_sb", [128, 4, 1024], F32)
    xa = [sb(f"xa{i}", [128, 1024], F32) for i in range(2)]
    oa = [sb(f"oa{i}", [128, 1024], F32) for i in range(2)]
    x1 = sb("x1", [128, 1024], F32)
    oT_bf = sb("oT_bf", [128, 8, 128], BF16)
    xT_bf = sb("xT_bf", [128, 8, 128], BF16)
    qT_bf = sb("qT_bf", [128, 16, 128], BF16)
    s_sb = sb("s_sb", [128, 16, 128], F32)
    s_wk = sb("s_wk", [128, 16, 128], F32)
    oh = s_sb[:].rearrange("p a b -> p (a b)").rearrange("p (h k a) -> p h k a", h=8, k=16)
    stop = sb("stop", [128, 8, 2, 16], F32)
    itop = sb("itop", [128, 8, 2, 16], U32)
    itopf = sb("itopf", [128, 8, 2, 16], F32)
    cand = sb("cand", [128, 8, 16, 16], F32)
    best = sb("best", [128, 8, 16], F32)
    sel = sb("sel", [128, 8, 16], U32)
    selA = sb("selA", [128, 8, 16], U32)
    selB = sb("selB", [128, 8, 16], U32)
    selAf = sb("selAf", [128, 8, 16], F32)
    selBf = sb("selBf", [128, 8, 16], F32)
    i1sel = sb("i1sel", [128, 8, 16], F32)
    i2sel = sb("i2sel", [128, 8, 16], F32)
    idxf = sb("idxf", [128, 128], F32)
    idx = sb("idx", [128, 128], U32)
    gd = sb("gd", [128, 8, 16], F32)
    gz = sb("gz", [128, 8], F32)
    gate = sb("gate", [128, 128], F32)
    hacc = sb("hacc", [128, 128], F32)
    gh = sb("gh", [128, 128], F32)
    NB = 8
    gbuf = [sb(f"gbuf{i}", [128, 2048], BF16) for i in range(NB)]
    uvb = C.dscratch(pfx + "uv_bf", [16384, 2048], BF16)
    cst = [sb(f"cst{i}", [128, 4, 1024], BF16) for i in range(2)]
    ci = 0
    uv_keys = []
    for (src_t, col0, nm) in ((u, 0, "ubf"), (v, 1024, "vbf")):
        for blk in range(32):
            cb = cst[ci % 2]
            ck = ("cst", ci % 2)
            ci += 1
            S.emit("poolq", lambda: nc.gpsimd.dma_start(out=cb[:], in_=src_t[blk * 512:(blk + 1) * 512, :].rearrange("(p r) d -> p r d", p=128)),
                   writes=[ck])
            S.emit("sp", lambda: nc.sync.dma_start(out=uvb[blk * 512:(blk + 1) * 512, col0:col0 + 1024].rearrange("(p r) d -> p r d", p=128), in_=cb[:]),
                   reads=[ck], writes=[(nm, blk)])
            uv_keys.append((nm, blk))
    ghc = sb("ghc", [128, 128], F32)
    yacc = sb("yacc", [128, 1024], F32)
    junk = sb("junk", [128, 1024], F32)
    lnst = sb("lnst", [128, 8], F32)
    zt = sb("zt", [128, 1024], F32)
    ot = [sb(f"ot{i}", [128, 1024], F32) for i in range(2)]
    psA = PS8[:, 0:4, :].rearrange("p b (c t) -> p (b c) t", t=128)
    psB = PS8[:, 4:6, :].rearrange("p b (c t) -> p (b c) t", t=128)
    psY = PS8[:, 6:8, :]

    for c in range(8):
        S.emit("poolq", lambda: nc.gpsimd.dma_start(out=wq_bf[:, c, :], in_=wq[c * 128:(c + 1) * 128, :]), writes=[("wq", c)])
        S.emit("poolq", lambda: nc.gpsimd.dma_start(out=wo_bf[:, c, :], in_=w_out[c * 128:(c + 1) * 128, :]), writes=[("wo", c)])
    S.emit("poolq", lambda: nc.gpsimd.dma_start(out=keysT_bf[:].rearrange("p a b -> p (a b)"), in_=keysT[:, :]), writes=["keysT"])
    for i in range(4):
        S.emit("sp", lambda: nc.sync.dma_start(out=ln_sb[:, i, :], in_=lnw[i]), writes=["lnw"] if i == 3 else [("lnw", i)])
    lnw_all = [("lnw", 0), ("lnw", 1), ("lnw", 2), "lnw"]

    def prefetch(t):
        if tokidx is None:
            S.emit("sp", lambda: nc.sync.dma_start(out=xa[t % 2][:], in_=x[t * 128:(t + 1) * 128, :]), writes=[("xa", t % 2)])
            S.emit("sp", lambda: nc.sync.dma_start(out=oa[t % 2][:], in_=oin[t * 128:(t + 1) * 128, :]), writes=[("oa", t % 2)])
        else:
            S.emit("poolq", lambda: nc.gpsimd.indirect_dma_start(
                out=xa[t % 2][:], out_offset=None, in_=x, in_offset=bass.IndirectOffsetOnAxis(ap=tokidx[:, t:t + 1], axis=0)),
                reads=["tokidx"], writes=[("xa", t % 2)])
            S.emit("poolq", lambda: nc.gpsimd.indirect_dma_start(
                out=oa[t % 2][:], out_offset=None, in_=oin, in_offset=bass.IndirectOffsetOnAxis(ap=tokidx[:, t:t + 1], axis=0)),
                reads=["tokidx"], writes=[("oa", t % 2)])

    prefetch(0)
    gi = 0
    for t in range(NT):
        if t + 1 < NT:
            prefetch(t + 1)
        xb, ob = xa[t % 2], oa[t % 2]
        xk, okk = ("xa", t % 2), ("oa", t % 2)
        for c in range(8):
            S.emit("pe", lambda: nc.tensor.transpose(out=psB[:, c, :], in_=ob[:, c * 128:(c + 1) * 128], identity=ident_sb[:]),
                   reads=[okk, "ident"], writes=[("psB", c)])
        S.emit("act", lambda: nc.scalar.copy(out=oT_bf[:], in_=psB), reads=[("psB", c) for c in range(8)], writes=["oT"])
        for half in range(2):
            for c in range(8):
                S.emit("pe", lambda: nc.tensor.matmul(out=psY[:, half, :], lhsT=oT_bf[:, c, :], rhs=wo_bf[:, c, half * 512:(half + 1) * 512],
                                                      start=(c == 0), stop=(c == 7)),
                       reads=["oT", ("wo", c)], writes=[("psY", half)])
        S.emit("dve", lambda: nc.vector.scalar_tensor_tensor(out=zt[:], in0=xb[:], scalar=DN_ALPHA, in1=psY.rearrange("p a b -> p (a b)"),
                                                              op0=ALU.mult, op1=ALU.add),
               reads=[xk, ("psY", 0), ("psY", 1)], writes=["zt"])
        if t == 0:
            S.emit("dve", lambda: nc.vector.tensor_copy(out=lnst[:, 0:1], in_=ln_sb[:, 0, 0:1]), reads=lnw_all, writes=["lnst0"])
        emit_ln(C, zt[:], "zt", x1[:], "x1", ln_sb[:, 0, :], ln_sb[:, 1, :], lnst, junk)
        for c in range(8):
            S.emit("pe", lambda: nc.tensor.transpose(out=psB[:, c, :], in_=x1[:, c * 128:(c + 1) * 128], identity=ident_sb[:]),
                   reads=["x1", "ident"], writes=[("psB", c)])
        S.emit("act", lambda: nc.scalar.copy(out=xT_bf[:], in_=psB), reads=[("psB", c) for c in range(8)], writes=["xT"])
        for blk in range(16):
            for c in range(8):
                S.emit("pe", lambda: nc.tensor.matmul(out=psA[:, blk, :], lhsT=wq_bf[:, c, blk * 128:(blk + 1) * 128], rhs=xT_bf[:, c, :],
                                                      start=(c == 0), stop=(c == 7)),
                       reads=[("wq", c), "xT"], writes=[("psA", blk)])
        S.emit("act", lambda: nc.scalar.copy(out=qT_bf[:], in_=psA), reads=[("psA", b) for b in range(16)], writes=["qT"])
        for hp in range(16):
            S.emit("pe", lambda: nc.tensor.matmul(out=psA[:, hp, :], lhsT=qT_bf[:, hp, :], rhs=keysT_bf[:, hp, :], start=True, stop=True),
                   reads=["qT", "keysT"], writes=[("psA", hp)])
        S.emit("dve", lambda: nc.vector.tensor_copy(out=s_sb[:], in_=psA), reads=[("psA", b) for b in range(16)], writes=["s"])
        for hp in range(16):
            h_, p_ = hp // 2, hp % 2
            sv = s_sb[:, hp, :]
            sw = s_wk[:, hp, :]
            S.emit("dve", lambda: nc.vector.max(out=stop[:, h_, p_, 0:8], in_=sv), reads=["s"], writes=[("stop", hp, 0)])
            S.emit("dve", lambda: nc.vector.max_index(out=itop[:, h_, p_, 0:8], in_max=stop[:, h_, p_, 0:8], in_values=sv),
                   reads=["s", ("stop", hp, 0)], writes=[("itop", hp, 0)])
            S.emit("dve", lambda: nc.vector.match_replace(out=sw, in_to_replace=stop[:, h_, p_, 0:8], in_values=sv, imm_value=-1e30),
                   reads=["s", ("stop", hp, 0)], writes=[("swk", hp)])
            S.emit("dve", lambda: nc.vector.max(out=stop[:, h_, p_, 8:16], in_=sw), reads=[("swk", hp)], writes=[("stop", hp, 1)])
            S.emit("dve", lambda: nc.vector.max_index(out=itop[:, h_, p_, 8:16], in_max=stop[:, h_, p_, 8:16], in_values=sw),
                   reads=[("swk", hp), ("stop", hp, 1)], writes=[("itop", hp, 1)])
        stop_keys = [("stop", hp, i) for hp in range(16) for i in range(2)]
        itop_keys = [("itop", hp, i) for hp in range(16) for i in range(2)]
        S.emit("dve", lambda: nc.vector.tensor_tensor(
            out=cand[:], in0=stop[:, :, 0, :].unsqueeze(3).to_broadcast([128, 8, 16, 16]),
            in1=stop[:, :, 1, :].unsqueeze(2).to_broadcast([128, 8, 16, 16]), op=ALU.add), reads=stop_keys, writes=["cand"])
        for h_ in range(8):
            cv = cand[:, h_, :, :].rearrange("p a b -> p (a b)")
            cw = s_wk[:, 2 * h_:2 * h_ + 2, :].rearrange("p a b -> p (a b)")
            cwk = [("swk", 2 * h_), ("swk", 2 * h_ + 1)]
            S.emit("dve", lambda: nc.vector.max(out=best[:, h_, 0:8], in_=cv), reads=["cand"], writes=[("best", h_, 0)])
            S.emit("dve", lambda: nc.vector.max_index(out=sel[:, h_, 0:8], in_max=best[:, h_, 0:8], in_values=cv),
                   reads=["cand", ("best", h_, 0)], writes=[("sel", h_, 0)])
            S.emit("dve", lambda: nc.vector.match_replace(out=cw, in_to_replace=best[:, h_, 0:8], in_values=cv, imm_value=-1e30),
                   reads=["cand", ("best", h_, 0)], writes=cwk)
            S.emit("dve", lambda: nc.vector.max(out=best[:, h_, 8:16], in_=cw), reads=cwk, writes=[("best", h_, 1)])
            S.emit("dve", lambda: nc.vector.max_index(out=sel[:, h_, 8:16], in_max=best[:, h_, 8:16], in_values=cw),
                   reads=cwk + [("best", h_, 1)], writes=[("sel", h_, 1)])
        best_keys = [("best", h_, i) for h_ in range(8) for i in range(2)]
        sel_keys = [("sel", h_, i) for h_ in range(8) for i in range(2)]
        S.emit("dve", lambda: nc.vector.tensor_single_scalar(out=selA[:], in_=sel[:], scalar=4, op=ALU.logical_shift_right),
               reads=sel_keys, writes=["selA"])
        S.emit("dve", lambda: nc.vector.tensor_single_scalar(out=selB[:], in_=sel[:], scalar=15, op=ALU.bitwise_and),
               reads=sel_keys, writes=["selB"])
        S.emit("dve", lambda: nc.vector.tensor_copy(out=selAf[:], in_=selA[:]), reads=["selA"], writes=["selAf"])
        S.emit("dve", lambda: nc.vector.tensor_copy(out=selBf[:], in_=selB[:]), reads=["selB"], writes=["selBf"])
        S.emit("dve", lambda: nc.vector.tensor_copy(out=itopf[:], in_=itop[:]), reads=itop_keys, writes=["itopf"])
        iota_b = iota_sb[:].unsqueeze(1).unsqueeze(1).to_broadcast([128, 8, 16, 16])
        for (self_, pidx, dst, dkey, skey) in ((selAf, 0, i1sel, "i1sel", "selAf"), (selBf, 1, i2sel, "i2sel", "selBf")):
            S.emit("dve", lambda: nc.vector.tensor_tensor(out=oh, in0=iota_b, in1=self_[:].unsqueeze(3).to_broadcast([128, 8, 16, 16]),
                                                          op=ALU.is_equal), reads=["iota", skey], writes=["s"])
            S.emit("dve", lambda: nc.vector.tensor_tensor(out=oh, in0=oh, in1=itopf[:, :, pidx, :].unsqueeze(2).to_broadcast([128, 8, 16, 16]),
                                                          op=ALU.mult), reads=["s", "itopf"], writes=["s"])
            S.emit("dve", lambda: nc.vector.tensor_reduce(out=dst[:], in_=oh, axis=AX.X, op=ALU.add), reads=["s"], writes=[dkey])
        S.emit("dve", lambda: nc.vector.scalar_tensor_tensor(
            out=idxf[:], in0=i1sel[:].rearrange("p a b -> p (a b)"), scalar=128.0, in1=i2sel[:].rearrange("p a b -> p (a b)"),
            op0=ALU.mult, op1=ALU.add), reads=["i1sel", "i2sel"], writes=["idxf"])
        S.emit("dve", lambda: nc.vector.tensor_copy(out=idx[:], in_=idxf[:]), reads=["idxf"], writes=["idx"])
        S.emit("dve", lambda: nc.vector.tensor_tensor(out=gd[:], in0=best[:], in1=best[:, :, 0:1].to_broadcast([128, 8, 16]),
                                                      op=ALU.subtract), reads=best_keys, writes=["gd"])
        S.emit("act", lambda: nc.scalar.activation(out=gd[:], in_=gd[:], func=AF.Exp), reads=["gd"], writes=["gd"])
        S.emit("dve", lambda: nc.vector.tensor_reduce(out=gz[:], in_=gd[:], axis=AX.X, op=ALU.add), reads=["gd"], writes=["gz"])
        S.emit("dve", lambda: nc.vector.reciprocal(out=gz[:], in_=gz[:]), reads=["gz"], writes=["gz"])
        S.emit("dve", lambda: nc.vector.tensor_tensor(out=gate[:].rearrange("p (a b) -> p a b", b=16), in0=gd[:],
                                                      in1=gz[:].unsqueeze(2).to_broadcast([128, 8, 16]), op=ALU.mult),
               reads=["gd", "gz"], writes=["gate"])
        GS = 4
        for g0 in range(0, 128, GS):
            bufs = []
            for j in range(g0, g0 + GS):
                b = gi % NB
                gi += 1
                bufs.append(b)
                S.emit("poolq", lambda: nc.gpsimd.indirect_dma_start(
                    out=gbuf[b][:], out_offset=None, in_=uvb[:, :], in_offset=bass.IndirectOffsetOnAxis(ap=idx[:, j:j + 1], axis=0)),
                    reads=["idx"] + uv_keys, writes=[("gbuf", b)])
            for k_, j in enumerate(range(g0, g0 + GS)):
                b = bufs[k_]
                S.emit("dve", lambda: nc.vector.scalar_tensor_tensor(out=junk[:], in0=gbuf[b][:, 0:1024], scalar=1.0, in1=x1[:], op0=ALU.mult,
                                                                      op1=ALU.mult, accum_out=hacc[:, j:j + 1]),
                       reads=[("gbuf", b), "x1"], writes=["junk", ("hacc", j)])
            hk = [("hacc", j) for j in range(g0, g0 + GS)]
            S.emit("act", lambda: nc.scalar.activation(out=ghc[:, g0:g0 + GS], in_=hacc[:, g0:g0 + GS], func=AF.Gelu), reads=hk, writes=[("ghc", g0)])
            S.emit("dve", lambda: nc.vector.tensor_tensor(out=gh[:, g0:g0 + GS], in0=ghc[:, g0:g0 + GS], in1=gate[:, g0:g0 + GS], op=ALU.mult),
                   reads=[("ghc", g0), "gate"], writes=[("gh", g0)])
            for k_, j in enumerate(range(g0, g0 + GS)):
                b = bufs[k_]
                if j == 0:
                    S.emit("dve", lambda: nc.vector.tensor_scalar(out=yacc[:], in0=gbuf[b][:, 1024:2048], scalar1=gh[:, 0:1], scalar2=None, op0=ALU.mult),
                           reads=[("gbuf", b), ("gh", g0)], writes=["yacc"])
                else:
                    S.emit("dve", lambda: nc.vector.scalar_tensor_tensor(out=yacc[:], in0=gbuf[b][:, 1024:2048], scalar=gh[:, j:j + 1], in1=yacc[:],
                                                                          op0=ALU.mult, op1=ALU.add),
                           reads=[("gbuf", b), ("gh", g0), "yacc"], writes=["yacc"])
        S.emit("dve", lambda: nc.vector.scalar_tensor_tensor(out=zt[:], in0=x1[:], scalar=DN_ALPHA, in1=yacc[:], op0=ALU.mult, op1=ALU.add),
               reads=["x1", "yacc"], writes=["zt"])
        obuf = ot[t % 2]
        obk = ("ot", t % 2)
        emit_ln(C, zt[:], "zt", obuf[:], obk, ln_sb[:, 2, :], ln_sb[:, 3, :], lnst, junk)
        S.emit("sp", lambda: nc.sync.dma_start(out=y[t * 128:(t + 1) * 128, :], in_=obuf[:]), reads=[obk], writes=[("y", t)])
    S.barrier()


def _rel_bucket(dist):
    n = np.maximum(dist, 0)
    nf = np.maximum(n, 1).astype(np.float32)
    large = 16 + (np.log(nf / np.float32(16)) / np.float32(math.log(2048 / 16)) * np.float32(16)).astype(np.int32)
    large = np.minimum(large, 31)
    return np.where(n < 16, n, large)


def _bias_tiles(rel_bias, heads, dil, max_m, nd):
    k = np.arange(128)[:, None]
    q = np.arange(128)[None, :]
    bias = np.zeros((nd, 128, len(heads), 128), np.float32)
    mask = np.zeros((nd, 128, 128), np.float32)
    for d in range(nd):
        dist = q + d * 128 - k
        valid = (dist >= 0) & (dist % dil == 0) & (dist // dil <= max_m)
        bk = _rel_bucket(np.where(valid, dist, 0))
        bias[d] = np.transpose(rel_bias[bk][:, :, heads], (0, 2, 1))
        mask[d] = np.where(valid, np.float32(0), np.float32(NEGM))
    return bias, mask


_IDENT = np.eye(128, dtype=np.float32)
_IOTA16 = np.ascontiguousarray(np.broadcast_to(np.arange(16, dtype=np.float32)[None, :], (128, 16)))
_PROG = {}
KINDS = ("swa", "dil", "mla")


def _bc(vec, n=128):
    return np.ascontiguousarray(np.broadcast_to(np.asarray(vec, np.float32)[None, :], (n, vec.shape[0])))


def build_fused():
    C = Ctx()
    nc, S = C.nc, C.S
    x_in = C.din("x", [SEQ, 1024])
    ident = C.din("ident", [128, 128])
    iota = C.din("iota16", [128, 16])
    NT_LAST = SEQ // 256
    tok = C.din("tokidx", [128, NT_LAST], U32)
    y = C.dout("y", [SEQ // 2, 1024])
    ident_sb = C.sb("ident_sb", [128, 128], F32)
    ident_bf = C.sb("ident_bf", [128, 128], BF16)
    iota_sb = C.sb("iota_sb", [128, 16], F32)
    tok_sb = C.sb("tok_sb", [128, NT_LAST], U32)
    PS8 = C.ps("PS8", [128, 8, 512], F32)
    S.emit("sp", lambda: nc.sync.dma_start(out=ident_sb[:], in_=ident[:, :]), writes=["ident"])
    S.emit("dve", lambda: nc.vector.tensor_copy(out=ident_bf[:], in_=ident_sb[:]), reads=["ident"], writes=["identbf"])
    S.emit("sp", lambda: nc.sync.dma_start(out=iota_sb[:], in_=iota[:, :]), writes=["iota"])
    S.emit("sp", lambda: nc.sync.dma_start(out=tok_sb[:], in_=tok[:, :]), writes=["tokidx"])
    xbuf = [C.dscratch(f"xs{i}", [SEQ, 1024], F32) for i in range(2)]
    o_d = C.dscratch("o_d", [SEQ, 1024], F32)
    cur = x_in
    for i in range(DEPTH):
        kind = KINDS[i % 3]
        for hh in range(2):
            with ExitStack() as stk:
                emit_att(C, kind, f"L{i}h{hh}_", cur, o_d[:, hh * 512:(hh + 1) * 512], ident_sb, ident_bf, PS8, stk)
        with ExitStack() as stk:
            if i < DEPTH - 1:
                emit_post(C, f"L{i}p_", cur, o_d, xbuf[i % 2], SEQ // 128, ident_sb, iota_sb, PS8, stk)
            else:
                emit_post(C, f"L{i}p_", cur, o_d, y, NT_LAST, ident_sb, iota_sb, PS8, stk, tokidx=tok_sb)
        cur = xbuf[i % 2]
    S.finish()
    print("fused instr:", {k: v for k, v in S.cnt.items() if v and not k[-1].isdigit()},
          "dma:", sum(v for k, v in S.cnt.items() if k[-1].isdigit()), "sems", S.nsem, "waits", S.nwait)
    C.stack.close()
    return nc


def _post_inputs(w_out, g1, b1, w_q, keys, u, v, g2, b2):
    return {
        "w_out": np.ascontiguousarray(w_out), "wq": np.ascontiguousarray(w_q),
        "keysT": np.ascontiguousarray(np.transpose(keys, (3, 0, 1, 2)).reshape(128, 2048)),
        "u": np.ascontiguousarray(u), "v": np.ascontiguousarray(v),
        "lnw": np.stack([_bc(g1), _bc(b1), _bc(g2), _bc(b2)], 0),
    }


def _swa_inputs(w_in, sinks, rel_bias):
    def per_core(hh):
        heads = list(range(hh * 8, hh * 8 + 8))
        bias, mask = _bias_tiles(rel_bias, heads, 1, 127, 2)
        return {
            "wq": np.ascontiguousarray(w_in[None, :, hh * 512:(hh + 1) * 512]),
            "wk": np.ascontiguousarray(w_in[None, :, 1024 + hh * 64:1024 + (hh + 1) * 64]),
            "wv": np.ascontiguousarray(w_in[None, :, 1152 + hh * 64:1152 + (hh + 1) * 64]),
            "sinks": _bc(sinks[hh * 8:(hh + 1) * 8]),
            "bias0": bias, "mask0": mask,
        }
    return per_core


def _dil_inputs(w_in, rel_bias):
    w = w_in.reshape(1024, 3, 3, 16, 64)

    def per_core(hh):
        heads = list(range(hh * 8, hh * 8 + 8))
        m = {}
        for j, nm in enumerate(("wq", "wk", "wv")):
            m[nm] = np.ascontiguousarray(np.transpose(w[:, :, j, hh * 8:(hh + 1) * 8, :], (1, 0, 2, 3)).reshape(3, 1024, 512))
        for g, dil in enumerate(DILS):
            bias, mask = _bias_tiles(rel_bias, heads, dil, 128, dil + 1)
            m[f"bias{g}"] = bias
            m[f"mask{g}"] = mask
        return m
    return per_core


def _mla_inputs(w_in, q_norm, w_uq, kv_norm, w_ukv):
    pos = np.arange(SEQ, dtype=np.float32)
    freq = (np.float32(10000.0) ** (-np.arange(16, dtype=np.float32) / np.float32(16))).astype(np.float32)
    ang = (pos[:, None] * freq[None, :]).astype(np.float32)
    cos, sin = np.cos(ang).astype(np.float32), np.sin(ang).astype(np.float32)
    k = np.arange(128)[:, None]
    q = np.arange(128)[None, :]
    cm = np.where(k <= q, np.float32(0), np.float32(NEGM))[None].astype(np.float32)
    wkv = w_ukv.reshape(128, 16, 2, 64)

    def per_core(hh):
        return {
            "w_in": np.ascontiguousarray(w_in), "qn": _bc(q_norm), "kvn": _bc(kv_norm),
            "w_uq": np.ascontiguousarray(w_uq[:, hh * 768:(hh + 1) * 768]),
            "w_uk": np.ascontiguousarray(wkv[:, hh * 8:(hh + 1) * 8, 0, :].reshape(128, 512)),
            "w_uv": np.ascontiguousarray(wkv[:, hh * 8:(hh + 1) * 8, 1, :].reshape(128, 512)),
            "cos": cos, "sin": sin, "mask": cm,
        }
    return per_core


def _shared_inputs(rel_bias, ln_g, ln_b, swa_w_in, swa_sinks, swa_w_out, dil_w_in, dil_w_out,
                   mla_w_in, mla_q_norm, mla_w_uq, mla_kv_norm, mla_w_ukv, mla_w_out,
                   peer_w_q, peer_keys, peer_u, peer_v):
    m = {"ident": _IDENT, "iota16": _IOTA16}
    for i in range(DEPTH):
        kind, j = i % 3, i // 3
        if kind == 0:
            pc, w_out = _swa_inputs(swa_w_in[j], swa_sinks[j], rel_bias), swa_w_out[j]
        elif kind == 1:
            pc, w_out = _dil_inputs(dil_w_in[j], rel_bias), dil_w_out[j]
        else:
            pc, w_out = _mla_inputs(mla_w_in[j], mla_q_norm[j], mla_w_uq[j], mla_kv_norm[j], mla_w_ukv[j]), mla_w_out[j]
        for hh in range(2):
            for k_, v_ in pc(hh).items():
                m[f"L{i}h{hh}_{k_}"] = v_
        for k_, v_ in _post_inputs(w_out, ln_g[i, 0], ln_b[i, 0], peer_w_q[i], peer_keys[i], peer_u[i], peer_v[i],
                                   ln_g[i, 1], ln_b[i, 1]).items():
            m[f"L{i}p_{k_}"] = v_
    return m


def kernel(x, rel_bias, ln_g, ln_b, swa_w_in, swa_sinks, swa_w_out, dil_w_in, dil_w_out,
           mla_w_in, mla_q_norm, mla_w_uq, mla_kv_norm, mla_w_ukv, mla_w_out,
           peer_w_q, peer_keys, peer_u, peer_v):
    f = lambda a: np.asarray(a, dtype=np.float32)
    x = f(x)
    shared = _shared_inputs(f(rel_bias), f(ln_g), f(ln_b), f(swa_w_in), f(swa_sinks), f(swa_w_out), f(dil_w_in), f(dil_w_out),
                            f(mla_w_in), f(mla_q_norm), f(mla_w_uq), f(mla_kv_norm), f(mla_w_ukv), f(mla_w_out),
                            f(peer_w_q), f(peer_keys), f(peer_u), f(peer_v))
    if "fused" not in _PROG:
        _PROG["fused"] = build_fused()
    nc = _PROG["fused"]
    half_tok = SEQ // 2
    in_maps = []
    for c in range(8):
        b, half = c // 2, c % 2
        m = dict(shared)
        m["x"] = np.ascontiguousarray(x[b])
        m["tokidx"] = (half * half_tok + np.arange(half_tok, dtype=np.uint32).reshape(-1, 128).T).astype(np.uint32).copy()
        in_maps.append(m)
    res = run_bass_kernel_spmd(nc, in_maps, core_ids=list(range(8)))
    out = np.empty((4, SEQ, 1024), np.float32)
    for c in range(8):
        b, half = c // 2, c % 2
        out[b, half * half_tok:(half + 1) * half_tok] = res.results[c]["y"]
    return out
```
